# Optimizing a Trainium2 kernel written in Bass

```python
import math
import jax, jax.numpy as jnp
from jax import lax
import numpy as np

D_MODEL = 1024
BATCH = 8
SEQ = 4096
DEPTH = 1

RET_HEADS = 4
RET_QK_DIM = 256
RET_V_DIM = 512
RET_QK = RET_HEADS * RET_QK_DIM
RET_V = RET_HEADS * RET_V_DIM
RET_CHUNK = 128
ROPE_BASE = 10000.0
SSM_INNER = 2 * D_MODEL
SSM_HEAD_DIM = 64
SSM_HEADS = SSM_INNER // SSM_HEAD_DIM
SSM_GROUPS = 4
SSM_HPG = SSM_HEADS // SSM_GROUPS
SSM_STATE = 128
SSM_CONV = 5
SSM_CHUNK = 128
SSM_XBC = SSM_INNER + 2 * SSM_GROUPS * SSM_STATE
D_FF = 4 * D_MODEL
N_BRANCH = 2
EPS = 1e-6
IN_SIZES = (RET_QK, RET_QK, RET_V, RET_V, SSM_INNER, SSM_XBC, 2 * SSM_HEADS, N_BRANCH * D_MODEL)
D_IN = RET_QK * 2 + RET_V * 2 + SSM_INNER + SSM_XBC + 2 * SSM_HEADS + N_BRANCH * D_MODEL

kernel_name = 'hybrid_retention_ssd_gated_encoder'


def _split(t, sizes):
    offs = np.cumsum(np.array(sizes))[:-1].tolist()
    return jnp.split(t, offs, axis=-1)


def _rmsnorm(x, g):
    xf = x.astype(jnp.float32)
    y = xf * lax.rsqrt(jnp.mean(xf * xf, axis=-1, keepdims=True) + EPS)
    return (y * g.astype(jnp.float32)).astype(x.dtype)


def _rope(t, positions):
    half = t.shape[-1] // 2
    inv = ROPE_BASE ** (-jnp.arange(half, dtype=jnp.float32) / half)
    ang = positions.astype(jnp.float32)[..., None] * inv
    cos = jnp.cos(ang)[:, :, None, :]
    sin = jnp.sin(ang)[:, :, None, :]
    t1, t2 = t[..., :half], t[..., half:]
    return jnp.concatenate([t1 * cos - t2 * sin, t1 * sin + t2 * cos], axis=-1)


def _ret_cross(qc, kc, vc, log_g, reverse):
    pos = jnp.arange(RET_CHUNK, dtype=jnp.float32)
    if reverse:
        q_exp, k_exp = RET_CHUNK - pos, pos
    else:
        q_exp, k_exp = pos + 1.0, RET_CHUNK - 1.0 - pos
    q_dec = jnp.exp(q_exp[None, :] * log_g[:, None])[:, :, None]
    k_dec = jnp.exp(k_exp[None, :] * log_g[:, None])[:, :, None]
    chunk_dec = jnp.exp(RET_CHUNK * log_g)[:, None, None]
    qs, ks = qc * q_dec, kc * k_dec

    def step(state, inp):
        q_i, k_i, v_i = inp
        out = jnp.einsum('bhck,bhkv->bhcv', q_i, state)
        state = state * chunk_dec + jnp.einsum('bhck,bhcv->bhkv', k_i, v_i)
        return state, out

    init = jnp.zeros((qc.shape[1], RET_HEADS, RET_QK_DIM, RET_V_DIM), jnp.float32)
    _, out = lax.scan(step, init, (qs, ks, vc), reverse=reverse)
    return out


def _retention(q, k, v, g, positions, gn_g):
    f32 = jnp.float32
    bsz, seq = q.shape[0], q.shape[1]
    nc = seq // RET_CHUNK
    q = _rope(q.astype(f32).reshape(bsz, seq, RET_HEADS, RET_QK_DIM), positions)
    k = _rope(k.astype(f32).reshape(bsz, seq, RET_HEADS, RET_QK_DIM), positions) * (RET_QK_DIM ** -0.5)
    v = v.astype(f32).reshape(bsz, seq, RET_HEADS, RET_V_DIM)
    log_g = jnp.log1p(-jnp.exp2(-5.0 - jnp.arange(RET_HEADS, dtype=f32)))

    def to_chunks(t):
        return t.reshape(bsz, nc, RET_CHUNK, RET_HEADS, t.shape[-1]).transpose(1, 0, 3, 2, 4)

    qc, kc, vc = to_chunks(q), to_chunks(k), to_chunks(v)
    idx = jnp.arange(RET_CHUNK, dtype=f32)
    dist = jnp.abs(idx[:, None] - idx[None, :])
    intra_dec = jnp.exp(dist[None] * log_g[:, None, None])
    scores = jnp.einsum('nbhik,nbhjk->nbhij', qc, kc) * intra_dec
    y = jnp.einsum('nbhij,nbhjv->nbhiv', scores, vc)
    y = y + _ret_cross(qc, kc, vc, log_g, False) + _ret_cross(qc, kc, vc, log_g, True)
    y = y.transpose(1, 0, 3, 2, 4).reshape(bsz, seq, RET_HEADS, RET_V_DIM)
    mu = jnp.mean(y, axis=-1, keepdims=True)
    var = jnp.mean(jnp.square(y - mu), axis=-1, keepdims=True)
    y = ((y - mu) * lax.rsqrt(var + EPS)).reshape(bsz, seq, RET_V) * gn_g.astype(f32)
    return (y * jax.nn.silu(g.astype(f32))).astype(g.dtype)


def _ssd_scan(xdt, la, bm, cm):
    bsz, seq = xdt.shape[0], xdt.shape[1]
    nc = seq // SSM_CHUNK

    def to_chunks(t):
        return jnp.moveaxis(t.reshape((bsz, nc, SSM_CHUNK) + t.shape[2:]), 1, 0)

    causal = jnp.tril(jnp.ones((SSM_CHUNK, SSM_CHUNK), dtype=bool))

    def step(state, inp):
        xc, lac, bc, cc = inp
        cum = jnp.cumsum(lac, axis=1)
        seg = cum[:, :, None, :] - cum[:, None, :, :]
        L = jnp.exp(jnp.where(causal[None, :, :, None], seg, -jnp.inf))
        L = L.reshape(bsz, SSM_CHUNK, SSM_CHUNK, SSM_GROUPS, SSM_HPG)
        xg = xc.reshape(bsz, SSM_CHUNK, SSM_GROUPS, SSM_HPG, SSM_HEAD_DIM)
        cb = jnp.einsum('bign,bjgn->bgij', cc, bc)
        y_in = jnp.einsum('bgij,bijgh,bjghp->bighp', cb, L, xg)
        dec_in = jnp.exp(cum).reshape(bsz, SSM_CHUNK, SSM_GROUPS, SSM_HPG, 1)
        y_st = jnp.einsum('bign,bghpn->bighp', cc, state) * dec_in
        dec_out = jnp.exp(cum[:, -1:, :] - cum).reshape(bsz, SSM_CHUNK, SSM_GROUPS, SSM_HPG)
        dec_chunk = jnp.exp(cum[:, -1, :]).reshape(bsz, SSM_GROUPS, SSM_HPG, 1, 1)
        state = state * dec_chunk + jnp.einsum('bjgn,bjgh,bjghp->bghpn', bc, dec_out, xg)
        return state, (y_in + y_st).reshape(bsz, SSM_CHUNK, SSM_HEADS, SSM_HEAD_DIM)

    init = jnp.zeros((bsz, SSM_GROUPS, SSM_HPG, SSM_HEAD_DIM, SSM_STATE), jnp.float32)
    _, y = lax.scan(step, init, (to_chunks(xdt), to_chunks(la), to_chunks(bm), to_chunks(cm)))
    return jnp.moveaxis(y, 0, 1).reshape(bsz, seq, SSM_HEADS, SSM_HEAD_DIM)


def _ssd_mixer(z, xbc, dt_raw, conv_w, conv_b, dt_bias_f, dt_bias_b, a_log_f, a_log_b, d_skip, norm_g):
    f32 = jnp.float32
    bsz, seq = z.shape[0], z.shape[1]
    xbc = lax.conv_general_dilated(
        xbc, conv_w[:, None, :].astype(xbc.dtype), window_strides=(1,),
        padding=((SSM_CONV // 2, SSM_CONV // 2),),
        dimension_numbers=('NWC', 'WIO', 'NWC'), feature_group_count=SSM_XBC)
    xbc = jax.nn.silu((xbc + conv_b).astype(f32))
    xs, bm, cm = _split(xbc, (SSM_INNER, SSM_GROUPS * SSM_STATE, SSM_GROUPS * SSM_STATE))
    xh = xs.reshape(bsz, seq, SSM_HEADS, SSM_HEAD_DIM)
    bm = bm.reshape(bsz, seq, SSM_GROUPS, SSM_STATE)
    cm = cm.reshape(bsz, seq, SSM_GROUPS, SSM_STATE)
    dt_f_raw, dt_b_raw = _split(dt_raw.astype(f32), (SSM_HEADS, SSM_HEADS))
    dt_f = jax.nn.softplus(dt_f_raw + dt_bias_f.astype(f32))
    dt_b = jax.nn.softplus(dt_b_raw + dt_bias_b.astype(f32))
    a_f = -jnp.exp(a_log_f.astype(f32))
    a_b = -jnp.exp(a_log_b.astype(f32))
    y_f = _ssd_scan(xh * dt_f[..., None], dt_f * a_f, bm, cm)

    def flip(t):
        return jnp.flip(t, axis=1)

    y_b = flip(_ssd_scan(flip(xh * dt_b[..., None]), flip(dt_b * a_b), flip(bm), flip(cm)))
    y = y_f + y_b + d_skip.astype(f32)[:, None] * xh
    y = y.reshape(bsz, seq, SSM_INNER) * jax.nn.silu(z.astype(f32))
    yg = y.reshape(bsz, seq, SSM_GROUPS, SSM_INNER // SSM_GROUPS)
    yg = yg * lax.rsqrt(jnp.mean(yg * yg, axis=-1, keepdims=True) + EPS)
    return (yg.reshape(bsz, seq, SSM_INNER) * norm_g.astype(f32)).astype(z.dtype)


def setup_inputs(seed: int = 0) -> dict:
    key = jax.random.key(seed)
    ks = jax.random.split(key, 20)
    nrm = jax.random.normal

    def gain(k, n):
        return 1.0 + 0.02 * nrm(k, (DEPTH, n), jnp.float32)

    def dt_bias(k):
        u = jax.random.uniform(k, (DEPTH, SSM_HEADS), jnp.float32)
        dt = jnp.exp(u * (math.log(0.1) - math.log(1e-3)) + math.log(1e-3))
        return dt + jnp.log(-jnp.expm1(-dt))

    def a_log(k):
        return jnp.log(jax.random.uniform(k, (DEPTH, SSM_HEADS), jnp.float32, 1.0, 16.0))

    offset = jax.random.randint(ks[1], (BATCH, 1), 0, 1024, dtype=jnp.int32)
    positions = jnp.arange(SEQ, dtype=jnp.int32)[None, :] + offset
    return {
        'x': nrm(ks[0], (BATCH, SEQ, D_MODEL), jnp.float32),
        'positions': positions,
        'norm_mix_g': gain(ks[2], D_MODEL),
        'w_in': nrm(ks[3], (DEPTH, D_MODEL, D_IN), jnp.float32) * D_MODEL ** -0.5,
        'ret_gn_g': gain(ks[4], RET_V),
        'w_ret_o': nrm(ks[5], (DEPTH, RET_V, D_MODEL), jnp.float32) * RET_V ** -0.5,
        'conv_w': nrm(ks[6], (DEPTH, SSM_CONV, SSM_XBC), jnp.float32) * SSM_CONV ** -0.5,
        'conv_b': 0.02 * nrm(ks[7], (DEPTH, SSM_XBC), jnp.float32),
        'dt_bias_f': dt_bias(ks[8]),
        'dt_bias_b': dt_bias(ks[9]),
        'a_log_f': a_log(ks[10]),
        'a_log_b': a_log(ks[11]),
        'ssm_d': 1.0 + 0.02 * nrm(ks[12], (DEPTH, SSM_HEADS), jnp.float32),
        'ssm_norm_g': gain(ks[13], SSM_INNER),
        'w_ssm_o': nrm(ks[14], (DEPTH, SSM_INNER, D_MODEL), jnp.float32) * SSM_INNER ** -0.5,
        'w_out': nrm(ks[15], (DEPTH, D_MODEL, D_MODEL), jnp.float32) * D_MODEL ** -0.5,
        'norm_mlp_g': gain(ks[16], D_MODEL),
        'w_mlp_up': nrm(ks[17], (DEPTH, D_MODEL, D_FF), jnp.float32) * D_MODEL ** -0.5,
        'w_mlp_down': nrm(ks[18], (DEPTH, D_FF, D_MODEL), jnp.float32) * D_FF ** -0.5,
        'norm_final_g': 1.0 + 0.02 * nrm(ks[19], (D_MODEL,), jnp.float32),
    }


def reference(x, positions, norm_mix_g, w_in, ret_gn_g, w_ret_o, conv_w, conv_b, dt_bias_f, dt_bias_b,
              a_log_f, a_log_b, ssm_d, ssm_norm_g, w_ssm_o, w_out, norm_mlp_g, w_mlp_up, w_mlp_down,
              norm_final_g):
    for l in range(DEPTH):
        h = _rmsnorm(x, norm_mix_g[l])
        proj = jnp.einsum('bsd,de->bse', h, w_in[l])
        q, k, v, g, z, xbc, dt_raw, gates = _split(proj, IN_SIZES)
        y_ret = jnp.einsum('bse,ed->bsd', _retention(q, k, v, g, positions, ret_gn_g[l]), w_ret_o[l])
        y_ssm = jnp.einsum('bse,ed->bsd', _ssd_mixer(z, xbc, dt_raw, conv_w[l], conv_b[l], dt_bias_f[l],
                                                     dt_bias_b[l], a_log_f[l], a_log_b[l], ssm_d[l],
                                                     ssm_norm_g[l]), w_ssm_o[l])
        gate_ret, gate_ssm = _split(gates, (D_MODEL, D_MODEL))
        mixed = jax.nn.sigmoid(gate_ret) * y_ret + jax.nn.sigmoid(gate_ssm) * y_ssm
        x = x + jnp.einsum('bsd,de->bse', mixed, w_out[l])
        h = _rmsnorm(x, norm_mlp_g[l])
        up = jnp.square(jax.nn.relu(jnp.einsum('bsd,df->bsf', h, w_mlp_up[l])))
        x = x + jnp.einsum('bsf,fd->bsd', up, w_mlp_down[l])
    return _rmsnorm(x, norm_final_g)
```

```python
import math
import numpy as np
import ml_dtypes
import concourse.bass as bass
import concourse.mybir as mybir
from concourse.bass_utils import run_bass_kernel_spmd

F32 = mybir.dt.float32
BF16 = mybir.dt.bfloat16
I32 = mybir.dt.int32
AF = mybir.ActivationFunctionType
ALU = mybir.AluOpType
AX = mybir.AxisListType

T = 4096
NCH = 32
D = 1024
DIN = 13376
EPS = 1e-6
OQ, OK_, OV, OG, OZ, OX, ODT, OGATE = 0, 1024, 2048, 4096, 6144, 8192, 11264, 11328


class Tk:
    __slots__ = ("name", "lw", "rd", "multi", "ws", "excl")

    def __init__(self, name="", multi=False, excl=False):
        self.name = name
        self.lw = None
        self.rd = {}
        self.multi = multi
        self.ws = {}
        self.excl = excl


class Prog:
    ENGS = ("pe", "act", "dve", "pool", "sp")

    def __init__(self, nc):
        self.nc = nc
        self.eng = {"pe": nc.tensor, "act": nc.scalar, "dve": nc.vector,
                    "pool": nc.gpsimd, "sp": nc.sync}
        self.sem = {}
        self.cnt = {}
        self.waited = {e: {} for e in self.ENGS}
        for e in self.ENGS:
            k = "E_" + e
            self.sem[k] = nc.alloc_semaphore(name=k)
            self.cnt[k] = 0
        self.dma_pool = {}
        self.dma_rr = {}
        for q, n in (("sp", 20), ("pool", 12), ("act", 8)):
            ks = []
            for i in range(n):
                k = "D_%s%d" % (q, i)
                self.sem[k] = nc.alloc_semaphore(name=k)
                self.cnt[k] = 0
                ks.append(k)
            self.dma_pool[q] = ks
            self.dma_rr[q] = 0
        self.n_inst = 0

    def _wait(self, eng, deps):
        w = self.waited[eng]
        e = self.eng[eng]
        own = "E_" + eng
        for (k, v, kind) in deps:
            if k == own and (eng == "pe" or eng == "sp" or kind == "war"):
                continue
            if w.get(k, 0) >= v:
                continue
            w[k] = v
            e.wait_ge(self.sem[k], v)

    @staticmethod
    def _deps(reads, writes):
        deps = []
        for t in reads:
            if t.multi:
                for k, v in t.ws.items():
                    deps.append((k, v, "raw"))
            elif t.lw is not None:
                deps.append((t.lw[0], t.lw[1], "raw"))
            if t.excl:
                for k, v in t.rd.items():
                    deps.append((k, v, "war"))
        for t in writes:
            if t.multi:
                continue
            if t.lw is not None:
                deps.append((t.lw[0], t.lw[1], "waw"))
            for k, v in t.rd.items():
                deps.append((k, v, "war"))
        return deps

    @staticmethod
    def _commit(ev, reads, writes):
        for t in reads:
            if t.rd.get(ev[0], 0) < ev[1]:
                t.rd[ev[0]] = ev[1]
        for t in writes:
            if t.multi:
                if t.ws.get(ev[0], 0) < ev[1]:
                    t.ws[ev[0]] = ev[1]
                continue
            t.lw = ev
            t.rd = {}

    def op(self, eng, fn, reads=(), writes=()):
        self._wait(eng, self._deps(reads, writes))
        k = "E_" + eng
        self.cnt[k] += 1
        ins = fn(self.eng[eng])
        ins.then_inc(self.sem[k], 1)
        self._commit((k, self.cnt[k]), reads, writes)
        self.n_inst += 1

    def dma(self, q, out, in_, reads=(), writes=()):
        deps = self._deps(reads, writes)
        pool = self.dma_pool[q]
        s = pool[self.dma_rr[q] % len(pool)]
        self.dma_rr[q] += 1
        if self.cnt[s] > 0:
            deps.append((s, self.cnt[s], "raw"))
        self._wait(q, deps)
        self.cnt[s] += 16
        self.eng[q].dma_start(out=out, in_=in_).then_inc(self.sem[s], 16)
        self._commit((s, self.cnt[s]), reads, writes)
        self.n_inst += 1

    def barrier(self):
        allk = [(k, v, "raw") for k, v in self.cnt.items() if v > 0]
        for e in self.ENGS:
            self._wait(e, [d for d in allk if d[0] != "E_" + e])

    def finish(self):
        allk = [(k, v, "raw") for k, v in self.cnt.items() if v > 0]
        self._wait("sp", allk)


_UNIQ = [0]


def _uniq(name):
    _UNIQ[0] += 1
    return "s%d_%s" % (_UNIQ[0], name)


def _bf(a):
    return np.asarray(a, dtype=np.float32).astype(ml_dtypes.bfloat16)


def host_consts():
    c = {}
    c["ident_bf"] = _bf(np.eye(128))
    c["ident_f"] = np.eye(128, dtype=np.float32)
    half = 128
    inv = (10000.0 ** (-np.arange(half, dtype=np.float32) / half)).astype(np.float32)
    c["invf"] = inv.reshape(128, 1).astype(np.float32)
    gam = np.array([1.0 - 2.0 ** (-5 - h) for h in range(4)], dtype=np.float64)
    idx = np.arange(128, dtype=np.float64)
    dist = np.abs(idx[:, None] - idx[None, :])
    c["dmat"] = np.stack([gam[h] ** dist / 16.0 for h in range(4)], 1).astype(np.float32)
    qf = np.stack([gam[h] ** (idx + 1.0) for h in range(4)], 0)
    qb = np.stack([gam[h] ** (128.0 - idx) for h in range(4)], 0)
    c["qdf"] = np.broadcast_to(np.repeat(qf, 2, axis=0)[None], (128, 8, 128)).astype(np.float32).copy()
    c["qdb"] = np.broadcast_to(np.repeat(qb, 2, axis=0)[None], (128, 8, 128)).astype(np.float32).copy()
    kf = np.stack([gam[h] ** (127.0 - idx) / 16.0 for h in range(4)], 1)
    kb = np.stack([gam[h] ** idx / 16.0 for h in range(4)], 1)
    c["kdec"] = np.concatenate([kf, kb], 1).astype(np.float32)
    c["maskF"] = (idx[None, :] >= idx[:, None]).astype(np.float32)
    c["maskB"] = (idx[None, :] <= idx[:, None]).astype(np.float32)
    return c


def build(stage=99, debug=()):
    nc = bass.Bass("TRN2", target_bir_lowering=False)
    P = Prog(nc)
    dbg = set(debug)

    in_names = set()

    def din(name, shape, dt):
        in_names.add(name)
        return nc.dram_tensor(name, list(shape), dt, kind="ExternalInput").ap()

    def dscr(name, shape, dt):
        kind = "ExternalOutput" if name in dbg else "Internal"
        return nc.dram_tensor(name, list(shape), dt, kind=kind).ap()

    x_d = din("x", [T, D], F32)
    pos_d = din("pos", [1, T], I32)
    w_in_d = din("w_in", [D, DIN], F32)
    gmix_d = din("gmix", [128, 8], F32)
    dtb_d = din("dtb", [1, 64], F32)
    ident_bf_d = din("ident_bf", [128, 128], BF16)
    invf_d = din("invf", [128, 1], F32)
    out_d = nc.dram_tensor("out", [T, D], F32, kind="ExternalOutput").ap()
    dmat_d = din("dmat", [128, 4, 128], F32)
    qdf_d = din("qdf", [128, 8, 128], F32)
    qdb_d = din("qdb", [128, 8, 128], F32)
    kdec_d = din("kdec", [128, 8], F32)
    maskF_d = din("maskF", [128, 128], F32)
    maskB_d = din("maskB", [128, 128], F32)
    cw_d = din("cw", [128, 24, 5], F32)
    cbc_d = din("cbc", [128, 24], F32)
    alog_d = din("alog", [1, 64], F32)
    gains_d = din("gains", [128, 40], F32)
    gfin_d = din("gfin", [1, 1024], F32)
    w_ret_o_d = din("w_ret_o", [2048, 1024], F32)
    w_ssm_o_d = din("w_ssm_o", [2048, 1024], F32)
    w_out_d = din("w_out", [1024, 1024], F32)
    w_up_d = din("w_up", [1024, 4096], F32)
    w_dn_d = din("w_dn", [4096, 1024], F32)
    dskip_d = din("dskip", [1, 32], F32)

    qT_d = dscr("qT", [1024, T], BF16)
    kT_d = dscr("kT", [1024, T], BF16)
    v_d = dscr("v", [T, 2048], BF16)
    g_d = dscr("g", [T, 2048], BF16)
    z_d = dscr("z", [T, 2048], BF16)
    xbcT_d = dscr("xbcT", [3072, T + 4], BF16)
    gateT_d = dscr("gateT", [2048, T], BF16)
    dt_d = dscr("dt", [T, 64], F32)
    ktok_d = dscr("ktok", [T, 1024], BF16)
    Sbret_d = dscr("Sbret", [NCH, 128, 4096], BF16)
    rT_d = dscr("rT", [2048, T], BF16)
    BCT_d = dscr("BCT", [1024, T], BF16)
    xtok_d = dscr("xtok", [T, 2560], BF16)
    cumT_d = dscr("cumT", [NCH, 2, 32, 128], F32)
    sm_d = dscr("sm", [NCH, 128, 384], F32)
    Sbssd_d = dscr("Sbssd", [NCH, 128, 2048], BF16)
    sT_d = dscr("sT", [2048, T], BF16)
    x1_d = dscr("x1", [T, 1024], F32)
    h2T_d = dscr("h2T", [1024, T], BF16)
    scr8_tk = Tk("scr8", multi=True)
    scr5_tk = Tk("scr5", multi=True)
    scr6_tk = Tk("scr6", multi=True)
    scr7_tk = Tk("scr7", multi=True)
    scr2_tk = Tk("scr2", multi=True)
    scr3_tk = Tk("scr3", multi=True)

    from contextlib import ExitStack
    with ExitStack() as es:
        def sb(name, shape, dt):
            return es.enter_context(nc.sbuf_tensor(_uniq(name), list(shape), dt))

        def ps(name, shape, dt):
            return es.enter_context(nc.psum_tensor("p_" + name, list(shape), dt))

        banks = [ps("bank%d" % i, [128, 512], F32) for i in range(8)]
        bank_tk = [Tk("bank%d" % i, excl=True) for i in range(8)]
        bank_rr = [0]

        def next_bank():
            i = bank_rr[0] % 8
            bank_rr[0] += 1
            return banks[i], bank_tk[i]

        ident_bf = sb("ident_bf", [128, 128], BF16)
        invf = sb("invf", [128, 1], F32)
        gmix = sb("gmix", [128, 8], F32)
        c_tk = Tk("consts")
        P.dma("sp", ident_bf[:], ident_bf_d, writes=[c_tk])
        P.dma("sp", invf[:], invf_d, writes=[c_tk])
        P.dma("sp", gmix[:], gmix_d, writes=[c_tk])

        es_h = ExitStack()
        hT = es_h.enter_context(nc.sbuf_tensor(_uniq("hT"), [128, 8, T], BF16))
        hT_tk = [Tk("hT%d" % c) for c in range(NCH)]
        with ExitStack() as es1:
            def sb1(name, shape, dt):
                return es1.enter_context(nc.sbuf_tensor(_uniq(name), list(shape), dt))
            NB = 3
            xt = [sb1("xt%d" % i, [128, D], F32) for i in range(NB)]
            xt_tk = [Tk() for _ in range(NB)]
            junk = sb1("junk", [128, D], BF16)
            junk_tk = Tk()
            st = [sb1("st%d" % i, [128, 4], F32) for i in range(NB)]
            st_tk = [Tk() for _ in range(NB)]
            hb = [sb1("hb%d" % i, [128, D], BF16) for i in range(NB)]
            hb_tk = [Tk() for _ in range(NB)]
            for c in range(NCH):
                i = c % NB
                P.dma("sp", xt[i][:], x_d[c * 128:(c + 1) * 128, :], writes=[xt_tk[i]])
                P.op("act", lambda e, i=i: e.activation(out=junk[:], in_=xt[i][:], func=AF.Square,
                                                        accum_out=st[i][:, 0:1]),
                     reads=[xt_tk[i]], writes=[junk_tk, st_tk[i]])
                P.op("act", lambda e, i=i: e.activation(out=st[i][:, 1:2], in_=st[i][:, 0:1], func=AF.Ln,
                                                        scale=1.0 / D, bias=EPS),
                     reads=[st_tk[i]], writes=[st_tk[i]])
                P.op("act", lambda e, i=i: e.activation(out=st[i][:, 2:3], in_=st[i][:, 1:2], func=AF.Exp,
                                                        scale=-0.5),
                     reads=[st_tk[i]], writes=[st_tk[i]])
                P.op("dve", lambda e, i=i: e.tensor_scalar(out=hb[i][:], in0=xt[i][:], scalar1=st[i][:, 2:3],
                                                          scalar2=None, op0=ALU.mult),
                     reads=[xt_tk[i], st_tk[i]], writes=[hb_tk[i]])
                bk, btk = next_bank()
                bkb = bk[:].bitcast(BF16)
                for k in range(8):
                    P.op("pe", lambda e, i=i, k=k, bkb=bkb: e.transpose(out=bkb[:, k * 128:(k + 1) * 128],
                                                                         in_=hb[i][:, k * 128:(k + 1) * 128],
                                                                         identity=ident_bf[:]),
                         reads=[hb_tk[i], c_tk], writes=[btk])
                P.op("dve" if c % 2 == 0 else "act",
                     (lambda e, bkb=bkb, c=c: e.tensor_copy(out=hT[:, :, c * 128:(c + 1) * 128],
                                                            in_=bkb.rearrange("p (k t) -> p k t", k=8)))
                     if c % 2 == 0 else
                     (lambda e, bkb=bkb, c=c: e.copy(out=hT[:, :, c * 128:(c + 1) * 128],
                                                     in_=bkb.rearrange("p (k t) -> p k t", k=8))),
                     reads=[btk], writes=[hT_tk[c]])
        P.barrier()
        if "hT" in dbg:
            hT_o = nc.dram_tensor("hT_o", [128, 8, T], BF16, kind="ExternalOutput").ap()
            P.dma("sp", hT_o, hT[:], reads=hT_tk)


        with ExitStack() as es2:
            def sb2(name, shape, dt):
                return es2.enter_context(nc.sbuf_tensor(_uniq(name), list(shape), dt))
            cosT = sb2("cosT", [128, T], F32)
            sinT = sb2("sinT", [128, T], F32)
            tab_tk = Tk("tab")
            with ExitStack() as es2a:
                def sb2a(name, shape, dt):
                    return es2a.enter_context(nc.sbuf_tensor(_uniq(name), list(shape), dt))
                posi = sb2a("posi", [128, T], I32)
                ang = sb2a("ang", [128, T], F32)
                tmpf = sb2a("tmpf", [128, T], F32)
                ki = sb2a("ki", [128, T], I32)
                tt_ = Tk()
                P.dma("sp", posi[:], pos_d.partition_broadcast(128), writes=[tt_])
                P.op("dve", lambda e: e.tensor_copy(out=ang[:], in_=posi[:]), reads=[tt_], writes=[tt_])
                P.op("dve", lambda e: e.tensor_scalar(out=ang[:], in0=ang[:], scalar1=invf[:, 0:1], scalar2=None,
                                                      op0=ALU.mult), reads=[tt_, c_tk], writes=[tt_])
                for (dst, shift) in ((sinT, 0.0), (cosT, math.pi / 2)):
                    P.op("dve", lambda e, shift=shift: e.tensor_scalar(out=tmpf[:], in0=ang[:], scalar1=shift,
                                                                       scalar2=None, op0=ALU.add),
                         reads=[tt_], writes=[tt_])
                    P.op("dve", lambda e: e.tensor_scalar(out=ki[:], in0=tmpf[:], scalar1=1.0 / (2 * math.pi),
                                                          scalar2=None, op0=ALU.mult), reads=[tt_], writes=[tt_])
                    P.op("dve", lambda e, dst=dst: e.tensor_copy(out=dst[:], in_=ki[:]), reads=[tt_], writes=[tt_])
                    P.op("dve", lambda e, dst=dst: e.scalar_tensor_tensor(out=dst[:], in0=dst[:], scalar=-2 * math.pi,
                                                                          in1=tmpf[:], op0=ALU.mult, op1=ALU.add),
                         reads=[tt_], writes=[tt_])
                    P.op("dve", lambda e, dst=dst: e.tensor_scalar(out=dst[:], in0=dst[:], scalar1=-math.pi,
                                                                   scalar2=math.pi, op0=ALU.max, op1=ALU.min),
                         reads=[tt_], writes=[tt_])
                    P.op("act", lambda e, dst=dst: e.activation(out=dst[:], in_=dst[:], func=AF.Sin),
                         reads=[tt_], writes=[tt_, tab_tk])
            P.barrier()

            wst = [sb2("wst%d" % i, [128, 8, 512], F32) for i in range(2)]
            wst_tk = [Tk() for _ in range(2)]
            wb = [sb2("wb%d" % i, [128, 8, 512], BF16) for i in range(2)]
            wb_tk = [[Tk() for _ in range(8)] for _ in range(2)]
            stg = [sb2("stg%d" % i, [128, 4, 512], BF16) for i in range(2)]
            stg_tk = [[Tk() for _ in range(4)] for _ in range(2)]
            rtmp = [[sb2("rt%d_%d" % (i, j), [128, 512], F32) for j in range(4)] for i in range(2)]
            rtmp_tk = [[Tk() for _ in range(4)] for _ in range(2)]
            dtall = sb2("dtall", [128, NCH, 64], F32)
            dtall_tk = Tk()
            dtb_bc = sb2("dtb_bc", [128, 64], F32)
            zpad = sb2("zpad", [128, 24, 2], BF16)
            P.dma("sp", dtb_bc[:], dtb_d.partition_broadcast(128), writes=[c_tk])
            zp_tk = Tk()
            P.op("pool", lambda e: e.memset(zpad[:], 0.0), writes=[zp_tk])
            xpad_tk = Tk(multi=True)
            P.dma("pool", xbcT_d[:, 0:2].rearrange("(c p) w -> p c w", p=128), zpad[:], reads=[zp_tk], writes=[xpad_tk])
            P.dma("pool", xbcT_d[:, T + 2:T + 4].rearrange("(c p) w -> p c w", p=128), zpad[:], reads=[zp_tk],
                  writes=[xpad_tk])

            blocks = []
            for i in range(2):
                blocks.append((OQ + i * 512, 512, "qk", qT_d, i * 512))
            for i in range(2):
                blocks.append((OK_ + i * 512, 512, "qk", kT_d, i * 512))
            for i in range(4):
                blocks.append((OV + i * 512, 512, "tokcopy", v_d, i * 512))
            for i in range(4):
                blocks.append((OG + i * 512, 512, "toksilu", g_d, i * 512))
            for i in range(4):
                blocks.append((OZ + i * 512, 512, "toksilu", z_d, i * 512))
            for i in range(6):
                blocks.append((OX + i * 512, 512, "Tcopy", xbcT_d, i * 512))
            blocks.append((ODT, 64, "dt", None, 0))
            for i in range(4):
                blocks.append((OGATE + i * 512, 512, "Tsig", gateT_d, i * 512))
            if stage < 2:
                blocks = []
            w_view = w_in_d.rearrange("(k p) c -> p k c", p=128)
            scr_tk = Tk("p2scratch", multi=True)
            stg_n = [0]
            ev_rr = [0]

            def load_w(bi):
                col0, ncols, kind, dst, d0 = blocks[bi]
                b = bi % 2
                P.dma("sp", wst[b][:, :, 0:ncols], w_view[:, :, col0:col0 + ncols], writes=[wst_tk[b]])
                for k in range(8):
                    if k % 2 == 0:
                        P.op("pool", lambda e, b=b, k=k, ncols=ncols: e.tensor_scalar(
                            out=wb[b][:, k, 0:ncols], in0=wst[b][:, k, 0:ncols], scalar1=gmix[:, k:k + 1],
                            scalar2=1.0, op0=ALU.mult, op1=ALU.mult),
                            reads=[wst_tk[b], c_tk], writes=[wb_tk[b][k]])
                    else:
                        P.op("act", lambda e, b=b, k=k, ncols=ncols: e.activation(
                            out=wb[b][:, k, 0:ncols], in_=wst[b][:, k, 0:ncols], func=AF.Copy,
                            scale=gmix[:, k:k + 1]),
                            reads=[wst_tk[b], c_tk], writes=[wb_tk[b][k]])

            if blocks:
                load_w(0)
            for bi in range(len(blocks)):
                col0, ncols, kind, dst, d0 = blocks[bi]
                b = bi % 2
                if bi + 1 < len(blocks):
                    load_w(bi + 1)
                if kind in ("qk", "Tcopy", "Tsig"):
                    for tt in range(8):
                        tsl = slice(tt * 512, (tt + 1) * 512)
                        bks = []
                        for cc in range(4):
                            bk, btk = next_bank()
                            bks.append((bk, btk))
                            for k in range(8):
                                P.op("pe", lambda e, bk=bk, b=b, k=k, cc=cc, tsl=tsl: e.matmul(
                                    bk[:], lhsT=wb[b][:, k, cc * 128:(cc + 1) * 128], rhs=hT[:, k, tsl],
                                    start=(k == 0), stop=(k == 7)),
                                    reads=[wb_tk[b][k]] + hT_tk[tt * 4:(tt + 1) * 4], writes=[btk])
                        si = stg_n[0] % 2
                        stg_n[0] += 1
                        if kind == "qk":
                            for hh in range(2):
                                (A, Atk), (B, Btk) = bks[2 * hh], bks[2 * hh + 1]
                                r = rtmp[hh]
                                rk = rtmp_tk[hh]
                                P.op("dve", lambda e, A=A, r=r, tsl=tsl: e.tensor_tensor(out=r[0][:], in0=A[:], in1=cosT[:, tsl], op=ALU.mult),
                                     reads=[Atk, tab_tk], writes=[rk[0]])
                                P.op("dve", lambda e, B=B, r=r, tsl=tsl: e.tensor_tensor(out=r[1][:], in0=B[:], in1=sinT[:, tsl], op=ALU.mult),
                                     reads=[Btk, tab_tk], writes=[rk[1]])
                                P.op("dve", lambda e, A=A, r=r, tsl=tsl: e.tensor_tensor(out=r[2][:], in0=A[:], in1=sinT[:, tsl], op=ALU.mult),
                                     reads=[Atk, tab_tk], writes=[rk[2]])
                                P.op("dve", lambda e, B=B, r=r, tsl=tsl: e.tensor_tensor(out=r[3][:], in0=B[:], in1=cosT[:, tsl], op=ALU.mult),
                                     reads=[Btk, tab_tk], writes=[rk[3]])
                                P.op("pool", lambda e, r=r, si=si, hh=hh: e.tensor_tensor(out=stg[si][:, 2 * hh, :], in0=r[0][:], in1=r[1][:], op=ALU.subtract),
                                     reads=[rk[0], rk[1]], writes=[stg_tk[si][2 * hh]])
                                P.op("pool", lambda e, r=r, si=si, hh=hh: e.tensor_tensor(out=stg[si][:, 2 * hh + 1, :], in0=r[2][:], in1=r[3][:], op=ALU.add),
                                     reads=[rk[2], rk[3]], writes=[stg_tk[si][2 * hh + 1]])
                        else:
                            for cc in range(4):
                                bk, btk = bks[cc]
                                if kind == "Tsig":
                                    P.op("act", lambda e, bk=bk, si=si, cc=cc: e.activation(out=stg[si][:, cc, :], in_=bk[:], func=AF.Sigmoid),
                                         reads=[btk], writes=[stg_tk[si][cc]])
                                elif cc % 2 == 0:
                                    P.op("dve", lambda e, bk=bk, si=si, cc=cc: e.tensor_copy(out=stg[si][:, cc, :], in_=bk[:]),
                                         reads=[btk], writes=[stg_tk[si][cc]])
                                else:
                                    P.op("act", lambda e, bk=bk, si=si, cc=cc: e.copy(out=stg[si][:, cc, :], in_=bk[:]),
                                         reads=[btk], writes=[stg_tk[si][cc]])
                        toff = 2 if kind == "Tcopy" else 0
                        P.dma("pool", dst[d0:d0 + 512, toff + tt * 512:toff + (tt + 1) * 512].rearrange("(c p) t -> p c t", p=128),
                              stg[si][:], reads=stg_tk[si], writes=[scr_tk])
                elif kind in ("tokcopy", "toksilu"):
                    for c in range(NCH):
                        if c % 4 == 0:
                            si = stg_n[0] % 2
                            stg_n[0] += 1
                        bk, btk = next_bank()
                        for k in range(8):
                            P.op("pe", lambda e, bk=bk, b=b, k=k, c=c: e.matmul(
                                bk[:], lhsT=hT[:, k, c * 128:(c + 1) * 128], rhs=wb[b][:, k, :],
                                start=(k == 0), stop=(k == 7)),
                                reads=[wb_tk[b][k], hT_tk[c]], writes=[btk])
                        q = c % 4
                        if kind == "toksilu":
                            P.op("act", lambda e, bk=bk, si=si, q=q: e.activation(out=stg[si][:, q, :], in_=bk[:], func=AF.Silu),
                                 reads=[btk], writes=[stg_tk[si][q]])
                        elif c % 2 == 0:
                            P.op("dve", lambda e, bk=bk, si=si, q=q: e.tensor_copy(out=stg[si][:, q, :], in_=bk[:]),
                                 reads=[btk], writes=[stg_tk[si][q]])
                        else:
                            P.op("act", lambda e, bk=bk, si=si, q=q: e.copy(out=stg[si][:, q, :], in_=bk[:]),
                                 reads=[btk], writes=[stg_tk[si][q]])
                        if q == 3:
                            tt = c // 4
                            P.dma("pool", dst[tt * 512:(tt + 1) * 512, d0:d0 + 512].rearrange("(q p) c -> p q c", p=128),
                                  stg[si][:], reads=stg_tk[si], writes=[scr_tk])
                elif kind == "dt":
                    for c in range(NCH):
                        bk, btk = next_bank()
                        for k in range(8):
                            P.op("pe", lambda e, bk=bk, b=b, k=k, c=c: e.matmul(
                                bk[:, 0:64], lhsT=hT[:, k, c * 128:(c + 1) * 128], rhs=wb[b][:, k, 0:64],
                                start=(k == 0), stop=(k == 7)),
                                reads=[wb_tk[b][k], hT_tk[c]], writes=[btk])
                        P.op("dve", lambda e, bk=bk, c=c: e.tensor_tensor(out=dtall[:, c, :], in0=bk[:, 0:64], in1=dtb_bc[:], op=ALU.add),
                             reads=[btk, c_tk], writes=[dtall_tk])
                    P.op("act", lambda e: e.activation(out=dtall[:], in_=dtall[:], func=AF.Exp), reads=[dtall_tk], writes=[dtall_tk])
                    P.op("act", lambda e: e.activation(out=dtall[:], in_=dtall[:], func=AF.Ln, bias=1.0), reads=[dtall_tk], writes=[dtall_tk])
                    P.dma("pool", dt_d.rearrange("(c p) h -> p c h", p=128), dtall[:], reads=[dtall_tk], writes=[scr_tk])
            P.barrier()

        es_h.close()
        if stage >= 3:
            GAM = [1.0 - 2.0 ** (-5 - h) for h in range(4)]
            CDEC = [g ** 128 for g in GAM]
            with ExitStack() as es3:
                def sb3(name, shape, dt):
                    return es3.enter_context(nc.sbuf_tensor(_uniq(name), list(shape), dt))
                dmat = sb3("dmat", [128, 4, 128], F32)
                qdf = sb3("qdf", [128, 8, 128], F32)
                qdb = sb3("qdb", [128, 8, 128], F32)
                kdec = sb3("kdec", [128, 8], F32)
                rc_tk = Tk()
                P.dma("sp", dmat[:], dmat_d, writes=[rc_tk])
                P.dma("sp", qdf[:], qdf_d, writes=[rc_tk])
                P.dma("sp", qdb[:], qdb_d, writes=[rc_tk])
                P.dma("sp", kdec[:], kdec_d, writes=[rc_tk])
                kT_v = kT_d.rearrange("(k p) t -> p k t", p=128)
                qT_v = qT_d.rearrange("(k p) t -> p k t", p=128)
                rT_v = rT_d.rearrange("(e p) t -> p e t", p=128)
                G2 = 256
                with ExitStack() as es3a:
                    def sba(name, shape, dt):
                        return es3a.enter_context(nc.sbuf_tensor(_uniq(name), list(shape), dt))
                    Sb = sba("Sb", [128, 8, 512], F32)
                    Sb_tk = [Tk() for _ in range(8)]
                    Sbb = [sba("Sbb%d" % i, [128, 8, 512], BF16) for i in range(2)]
                    Sbb_tk = [[Tk() for _ in range(8)] for _ in range(2)]
                    kTg = [sba("kTg%d" % i, [128, 8, G2], BF16) for i in range(2)]
                    kTg_tk = [Tk() for _ in range(2)]
                    vg = [sba("vg%d" % i, [128, 2, 2048], BF16) for i in range(2)]
                    vg_tk = [Tk() for _ in range(2)]
                    ktok = [sba("ktok%d" % i, [128, 1024], BF16) for i in range(2)]
                    ktok_tk = [Tk() for _ in range(2)]
                    kb = [sba("kb%d" % i, [128, 1024], BF16) for i in range(2)]
                    kb_tk = [[Tk() for _ in range(4)] for _ in range(2)]
                    P.op("pool", lambda e: e.memset(Sb[:], 0.0), writes=Sb_tk)
                    for c in range(NCH - 1, -1, -1):
                        gi = c // 2
                        gb = gi % 2
                        ci = c % 2
                        if ci == 1:
                            P.dma("sp", kTg[gb][:], kT_v[:, :, gi * G2:(gi + 1) * G2], reads=[scr_tk], writes=[kTg_tk[gb]])
                            P.dma("sp", vg[gb][:], v_d[gi * G2:(gi + 1) * G2, :].rearrange("(q p) c -> p q c", p=128),
                                  reads=[scr_tk], writes=[vg_tk[gb]])
                        cb_ = c % 2
                        bk, btk = next_bank()
                        bkb = bk[:].bitcast(BF16)
                        for kk in range(8):
                            P.op("pe", lambda e, bkb=bkb, kk=kk, gb=gb, ci=ci: e.transpose(
                                out=bkb[:, kk * 128:(kk + 1) * 128], in_=kTg[gb][:, kk, ci * 128:(ci + 1) * 128],
                                identity=ident_bf[:]), reads=[kTg_tk[gb], c_tk], writes=[btk])
                        P.op("act", lambda e, bkb=bkb, cb_=cb_: e.copy(out=ktok[cb_][:], in_=bkb), reads=[btk], writes=[ktok_tk[cb_]])
                        P.dma("pool", ktok_d[c * 128:(c + 1) * 128, :], ktok[cb_][:], reads=[ktok_tk[cb_]], writes=[scr2_tk])
                        for h in range(4):
                            P.op("act", lambda e, bkb=bkb, cb_=cb_, h=h: e.activation(
                                out=kb[cb_][:, h * 256:(h + 1) * 256], in_=bkb[:, h * 256:(h + 1) * 256],
                                func=AF.Copy, scale=kdec[:, 4 + h:5 + h]),
                                reads=[btk, rc_tk], writes=[kb_tk[cb_][h]])
                        for idx in range(8):
                            h = idx // 2
                            P.op("act", lambda e, cb_=cb_, idx=idx: e.copy(out=Sbb[cb_][:, idx, :], in_=Sb[:, idx, :]),
                                 reads=[Sb_tk[idx]], writes=[Sbb_tk[cb_][idx]])
                            bk2, btk2 = next_bank()
                            P.op("pe", lambda e, bk2=bk2, cb_=cb_, idx=idx, h=h, gb=gb, ci=ci: e.matmul(
                                bk2[:], lhsT=kb[cb_][:, idx * 128:(idx + 1) * 128], rhs=vg[gb][:, ci, h * 512:(h + 1) * 512],
                                start=True, stop=True), reads=[kb_tk[cb_][h], vg_tk[gb]], writes=[btk2])
                            P.op("dve", lambda e, bk2=bk2, idx=idx, h=h: e.scalar_tensor_tensor(
                                out=Sb[:, idx, :], in0=Sb[:, idx, :], scalar=CDEC[h], in1=bk2[:], op0=ALU.mult, op1=ALU.add),
                                reads=[btk2, Sb_tk[idx]], writes=[Sb_tk[idx]])
                        P.dma("pool", Sbret_d[c], Sbb[cb_][:].rearrange("p a b -> p (a b)"), reads=Sbb_tk[cb_], writes=[scr2_tk])
                P.barrier()
                with ExitStack() as es3b:
                  if stage >= 4:
                      def sbb_(name, shape, dt):
                          return es3b.enter_context(nc.sbuf_tensor(_uniq(name), list(shape), dt))
                      Sf = sbb_("Sf", [128, 8, 512], F32)
                      Sf_tk = [Tk() for _ in range(8)]
                      Sfb2 = [sbb_("Sfb%d" % i, [128, 8, 512], BF16) for i in range(2)]
                      Sfb2_tk = [[Tk() for _ in range(8)] for _ in range(2)]
                      SbL = [sbb_("SbL%d" % i, [128, 8, 512], BF16) for i in range(2)]
                      SbL_tk = [Tk() for _ in range(2)]
                      qTg = [sbb_("qTg%d" % i, [128, 8, G2], BF16) for i in range(2)]
                      kTg = [sbb_("kTg%d" % i, [128, 8, G2], BF16) for i in range(2)]
                      ktg = [sbb_("ktg%d" % i, [128, 2, 1024], BF16) for i in range(2)]
                      vg = [sbb_("vg%d" % i, [128, 2, 2048], BF16) for i in range(2)]
                      gg = [sbb_("gg%d" % i, [128, 2, 2048], BF16) for i in range(2)]
                      ld_tk = [Tk() for _ in range(2)]
                      Pm = [sbb_("Pm%d" % i, [128, 4, 128], BF16) for i in range(2)]
                      Pm_tk = [Tk() for _ in range(2)]
                      qf = [sbb_("qf%d" % i, [128, 8, 128], BF16) for i in range(2)]
                      qf_tk = [Tk() for _ in range(2)]
                      qb = [sbb_("qb%d" % i, [128, 8, 128], BF16) for i in range(2)]
                      qb_tk = [Tk() for _ in range(2)]
                      kf = [sbb_("kf%d" % i, [128, 1024], BF16) for i in range(2)]
                      kf_tk = [[Tk() for _ in range(4)] for _ in range(2)]
                      stats = [sbb_("stats%d" % i, [128, 4, 6], F32) for i in range(2)]
                      mv = [sbb_("mv%d" % i, [128, 4, 2], F32) for i in range(2)]
                      rs = [sbb_("rs%d" % i, [128, 12], F32) for i in range(2)]
                      st_tk = [[Tk() for _ in range(4)] for _ in range(2)]
                      rs_tk = [Tk() for _ in range(2)]
                      yn = [sbb_("yn%d" % i, [128, 512], F32) for i in range(4)]
                      yn_tk = [Tk() for _ in range(4)]
                      rr = [sbb_("rr%d" % i, [128, 2048], BF16) for i in range(2)]
                      rr_tk = [[Tk() for _ in range(4)] for _ in range(2)]
                      rTs = [sbb_("rTs%d" % i, [128, 16, G2], BF16) for i in range(2)]
                      rTs_tk = [[Tk() for _ in range(4)] for _ in range(2)]
                      P.op("pool", lambda e: e.memset(Sf[:], 0.0), writes=Sf_tk)
                      P.op("pool", lambda e: e.memset(Sfb2[0][:], 0.0), writes=Sfb2_tk[0])
                      yn_n = [0]
                      for c in range(NCH):
                          gi = c // 2
                          gb = gi % 2
                          ci = c % 2
                          cb_ = c % 2
                          csl = slice(ci * 128, (ci + 1) * 128)
                          if ci == 0:
                              gs = slice(gi * G2, (gi + 1) * G2)
                              P.dma("sp", qTg[gb][:], qT_v[:, :, gs], reads=[scr_tk], writes=[ld_tk[gb]])
                              P.dma("sp", kTg[gb][:], kT_v[:, :, gs], reads=[scr_tk], writes=[ld_tk[gb]])
                              P.dma("sp", ktg[gb][:], ktok_d[gs, :].rearrange("(q p) c -> p q c", p=128), reads=[scr2_tk], writes=[ld_tk[gb]])
                              P.dma("sp", vg[gb][:], v_d[gs, :].rearrange("(q p) c -> p q c", p=128), reads=[scr_tk], writes=[ld_tk[gb]])
                              P.dma("sp", gg[gb][:], g_d[gs, :].rearrange("(q p) c -> p q c", p=128), reads=[scr_tk], writes=[ld_tk[gb]])
                          P.dma("sp", SbL[cb_][:].rearrange("p a b -> p (a b)"), Sbret_d[c], reads=[scr2_tk], writes=[SbL_tk[cb_]])
                          bkS, btkS = next_bank()
                          for h in range(4):
                              for dc in range(2):
                                  P.op("pe", lambda e, bkS=bkS, h=h, dc=dc, gb=gb, csl=csl: e.matmul(
                                      bkS[:, h * 128:(h + 1) * 128], lhsT=kTg[gb][:, 2 * h + dc, csl], rhs=qTg[gb][:, 2 * h + dc, csl],
                                      start=(dc == 0), stop=(dc == 1)), reads=[ld_tk[gb]], writes=[btkS])
                          P.op("dve", lambda e, bkS=bkS, cb_=cb_: e.tensor_tensor(
                              out=Pm[cb_][:].rearrange("p a b -> p (a b)"), in0=bkS[:], in1=dmat[:].rearrange("p a b -> p (a b)"),
                              op=ALU.mult), reads=[btkS, rc_tk], writes=[Pm_tk[cb_]])
                          P.op("dve", lambda e, cb_=cb_, gb=gb, csl=csl: e.tensor_tensor(
                              out=qf[cb_][:], in0=qTg[gb][:, :, csl], in1=qdf[:], op=ALU.mult),
                              reads=[ld_tk[gb], rc_tk], writes=[qf_tk[cb_]])
                          P.op("pool", lambda e, cb_=cb_, gb=gb, csl=csl: e.tensor_tensor(
                              out=qb[cb_][:], in0=qTg[gb][:, :, csl], in1=qdb[:], op=ALU.mult),
                              reads=[ld_tk[gb], rc_tk], writes=[qb_tk[cb_]])
                          for h in range(4):
                              P.op("act", lambda e, cb_=cb_, h=h, gb=gb, ci=ci: e.activation(
                                  out=kf[cb_][:, h * 256:(h + 1) * 256], in_=ktg[gb][:, ci, h * 256:(h + 1) * 256],
                                  func=AF.Copy, scale=kdec[:, h:h + 1]),
                                  reads=[ld_tk[gb], rc_tk], writes=[kf_tk[cb_][h]])
                          for idx in range(8):
                              h = idx // 2
                              bk2, btk2 = next_bank()
                              P.op("pe", lambda e, bk2=bk2, cb_=cb_, idx=idx, h=h, gb=gb, ci=ci: e.matmul(
                                  bk2[:], lhsT=kf[cb_][:, idx * 128:(idx + 1) * 128], rhs=vg[gb][:, ci, h * 512:(h + 1) * 512],
                                  start=True, stop=True), reads=[kf_tk[cb_][h], ld_tk[gb]], writes=[btk2])
                              P.op("dve", lambda e, bk2=bk2, idx=idx, h=h: e.scalar_tensor_tensor(
                                  out=Sf[:, idx, :], in0=Sf[:, idx, :], scalar=CDEC[h], in1=bk2[:], op0=ALU.mult, op1=ALU.add),
                                  reads=[btk2, Sf_tk[idx]], writes=[Sf_tk[idx]])
                              P.op("act", lambda e, idx=idx, cb_=cb_: e.copy(out=Sfb2[1 - cb_][:, idx, :], in_=Sf[:, idx, :]),
                                   reads=[Sf_tk[idx]], writes=[Sfb2_tk[1 - cb_][idx]])
                          ybk = []
                          for h in range(4):
                              bk, btk = next_bank()
                              ybk.append((bk, btk))
                              P.op("pe", lambda e, bk=bk, h=h, cb_=cb_, gb=gb, ci=ci: e.matmul(
                                  bk[:], lhsT=Pm[cb_][:, h, :], rhs=vg[gb][:, ci, h * 512:(h + 1) * 512], start=True, stop=False),
                                  reads=[Pm_tk[cb_], ld_tk[gb]], writes=[btk])
                              for dc in range(2):
                                  P.op("pe", lambda e, bk=bk, h=h, dc=dc, cb_=cb_: e.matmul(
                                      bk[:], lhsT=qf[cb_][:, 2 * h + dc, :], rhs=Sfb2[cb_][:, 2 * h + dc, :], start=False, stop=False),
                                      reads=[qf_tk[cb_], Sfb2_tk[cb_][2 * h + dc]], writes=[btk])
                              for dc in range(2):
                                  P.op("pe", lambda e, bk=bk, h=h, dc=dc, cb_=cb_: e.matmul(
                                      bk[:], lhsT=qb[cb_][:, 2 * h + dc, :], rhs=SbL[cb_][:, 2 * h + dc, :], start=False, stop=(dc == 1)),
                                      reads=[qb_tk[cb_], SbL_tk[cb_]], writes=[btk])
                              P.op("dve", lambda e, bk=bk, h=h, cb_=cb_: e.bn_stats(out=stats[cb_][:, h, :], in_=bk[:]),
                                   reads=[btk], writes=[st_tk[cb_][h]])
                              P.op("dve", lambda e, h=h, cb_=cb_: e.bn_aggr(out=mv[cb_][:, h, :], in_=stats[cb_][:, h, :]),
                                   reads=[st_tk[cb_][h]], writes=[st_tk[cb_][h]])
                          P.op("act", lambda e, cb_=cb_: e.activation(out=rs[cb_][:, 0:4], in_=mv[cb_][:, :, 1], func=AF.Ln, bias=EPS),
                               reads=st_tk[cb_], writes=[rs_tk[cb_]])
                          P.op("act", lambda e, cb_=cb_: e.activation(out=rs[cb_][:, 4:8], in_=rs[cb_][:, 0:4], func=AF.Exp, scale=-0.5),
                               reads=[rs_tk[cb_]], writes=[rs_tk[cb_]])
                          P.op("dve", lambda e, cb_=cb_: e.scalar_tensor_tensor(
                              out=rs[cb_][:, 8:12], in0=mv[cb_][:, :, 0], scalar=-1.0, in1=rs[cb_][:, 4:8], op0=ALU.mult, op1=ALU.mult),
                              reads=st_tk[cb_] + [rs_tk[cb_]], writes=[rs_tk[cb_]])
                          for h in range(4):
                              bk, btk = ybk[h]
                              yi = yn_n[0] % 4
                              yn_n[0] += 1
                              P.op("act", lambda e, bk=bk, h=h, cb_=cb_, yi=yi: e.activation(
                                  out=yn[yi][:], in_=bk[:], func=AF.Identity, scale=rs[cb_][:, 4 + h:5 + h], bias=rs[cb_][:, 8 + h:9 + h]),
                                  reads=[btk, rs_tk[cb_]], writes=[yn_tk[yi]])
                              P.op("dve", lambda e, h=h, cb_=cb_, yi=yi, gb=gb, ci=ci: e.tensor_tensor(
                                  out=rr[cb_][:, h * 512:(h + 1) * 512], in0=yn[yi][:], in1=gg[gb][:, ci, h * 512:(h + 1) * 512], op=ALU.mult),
                                  reads=[yn_tk[yi], ld_tk[gb]], writes=[rr_tk[cb_][h]])
                          for half_ in range(2):
                              bk, btk = next_bank()
                              bkb = bk[:].bitcast(BF16)
                              for j in range(8):
                                  e_ = half_ * 8 + j
                                  P.op("pe", lambda e, bkb=bkb, j=j, e_=e_, cb_=cb_: e.transpose(
                                      out=bkb[:, j * 128:(j + 1) * 128], in_=rr[cb_][:, e_ * 128:(e_ + 1) * 128], identity=ident_bf[:]),
                                      reads=[rr_tk[cb_][e_ // 4], c_tk], writes=[btk])
                              P.op("act" if half_ == 0 else "dve",
                                   (lambda e, bkb=bkb, gb=gb, half_=half_, csl=csl: e.copy(
                                       out=rTs[gb][:, half_ * 8:(half_ + 1) * 8, csl], in_=bkb.rearrange("p (a b) -> p a b", a=8)))
                                   if half_ == 0 else
                                   (lambda e, bkb=bkb, gb=gb, half_=half_, csl=csl: e.tensor_copy(
                                       out=rTs[gb][:, half_ * 8:(half_ + 1) * 8, csl], in_=bkb.rearrange("p (a b) -> p a b", a=8))),
                                   reads=[btk], writes=[rTs_tk[gb][ci * 2 + half_]])
                          if ci == 1:
                              P.dma("pool", rT_v[:, :, gi * G2:(gi + 1) * G2], rTs[gb][:], reads=rTs_tk[gb], writes=[scr3_tk])
                P.barrier()

        if stage >= 5:
            with ExitStack() as es5:
                def sb5(name, shape, dt):
                    return es5.enter_context(nc.sbuf_tensor(_uniq(name), list(shape), dt))
                maskF = sb5("maskF", [128, 128], F32)
                maskB = sb5("maskB", [128, 128], F32)
                sc_tk = Tk()
                P.dma("sp", maskF[:], maskF_d, writes=[sc_tk])
                P.dma("sp", maskB[:], maskB_d, writes=[sc_tk])
                BCT_v = BCT_d.rearrange("(k p) t -> p k t", p=128)
                sT_v = sT_d.rearrange("(e p) t -> p e t", p=128)
                with ExitStack() as es5a:
                    def sba(name, shape, dt):
                        return es5a.enter_context(nc.sbuf_tensor(_uniq(name), list(shape), dt))
                    ones_f = sba("ones_f", [128, 128], F32)
                    cw = sba("cw", [128, 24, 5], F32)
                    cbc = sba("cbc", [128, 24], F32)
                    a_bc = sba("a_bc", [128, 64], F32)
                    diag = sba("diag", [128, 24, 5, 128], BF16)
                    P.op("pool", lambda e: e.memset(ones_f[:], 1.0), writes=[sc_tk])
                    P.dma("sp", cw[:], cw_d, writes=[sc_tk])
                    P.dma("sp", cbc[:], cbc_d, writes=[sc_tk])
                    P.dma("sp", a_bc[:], alog_d.partition_broadcast(128), writes=[sc_tk])
                    P.op("act", lambda e: e.activation(out=a_bc[:], in_=a_bc[:], func=AF.Exp), reads=[sc_tk], writes=[sc_tk])
                    P.op("dve", lambda e: e.tensor_scalar(out=a_bc[:], in0=a_bc[:], scalar1=-1.0, scalar2=None, op0=ALU.mult),
                         reads=[sc_tk], writes=[sc_tk])
                    for cc in range(24):
                        for w in range(5):
                            eng = ("dve", "pool")[(cc * 5 + w) % 2]
                            P.op(eng, lambda e, cc=cc, w=w: e.tensor_scalar(
                                out=diag[:, cc, w, :], in0=ident_bf[:], scalar1=cw[:, cc, w:w + 1], scalar2=1.0,
                                op0=ALU.mult, op1=ALU.mult), reads=[sc_tk, c_tk], writes=[sc_tk])
                    with ExitStack() as es5d:
                        def sbd(name, shape, dt):
                            return es5d.enter_context(nc.sbuf_tensor(_uniq(name), list(shape), dt))
                        sm = sbd("sm", [128, NCH, 384], F32)
                        dta = sbd("dta", [128, NCH, 64], F32)
                        laa = sbd("laa", [128, NCH, 128], F32)
                        cts = [sbd("cts%d" % i, [64, 128], F32) for i in range(2)]
                        cts_tk = [Tk() for _ in range(2)]
                        sm_tk = Tk()
                        dta_tk = Tk()
                        P.dma("sp", dta[:], dt_d.rearrange("(c p) h -> p c h", p=128), reads=[scr_tk], writes=[dta_tk])
                        P.op("pool", lambda e: e.memset(laa[:], 0.0), writes=[dta_tk])
                        P.op("dve", lambda e: e.tensor_tensor(out=laa[:, :, 0:64], in0=dta[:], in1=a_bc[:].unsqueeze(1).to_broadcast([128, NCH, 64]),
                                                              op=ALU.mult), reads=[dta_tk, sc_tk], writes=[dta_tk])
                        for c in range(NCH):
                            bk, btk = next_bank()
                            P.op("pe", lambda e, bk=bk, c=c: e.matmul(bk[:, 0:32], lhsT=maskF[:], rhs=laa[:, c, 0:32], start=True, stop=True),
                                 reads=[sc_tk, dta_tk], writes=[btk])
                            P.op("pe", lambda e, bk=bk, c=c: e.matmul(bk[:, 32:64], lhsT=maskB[:], rhs=laa[:, c, 32:64], start=True, stop=True),
                                 reads=[sc_tk, dta_tk], writes=[btk])
                            P.op("pe", lambda e, bk=bk, c=c: e.matmul(bk[:, 64:128], lhsT=ones_f[:], rhs=laa[:, c, 0:64], start=True, stop=True),
                                 reads=[sc_tk, dta_tk], writes=[btk])
                            P.op("pe", lambda e, bk=bk, c=c: e.matmul(bk[:, 128:256], lhsT=laa[:, c, :], rhs=maskF[:], start=True, stop=True),
                                 reads=[sc_tk, dta_tk], writes=[btk])
                            P.op("pe", lambda e, bk=bk, c=c: e.matmul(bk[:, 256:384], lhsT=laa[:, c, :], rhs=maskB[:], start=True, stop=True),
                                 reads=[sc_tk, dta_tk], writes=[btk])
                            P.op("dve", lambda e, bk=bk, c=c: e.tensor_copy(out=sm[:, c, 0:128], in_=bk[:, 0:128]), reads=[btk], writes=[sm_tk])
                            ci_ = c % 2
                            P.op("act", lambda e, bk=bk, ci_=ci_: e.copy(out=cts[ci_][0:32, :], in_=bk[0:32, 128:256]),
                                 reads=[btk], writes=[cts_tk[ci_]])
                            P.op("act", lambda e, bk=bk, ci_=ci_: e.copy(out=cts[ci_][32:64, :], in_=bk[32:64, 256:384]),
                                 reads=[btk, cts_tk[ci_]], writes=[cts_tk[ci_]])
                            P.dma("pool", cumT_d[c].rearrange("d h i -> (d h) i"), cts[ci_][:], reads=[cts_tk[ci_]], writes=[scr5_tk])
                        P.op("act", lambda e: e.activation(out=sm[:, :, 128:192], in_=sm[:, :, 0:64], func=AF.Exp), reads=[sm_tk], writes=[sm_tk])
                        P.op("act", lambda e: e.activation(out=sm[:, :, 192:256], in_=sm[:, :, 64:128], func=AF.Exp), reads=[sm_tk], writes=[sm_tk])
                        P.op("dve", lambda e: e.tensor_tensor(out=sm[:, :, 256:320], in0=sm[:, :, 64:128], in1=sm[:, :, 0:64], op=ALU.subtract),
                             reads=[sm_tk], writes=[sm_tk])
                        P.op("act", lambda e: e.activation(out=sm[:, :, 256:320], in_=sm[:, :, 256:320], func=AF.Exp), reads=[sm_tk], writes=[sm_tk])
                        P.op("dve", lambda e: e.tensor_tensor(out=sm[:, :, 256:320], in0=sm[:, :, 256:320], in1=dta[:], op=ALU.mult),
                             reads=[sm_tk, dta_tk], writes=[sm_tk])
                        P.op("dve", lambda e: e.tensor_copy(out=sm[:, :, 64:128], in_=dta[:]), reads=[sm_tk, dta_tk], writes=[sm_tk])
                        P.op("act", lambda e: e.activation(out=sm[:, :, 352:384], in_=dta[:, :, 0:32], func=AF.Ln), reads=[sm_tk, dta_tk], writes=[sm_tk])
                        P.op("dve", lambda e: e.tensor_tensor(out=sm[:, :, 320:352], in0=sm[:, :, 352:384], in1=sm[:, :, 0:32], op=ALU.subtract),
                             reads=[sm_tk], writes=[sm_tk])
                        P.dma("pool", sm_d.rearrange("c p k -> p c k"), sm[:], reads=[sm_tk], writes=[scr5_tk])
                    P.barrier()
                    with ExitStack() as es5c:
                        def sbc(name, shape, dt):
                            return es5c.enter_context(nc.sbuf_tensor(_uniq(name), list(shape), dt))
                        xwin = [sbc("xwin%d" % i, [128, 24, 516], BF16) for i in range(2)]
                        xwin_tk = [Tk() for _ in range(2)]
                        bcs = [sbc("bcs%d" % i, [128, 24, 512], BF16) for i in range(2)]
                        bcs_tk = [[Tk() for _ in range(24)] for _ in range(2)]
                        xts = [sbc("xts%d" % i, [128, 2560], BF16) for i in range(3)]
                        xts_tk = [[Tk() for _ in range(3)] for _ in range(3)]
                        xbc_v = xbcT_d.rearrange("(k p) t -> p k t", p=128)
                        xn = [0]
                        S6 = sbc("S6", [128, 4, 512], F32)
                        S6_tk = [Tk() for _ in range(4)]
                        Sbf6 = [sbc("Sbf6_%d" % i, [128, 4, 512], BF16) for i in range(2)]
                        Sbf6_tk = [[Tk() for _ in range(4)] for _ in range(2)]
                        smc6 = [sbc("smc6_%d" % i, [128, 384], F32) for i in range(2)]
                        smc6_tk = [Tk() for _ in range(2)]
                        xs6 = [sbc("xs6_%d" % i, [128, 2048], BF16) for i in range(2)]
                        xs6_tk = [Tk() for _ in range(2)]
                        P.op("pool", lambda e: e.memset(S6[:], 0.0), writes=S6_tk)
                        def conv_load(tt):
                            wb_ = tt % 2
                            P.dma("sp", xwin[wb_][:], xbc_v[:, :, tt * 512:tt * 512 + 516], reads=[scr_tk, xpad_tk], writes=[xwin_tk[wb_]])

                        def conv_part(tt, part):
                            wb_ = tt % 2
                            for cc in range(part * 6, part * 6 + 6):
                                bk, btk = next_bank()
                                for w in range(5):
                                    P.op("pe", lambda e, bk=bk, cc=cc, w=w, wb_=wb_: e.matmul(
                                        bk[:], lhsT=diag[:, cc, w, :], rhs=xwin[wb_][:, cc, w:w + 512], start=(w == 0), stop=(w == 4)),
                                        reads=[sc_tk, xwin_tk[wb_]], writes=[btk])
                                P.op("act", lambda e, bk=bk, cc=cc, wb_=wb_: e.activation(
                                    out=bcs[wb_][:, cc, :], in_=bk[:], func=AF.Silu, bias=cbc[:, cc:cc + 1]),
                                    reads=[btk, sc_tk], writes=[bcs_tk[wb_][cc]])
                            if part == 3:
                                P.dma("pool", BCT_v[:, :, tt * 512:(tt + 1) * 512], bcs[wb_][:, 16:24, :], reads=bcs_tk[wb_][16:24], writes=[scr5_tk])

                        def chunk_work(tt, q):
                            wb_ = tt % 2
                            c = tt * 4 + q
                            xb_ = xn[0] % 3
                            xn[0] += 1
                            for grp in range(3):
                                n_ = 8 if grp < 2 else 4
                                bk, btk = next_bank()
                                bkb = bk[:].bitcast(BF16)
                                for j in range(n_):
                                    cc = grp * 8 + j
                                    P.op("pe", lambda e, bkb=bkb, cc=cc, j=j, q=q, wb_=wb_: e.transpose(
                                        out=bkb[:, j * 128:(j + 1) * 128], in_=bcs[wb_][:, cc, q * 128:(q + 1) * 128], identity=ident_bf[:]),
                                        reads=[bcs_tk[wb_][cc], c_tk], writes=[btk])
                                if grp % 2 == 0:
                                    P.op("dve", lambda e, bkb=bkb, grp=grp, xb_=xb_, n_=n_: e.tensor_copy(
                                        out=xts[xb_][:, grp * 1024:grp * 1024 + n_ * 128], in_=bkb[:, 0:n_ * 128]),
                                        reads=[btk], writes=[xts_tk[xb_][grp]])
                                else:
                                    P.op("act", lambda e, bkb=bkb, grp=grp, xb_=xb_, n_=n_: e.copy(
                                        out=xts[xb_][:, grp * 1024:grp * 1024 + n_ * 128], in_=bkb[:, 0:n_ * 128]),
                                        reads=[btk], writes=[xts_tk[xb_][grp]])
                            P.dma("pool", xtok_d[c * 128:(c + 1) * 128, :], xts[xb_][:], reads=xts_tk[xb_], writes=[scr5_tk])
                            return c, xb_

                        def s_phase(c, xb_):
                            sb_ = c % 2
                            P.dma("sp", smc6[sb_][:], sm_d[c], reads=[], writes=[smc6_tk[sb_]])
                            P.op("dve", lambda e, xb_=xb_, sb_=sb_: e.tensor_tensor(
                                out=xs6[sb_][:].rearrange("p (h d) -> p h d", h=32), in0=xts[xb_][:, 0:2048].rearrange("p (h d) -> p h d", h=32),
                                in1=smc6[sb_][:, 288:320].unsqueeze(2).to_broadcast([128, 32, 64]), op=ALU.mult),
                                reads=xts_tk[xb_] + [smc6_tk[sb_]], writes=[xs6_tk[sb_]])
                            for g in range(4):
                                P.op("act", lambda e, xb_=xb_, sb_=sb_, g=g: e.copy(out=Sbf6[sb_][:, g, :], in_=S6[:, g, :]), reads=[S6_tk[g]], writes=[Sbf6_tk[sb_][g]])
                                bk, btk = next_bank()
                                P.op("pe", lambda e, bk=bk, xb_=xb_, sb_=sb_, g=g: e.matmul(
                                    bk[:], lhsT=xts[xb_][:, 2048 + g * 128:2048 + (g + 1) * 128], rhs=xs6[sb_][:, g * 512:(g + 1) * 512],
                                    start=True, stop=True), reads=[xts_tk[xb_][2], xs6_tk[sb_]], writes=[btk])
                                P.op("dve", lambda e, xb_=xb_, sb_=sb_, g=g: e.tensor_tensor(
                                    out=S6[:, g, :].rearrange("p (h d) -> p h d", h=8), in0=S6[:, g, :].rearrange("p (h d) -> p h d", h=8),
                                    in1=smc6[sb_][:, 224 + g * 8:224 + (g + 1) * 8].unsqueeze(2).to_broadcast([128, 8, 64]), op=ALU.mult),
                                    reads=[S6_tk[g], smc6_tk[sb_]], writes=[S6_tk[g]])
                                P.op("dve", lambda e, bk=bk, g=g: e.tensor_tensor(out=S6[:, g, :], in0=S6[:, g, :], in1=bk[:], op=ALU.add),
                                     reads=[btk, S6_tk[g]], writes=[S6_tk[g]])
                            P.dma("pool", Sbssd_d[c], Sbf6[sb_][:].rearrange("p a b -> p (a b)"), reads=Sbf6_tk[sb_], writes=[scr6_tk])

                        conv_load(7)
                        for part in range(4):
                            conv_part(7, part)
                        pend = None
                        for tt in range(7, -1, -1):
                            if tt > 0:
                                conv_load(tt - 1)
                            for idx, q in enumerate((3, 2, 1, 0)):
                                if tt > 0:
                                    conv_part(tt - 1, idx)
                                cur = chunk_work(tt, q)
                                if pend is not None:
                                    s_phase(*pend)
                                pend = cur
                        s_phase(*pend)
                    P.barrier()
                P.barrier()
                d_bc = sb5("d_bc", [128, 32], F32)
                P.dma("sp", d_bc[:], dskip_d.partition_broadcast(128), writes=[sc_tk])

                if stage >= 7:
                    with ExitStack() as es7:
                        def sb7(name, shape, dt):
                            return es7.enter_context(nc.sbuf_tensor(_uniq(name), list(shape), dt))
                        cumT2 = cumT_d.rearrange("c d h i -> (c d) (h i)")
                        NBC = 3
                        bc = [sb7("bc%d" % i, [128, 32, 128], F32) for i in range(NBC)]
                        bc_tk = [[Tk() for _ in range(32)] for _ in range(NBC)]
                        L = [[sb7("L%d_%d" % (d, i), [128, 32, 128], BF16) for i in range(2)] for d in range(2)]
                        L_tk = [[Tk() for _ in range(2)] for _ in range(2)]
                        M_tk = [[[Tk() for _ in range(4)] for _ in range(2)] for _ in range(2)]
                        xt = [sb7("xt%d" % i, [128, 2560], BF16) for i in range(3)]
                        smc = [sb7("smc%d" % i, [128, 384], F32) for i in range(3)]
                        bct = [sb7("bct%d" % i, [128, 8, 128], BF16) for i in range(3)]
                        zt = [sb7("zt%d" % i, [128, 2048], BF16) for i in range(3)]
                        Sbl = [sb7("Sbl%d" % i, [128, 2048], BF16) for i in range(3)]
                        ld_tk = [Tk() for _ in range(3)]
                        ld2_tk = [Tk() for _ in range(3)]
                        cbF = [sb7("cbF%d" % i, [128, 4, 128], BF16) for i in range(2)]
                        cbB = [sb7("cbB%d" % i, [128, 4, 128], BF16) for i in range(2)]
                        cb_tk = [Tk() for _ in range(2)]
                        xdtb = [sb7("xdtb%d" % i, [128, 2048], BF16) for i in range(2)]
                        xdtb_tk = [Tk() for _ in range(2)]
                        xd = [sb7("xd%d" % i, [128, 2048], BF16) for i in range(2)]
                        xd_tk = [Tk() for _ in range(2)]
                        t1 = sb7("t1", [128, 4, 512], BF16)
                        t1_tk = [Tk() for _ in range(4)]
                        t2 = sb7("t2", [128, 4, 512], BF16)
                        t2_tk = [Tk() for _ in range(4)]
                        acc = sb7("acc", [128, 4, 512], F32)
                        acc_tk = [Tk() for _ in range(4)]
                        junk7 = sb7("junk7", [128, 512], BF16)
                        junk7_tk = Tk()
                        ms = [sb7("ms%d" % i, [128, 12], F32) for i in range(2)]
                        ms_tk = [[Tk() for _ in range(4)] for _ in range(2)]
                        rs_tk = [Tk() for _ in range(2)]
                        sbf = [sb7("sbf%d" % i, [128, 2048], BF16) for i in range(2)]
                        sbf_tk = [[Tk() for _ in range(4)] for _ in range(2)]
                        sTs = [sb7("sTs%d" % i, [128, 16, 128], BF16) for i in range(2)]
                        sTs_tk = [[Tk() for _ in range(2)] for _ in range(2)]
                        S = sb7("S", [128, 4, 512], F32)
                        S_tk = [Tk() for _ in range(4)]
                        Sfb2 = [sb7("Sfb%d" % i, [128, 4, 512], BF16) for i in range(2)]
                        Sfb2_tk = [[Tk() for _ in range(4)] for _ in range(2)]
                        xs = sb7("xs", [128, 2048], BF16)
                        xs_tk = Tk()
                        P.op("pool", lambda e: e.memset(S[:], 0.0), writes=S_tk)
                        P.op("pool", lambda e: e.memset(Sfb2[0][:], 0.0), writes=Sfb2_tk[0])
                        bc_n = [0]
                        bcmap = {}

                        def loads(c):
                            l_ = c % 3
                            P.dma("sp", xt[l_][:], xtok_d[c * 128:(c + 1) * 128, :], reads=[scr5_tk], writes=[ld_tk[l_]])
                            P.dma("sp", smc[l_][:], sm_d[c], reads=[scr5_tk], writes=[ld_tk[l_]])
                            P.dma("sp", bct[l_][:], BCT_v[:, :, c * 128:(c + 1) * 128], reads=[scr5_tk], writes=[ld_tk[l_]])
                            bcb = []
                            for d in range(2):
                                k_ = bc_n[0] % NBC
                                bc_n[0] += 1
                                bcb.append(k_)
                                P.dma("sp", bc[k_][:].rearrange("p h i -> p (h i)"),
                                      cumT2[c * 2 + d:c * 2 + d + 1, :].partition_broadcast(128), reads=[scr5_tk], writes=bc_tk[k_])
                            bcmap[c] = bcb
                            P.dma("sp", zt[l_][:], z_d[c * 128:(c + 1) * 128, :], reads=[scr_tk], writes=[ld2_tk[l_]])
                            P.dma("sp", Sbl[l_][:], Sbssd_d[c], reads=[scr6_tk], writes=[ld2_tk[l_]])

                        def stageA1(c):
                            b_ = c % 2
                            l_ = c % 3
                            bcb = bcmap[c]
                            bk, btk = next_bank()
                            for g in range(4):
                                P.op("pe", lambda e, bk=bk, g=g, l_=l_: e.matmul(
                                    bk[:, g * 128:(g + 1) * 128], lhsT=bct[l_][:, g, :], rhs=bct[l_][:, 4 + g, :], start=True, stop=True),
                                    reads=[ld_tk[l_]], writes=[btk])
                            P.op("dve", lambda e, bk=bk, b_=b_: e.tensor_tensor(
                                out=cbF[b_][:], in0=bk[:].rearrange("p (g i) -> p g i", g=4),
                                in1=maskF[:].unsqueeze(1).to_broadcast([128, 4, 128]), op=ALU.mult), reads=[btk, sc_tk], writes=[cb_tk[b_]])
                            P.op("dve", lambda e, bk=bk, b_=b_: e.tensor_tensor(
                                out=cbB[b_][:], in0=bk[:].rearrange("p (g i) -> p g i", g=4),
                                in1=maskB[:].unsqueeze(1).to_broadcast([128, 4, 128]), op=ALU.mult), reads=[btk, sc_tk, cb_tk[b_]], writes=[cb_tk[b_]])
                            for d in range(2):
                                k_ = bcb[d]
                                for h in range(32):
                                    if d == 0:
                                        P.op("dve", lambda e, k_=k_, h=h, l_=l_: e.tensor_scalar(
                                            out=bc[k_][:, h, :], in0=bc[k_][:, h, :], scalar1=smc[l_][:, 320 + h:321 + h],
                                            scalar2=smc[l_][:, 352 + h:353 + h], op0=ALU.add, op1=ALU.min),
                                            reads=[bc_tk[k_][h], ld_tk[l_]], writes=[bc_tk[k_][h]])
                                    else:
                                        P.op("act", lambda e, k_=k_, h=h, l_=l_: e.activation(
                                            out=bc[k_][:, h, :], in_=bc[k_][:, h, :], func=AF.Relu, scale=-1.0, bias=smc[l_][:, 32 + h:33 + h]),
                                            reads=[bc_tk[k_][h], ld_tk[l_]], writes=[bc_tk[k_][h]])
                                P.op("act", lambda e, k_=k_, d=d, b_=b_: e.activation(out=L[d][b_][:], in_=bc[k_][:], func=AF.Exp, scale=(1.0 if d == 0 else -1.0)),
                                     reads=bc_tk[k_], writes=[L_tk[d][b_]] + M_tk[d][b_])

                        def stageA2(c):
                            b_ = c % 2
                            l_ = c % 3
                            for d in range(2):
                                cbx = cbF if d == 0 else cbB
                                for g in range(4):
                                    P.op("dve", lambda e, d=d, g=g, b_=b_, cbx=cbx: e.tensor_tensor(
                                        out=L[d][b_][:, g * 8:(g + 1) * 8, :], in0=L[d][b_][:, g * 8:(g + 1) * 8, :],
                                        in1=cbx[b_][:, g, :].unsqueeze(1).to_broadcast([128, 8, 128]), op=ALU.mult),
                                        reads=[L_tk[d][b_], cb_tk[b_]], writes=[M_tk[d][b_][g]])
                            P.op("dve", lambda e, b_=b_, l_=l_: e.tensor_tensor(
                                out=xdtb[b_][:].rearrange("p (h q) -> p h q", h=32), in0=xt[l_][:, 0:2048].rearrange("p (h q) -> p h q", h=32),
                                in1=smc[l_][:, 96:128].unsqueeze(2).to_broadcast([128, 32, 64]), op=ALU.mult),
                                reads=[ld_tk[l_]], writes=[xdtb_tk[b_]])
                            P.op("dve", lambda e, b_=b_, l_=l_: e.tensor_tensor(
                                out=xd[b_][:].rearrange("p (h q) -> p h q", h=32), in0=xt[l_][:, 0:2048].rearrange("p (h q) -> p h q", h=32),
                                in1=d_bc[:].unsqueeze(2).to_broadcast([128, 32, 64]), op=ALU.mult),
                                reads=[ld_tk[l_], sc_tk], writes=[xd_tk[b_]])

                        def stageB(c):
                            b_ = c % 2
                            l_ = c % 3
                            P.op("dve", lambda e, l_=l_: e.tensor_tensor(
                                out=xs[:].rearrange("p (h q) -> p h q", h=32), in0=xt[l_][:, 0:2048].rearrange("p (h q) -> p h q", h=32),
                                in1=smc[l_][:, 256:288].unsqueeze(2).to_broadcast([128, 32, 64]), op=ALU.mult),
                                reads=[ld_tk[l_]], writes=[xs_tk])
                            for g in range(4):
                                bk, btk = next_bank()
                                P.op("pe", lambda e, bk=bk, g=g, l_=l_: e.matmul(
                                    bk[:], lhsT=xt[l_][:, 2048 + g * 128:2048 + (g + 1) * 128], rhs=xs[:, g * 512:(g + 1) * 512], start=True, stop=True),
                                    reads=[ld_tk[l_], xs_tk], writes=[btk])
                                P.op("dve", lambda e, g=g, l_=l_: e.tensor_tensor(
                                    out=S[:, g, :].rearrange("p (h q) -> p h q", h=8), in0=S[:, g, :].rearrange("p (h q) -> p h q", h=8),
                                    in1=smc[l_][:, 192 + g * 8:192 + (g + 1) * 8].unsqueeze(2).to_broadcast([128, 8, 64]), op=ALU.mult),
                                    reads=[S_tk[g], ld_tk[l_]], writes=[S_tk[g]])
                                P.op("dve", lambda e, bk=bk, g=g: e.tensor_tensor(out=S[:, g, :], in0=S[:, g, :], in1=bk[:], op=ALU.add),
                                     reads=[btk, S_tk[g]], writes=[S_tk[g]])
                                P.op("act", lambda e, g=g, b_=b_: e.copy(out=Sfb2[1 - b_][:, g, :], in_=S[:, g, :]), reads=[S_tk[g]], writes=[Sfb2_tk[1 - b_][g]])

                            for g in range(4):
                                bk, btk = next_bank()
                                P.op("pe", lambda e, bk=bk, g=g, l_=l_, b_=b_: e.matmul(bk[:], lhsT=bct[l_][:, 4 + g, :], rhs=Sfb2[b_][:, g, :], start=True, stop=True),
                                     reads=[ld_tk[l_], Sfb2_tk[b_][g]], writes=[btk])
                                P.op("dve", lambda e, bk=bk, g=g, l_=l_: e.tensor_tensor(
                                    out=t1[:, g, :].rearrange("p (h q) -> p h q", h=8), in0=bk[:].rearrange("p (h q) -> p h q", h=8),
                                    in1=smc[l_][:, 128 + g * 8:128 + (g + 1) * 8].unsqueeze(2).to_broadcast([128, 8, 64]), op=ALU.mult),
                                    reads=[btk, ld_tk[l_]], writes=[t1_tk[g]])
                            for g in range(4):
                                bk, btk = next_bank()
                                P.op("pe", lambda e, bk=bk, g=g, l_=l_: e.matmul(bk[:], lhsT=bct[l_][:, 4 + g, :], rhs=Sbl[l_][:, g * 512:(g + 1) * 512],
                                                                              start=True, stop=True), reads=[ld_tk[l_], ld2_tk[l_]], writes=[btk])
                                P.op("dve", lambda e, bk=bk, g=g, l_=l_: e.tensor_tensor(
                                    out=t2[:, g, :].rearrange("p (h q) -> p h q", h=8), in0=bk[:].rearrange("p (h q) -> p h q", h=8),
                                    in1=smc[l_][:, 160 + g * 8:160 + (g + 1) * 8].unsqueeze(2).to_broadcast([128, 8, 64]), op=ALU.mult),
                                    reads=[btk, ld_tk[l_]], writes=[t2_tk[g]])
                            for g in range(4):
                                bk, btk = next_bank()
                                P.op("pe", lambda e, bk=bk, g=g, b_=b_: e.matmul(bk[:], lhsT=ident_bf[:], rhs=xd[b_][:, g * 512:(g + 1) * 512], start=True, stop=False),
                                     reads=[xd_tk[b_], c_tk], writes=[btk])
                                for hh in range(8):
                                    h = g * 8 + hh
                                    P.op("pe", lambda e, bk=bk, hh=hh, h=h, b_=b_, l_=l_: e.matmul(
                                        bk[:, hh * 64:(hh + 1) * 64], lhsT=L[0][b_][:, h, :], rhs=xt[l_][:, h * 64:(h + 1) * 64],
                                        start=False, stop=False), reads=[M_tk[0][b_][g], ld_tk[l_]], writes=[btk])
                                    P.op("pe", lambda e, bk=bk, hh=hh, h=h, b_=b_: e.matmul(
                                        bk[:, hh * 64:(hh + 1) * 64], lhsT=L[1][b_][:, h, :], rhs=xdtb[b_][:, h * 64:(h + 1) * 64],
                                        start=False, stop=False), reads=[M_tk[1][b_][g], xdtb_tk[b_]], writes=[btk])
                                P.op("pe", lambda e, bk=bk, g=g: e.matmul(bk[:], lhsT=ident_bf[:], rhs=t1[:, g, :], start=False, stop=False),
                                     reads=[t1_tk[g], c_tk], writes=[btk])
                                P.op("pe", lambda e, bk=bk, g=g: e.matmul(bk[:], lhsT=ident_bf[:], rhs=t2[:, g, :], start=False, stop=True),
                                     reads=[t2_tk[g], c_tk], writes=[btk])
                                P.op("dve", lambda e, bk=bk, g=g, l_=l_: e.tensor_tensor(out=acc[:, g, :], in0=bk[:], in1=zt[l_][:, g * 512:(g + 1) * 512], op=ALU.mult),
                                     reads=[btk, ld2_tk[l_]], writes=[acc_tk[g]])
                                P.op("act", lambda e, g=g, b_=b_: e.activation(out=junk7[:], in_=acc[:, g, :], func=AF.Square, accum_out=ms[b_][:, g:g + 1]),
                                     reads=[acc_tk[g]], writes=[junk7_tk, ms_tk[b_][g]])
                            P.op("act", lambda e, b_=b_: e.activation(out=ms[b_][:, 4:8], in_=ms[b_][:, 0:4], func=AF.Ln, scale=1.0 / 512, bias=EPS),
                                 reads=ms_tk[b_], writes=[rs_tk[b_]])
                            P.op("act", lambda e, b_=b_: e.activation(out=ms[b_][:, 8:12], in_=ms[b_][:, 4:8], func=AF.Exp, scale=-0.5),
                                 reads=[rs_tk[b_]], writes=[rs_tk[b_]])
                            for g in range(4):
                                P.op("act", lambda e, g=g, b_=b_: e.activation(out=sbf[b_][:, g * 512:(g + 1) * 512], in_=acc[:, g, :], func=AF.Copy,
                                                                                scale=ms[b_][:, 8 + g:9 + g]),
                                     reads=[acc_tk[g], rs_tk[b_]], writes=[sbf_tk[b_][g]])
                            for half_ in range(2):
                                bk, btk = next_bank()
                                bkb = bk[:].bitcast(BF16)
                                for j in range(8):
                                    e_ = half_ * 8 + j
                                    P.op("pe", lambda e, bkb=bkb, j=j, e_=e_, b_=b_: e.transpose(
                                        out=bkb[:, j * 128:(j + 1) * 128], in_=sbf[b_][:, e_ * 128:(e_ + 1) * 128], identity=ident_bf[:]),
                                        reads=[sbf_tk[b_][e_ // 4], c_tk], writes=[btk])
                                P.op("act", lambda e, bkb=bkb, b_=b_, half_=half_: e.copy(
                                    out=sTs[b_][:, half_ * 8:(half_ + 1) * 8, :], in_=bkb.rearrange("p (a b) -> p a b", a=8)),
                                    reads=[btk], writes=[sTs_tk[b_][half_]])
                            P.dma("pool", sT_v[:, :, c * 128:(c + 1) * 128], sTs[b_][:], reads=sTs_tk[b_], writes=[scr7_tk])

                        loads(0)
                        stageA1(0)
                        loads(1)
                        stageA2(0)
                        for c in range(NCH):
                            if c + 1 < NCH:
                                stageA1(c + 1)
                            if c + 2 < NCH:
                                loads(c + 2)
                            stageB(c)
                            if c + 1 < NCH:
                                stageA2(c + 1)
                    P.barrier()

        TT = 256

        def load_cast(wdst, w_dram_v, nk, ncol, gain, stg, stg_tk, wtk, rr):
            kper = max(1, 4096 // ncol)
            for k0 in range(0, nk, kper):
                i = rr[0] % len(stg)
                rr[0] += 1
                P.dma("sp", stg[i][:, 0:kper, 0:ncol], w_dram_v[:, k0:k0 + kper, :], writes=[stg_tk[i]])
                for kk in range(kper):
                    k = k0 + kk
                    eng = ("act", "pool", "dve")[k % 3]
                    if gain is None:
                        if eng == "act":
                            P.op("act", lambda e, i=i, kk=kk, k=k: e.copy(out=wdst[:, k, :], in_=stg[i][:, kk, 0:ncol]), reads=[stg_tk[i]], writes=[wtk])
                        else:
                            P.op(eng, lambda e, i=i, kk=kk, k=k: e.tensor_copy(out=wdst[:, k, :], in_=stg[i][:, kk, 0:ncol]), reads=[stg_tk[i]], writes=[wtk])
                    elif eng == "act":
                        P.op("act", lambda e, i=i, kk=kk, k=k: e.activation(out=wdst[:, k, :], in_=stg[i][:, kk, 0:ncol], func=AF.Copy,
                                                                            scale=gain[:, k:k + 1]), reads=[stg_tk[i], c_tk], writes=[wtk])
                    else:
                        P.op(eng, lambda e, i=i, kk=kk, k=k: e.tensor_scalar(out=wdst[:, k, :], in0=stg[i][:, kk, 0:ncol], scalar1=gain[:, k:k + 1],
                                                                             scalar2=1.0, op0=ALU.mult, op1=ALU.mult), reads=[stg_tk[i], c_tk], writes=[wtk])

        if stage >= 8:
            with ExitStack() as es8:
                def sb8(name, shape, dt):
                    return es8.enter_context(nc.sbuf_tensor(_uniq(name), list(shape), dt))
                gains = sb8("gains", [128, 40], F32)
                P.dma("sp", gains[:], gains_d, writes=[c_tk])
                Wr = sb8("Wr", [128, 16, 1024], BF16)
                Ws = sb8("Ws", [128, 16, 1024], BF16)
                Wo = sb8("Wo", [128, 8, 1024], BF16)
                w8_tk = Tk(multi=True)
                with ExitStack() as es8s:
                    stg = [es8s.enter_context(nc.sbuf_tensor(_uniq("wstg"), [128, 4, 1024], F32)) for _ in range(4)]
                    stg_tk = [Tk() for _ in range(4)]
                    rr_ = [0]
                    load_cast(Wr, w_ret_o_d.rearrange("(k p) c -> p k c", p=128), 16, 1024, gains[:, 0:16], stg, stg_tk, w8_tk, rr_)
                    load_cast(Ws, w_ssm_o_d.rearrange("(k p) c -> p k c", p=128), 16, 1024, gains[:, 16:32], stg, stg_tk, w8_tk, rr_)
                    load_cast(Wo, w_out_d.rearrange("(k p) c -> p k c", p=128), 8, 1024, None, stg, stg_tk, w8_tk, rr_)
                    P.barrier()
                rT_v8 = rT_d.rearrange("(e p) t -> p e t", p=128)
                sT_v8 = sT_d.rearrange("(e p) t -> p e t", p=128)
                gT_v8 = gateT_d.rearrange("(e p) t -> p e t", p=128)
                h2T_v = h2T_d.rearrange("(k p) t -> p k t", p=128)
                rTt = [sb8("rTt%d" % i, [128, 16, TT], BF16) for i in range(2)]
                sTt = [sb8("sTt%d" % i, [128, 16, TT], BF16) for i in range(2)]
                gTt = [sb8("gTt%d" % i, [128, 16, TT], BF16) for i in range(2)]
                ld_tk = [Tk() for _ in range(2)]
                t1 = [sb8("t1_%d" % i, [128, TT], F32) for i in range(2)]
                t2 = [sb8("t2_%d" % i, [128, TT], F32) for i in range(2)]
                t_tk = [[Tk(), Tk()] for _ in range(2)]
                mixT = [sb8("mixT%d" % i, [128, 8, TT], BF16) for i in range(2)]
                mix_tk = [[Tk() for _ in range(8)] for _ in range(2)]
                xin = [sb8("xin%d" % i, [128, 1024], F32) for i in range(2)]
                xin_tk = [Tk() for _ in range(2)]
                x1t = [sb8("x1t%d" % i, [128, 1024], F32) for i in range(2)]
                x1_tk = [[Tk(), Tk()] for _ in range(2)]
                junk8 = sb8("junk8", [128, 1024], BF16)
                junk8_tk = Tk()
                st8 = [sb8("st8_%d" % i, [128, 4], F32) for i in range(2)]
                st8_tk = [Tk() for _ in range(2)]
                h2 = [sb8("h2_%d" % i, [128, 1024], BF16) for i in range(2)]
                h2_tk = [Tk() for _ in range(2)]
                h2Ts = [sb8("h2Ts%d" % i, [128, 8, 128], BF16) for i in range(2)]
                h2Ts_tk = [Tk() for _ in range(2)]
                tn = [0]
                def p8_loads(tt):
                    b_ = tt % 2
                    tsl = slice(tt * TT, (tt + 1) * TT)
                    P.dma("sp", rTt[b_][:], rT_v8[:, :, tsl], reads=[scr3_tk], writes=[ld_tk[b_]])
                    P.dma("sp", sTt[b_][:], sT_v8[:, :, tsl], reads=[scr7_tk], writes=[ld_tk[b_]])
                    P.dma("sp", gTt[b_][:], gT_v8[:, :, tsl], reads=[scr_tk], writes=[ld_tk[b_]])
                def p8_head(tt, d0, d1):
                    b_ = tt % 2
                    for dch in range(d0, d1):
                        bR, bRtk = next_bank()
                        for e_ in range(16):
                            P.op("pe", lambda e, bR=bR, e_=e_, dch=dch, b_=b_: e.matmul(
                                bR[:, 0:TT], lhsT=Wr[:, e_, dch * 128:(dch + 1) * 128], rhs=rTt[b_][:, e_, :], start=(e_ == 0), stop=(e_ == 15)),
                                reads=[w8_tk, ld_tk[b_]], writes=[bRtk])
                        bS, bStk = next_bank()
                        for e_ in range(16):
                            P.op("pe", lambda e, bS=bS, e_=e_, dch=dch, b_=b_: e.matmul(
                                bS[:, 0:TT], lhsT=Ws[:, e_, dch * 128:(dch + 1) * 128], rhs=sTt[b_][:, e_, :], start=(e_ == 0), stop=(e_ == 15)),
                                reads=[w8_tk, ld_tk[b_]], writes=[bStk])
                        ti = tn[0] % 2
                        tn[0] += 1
                        P.op("dve", lambda e, bR=bR, ti=ti, dch=dch, b_=b_: e.tensor_tensor(out=t1[ti][:], in0=bR[:, 0:TT], in1=gTt[b_][:, dch, :], op=ALU.mult),
                             reads=[bRtk, ld_tk[b_]], writes=[t_tk[ti][0]])
                        P.op("dve", lambda e, bS=bS, ti=ti, dch=dch, b_=b_: e.tensor_tensor(out=t2[ti][:], in0=bS[:, 0:TT], in1=gTt[b_][:, 8 + dch, :], op=ALU.mult),
                             reads=[bStk, ld_tk[b_]], writes=[t_tk[ti][1]])
                        P.op("dve", lambda e, ti=ti, dch=dch, b_=b_: e.tensor_tensor(out=mixT[b_][:, dch, :], in0=t1[ti][:], in1=t2[ti][:], op=ALU.add),
                             reads=t_tk[ti], writes=[mix_tk[b_][dch]])
                def p8_tail(tt):
                    b_ = tt % 2
                    for q in range(TT // 128):
                        c = tt * (TT // 128) + q
                        cb_ = c % 2
                        P.dma("sp", xin[cb_][:], x_d[c * 128:(c + 1) * 128, :], writes=[xin_tk[cb_]])
                        for hf in range(2):
                            bk, btk = next_bank()
                            for d_ in range(8):
                                P.op("pe", lambda e, bk=bk, d_=d_, q=q, hf=hf, b_=b_: e.matmul(
                                    bk[:], lhsT=mixT[b_][:, d_, q * 128:(q + 1) * 128], rhs=Wo[:, d_, hf * 512:(hf + 1) * 512],
                                    start=(d_ == 0), stop=(d_ == 7)), reads=[mix_tk[b_][d_], w8_tk], writes=[btk])
                            P.op("dve", lambda e, bk=bk, hf=hf, cb_=cb_: e.tensor_tensor(
                                out=x1t[cb_][:, hf * 512:(hf + 1) * 512], in0=bk[:], in1=xin[cb_][:, hf * 512:(hf + 1) * 512], op=ALU.add),
                                reads=[btk, xin_tk[cb_]], writes=[x1_tk[cb_][hf]])
                        P.dma("pool", x1_d[c * 128:(c + 1) * 128, :], x1t[cb_][:], reads=x1_tk[cb_], writes=[scr8_tk])
                        P.op("act", lambda e, cb_=cb_: e.activation(out=junk8[:], in_=x1t[cb_][:], func=AF.Square, accum_out=st8[cb_][:, 0:1]),
                             reads=x1_tk[cb_], writes=[junk8_tk, st8_tk[cb_]])
                        P.op("act", lambda e, cb_=cb_: e.activation(out=st8[cb_][:, 1:2], in_=st8[cb_][:, 0:1], func=AF.Ln, scale=1.0 / D, bias=EPS),
                             reads=[st8_tk[cb_]], writes=[st8_tk[cb_]])
                        P.op("act", lambda e, cb_=cb_: e.activation(out=st8[cb_][:, 2:3], in_=st8[cb_][:, 1:2], func=AF.Exp, scale=-0.5),
                             reads=[st8_tk[cb_]], writes=[st8_tk[cb_]])
                        P.op("dve", lambda e, cb_=cb_: e.tensor_scalar(out=h2[cb_][:], in0=x1t[cb_][:], scalar1=st8[cb_][:, 2:3], scalar2=None, op0=ALU.mult),
                             reads=x1_tk[cb_] + [st8_tk[cb_]], writes=[h2_tk[cb_]])
                        bk, btk = next_bank()
                        bkb = bk[:].bitcast(BF16)
                        for k in range(8):
                            P.op("pe", lambda e, bkb=bkb, k=k, cb_=cb_: e.transpose(out=bkb[:, k * 128:(k + 1) * 128], in_=h2[cb_][:, k * 128:(k + 1) * 128],
                                                                                 identity=ident_bf[:]), reads=[h2_tk[cb_], c_tk], writes=[btk])
                        P.op("act", lambda e, bkb=bkb, cb_=cb_: e.copy(out=h2Ts[cb_][:], in_=bkb.rearrange("p (a b) -> p a b", a=8)),
                             reads=[btk], writes=[h2Ts_tk[cb_]])
                        P.dma("pool", h2T_v[:, :, c * 128:(c + 1) * 128], h2Ts[cb_][:], reads=[h2Ts_tk[cb_]], writes=[scr8_tk])
                NT8 = T // TT
                p8_loads(0)
                p8_head(0, 0, 8)
                for tt in range(NT8):
                    if tt + 1 < NT8:
                        p8_loads(tt + 1)
                        p8_head(tt + 1, 0, 4)
                    p8_tail(tt)
                    if tt + 1 < NT8:
                        p8_head(tt + 1, 4, 8)
                P.barrier()

        if stage >= 9:
            with ExitStack() as es9:
                def sb9(name, shape, dt):
                    return es9.enter_context(nc.sbuf_tensor(_uniq(name), list(shape), dt))
                gains9 = sb9("gains9", [128, 40], F32)
                gfin = sb9("gfin", [128, 1024], F32)
                P.dma("sp", gains9[:], gains_d, writes=[c_tk])
                P.dma("sp", gfin[:], gfin_d.partition_broadcast(128), writes=[c_tk])
                Wu = sb9("Wu", [128, 8, 4096], BF16)
                Wd = sb9("Wd", [128, 32, 1024], BF16)
                w9_tk = Tk(multi=True)
                with ExitStack() as es9s:
                    stg = [es9s.enter_context(nc.sbuf_tensor(_uniq("wstg9"), [128, 4, 1024], F32)) for _ in range(4)]
                    stg_tk = [Tk() for _ in range(4)]
                    rr_ = [0]
                    load_cast(Wu, w_up_d.rearrange("(k p) c -> p k c", p=128), 8, 4096, gains9[:, 32:40],
                              [t_[:].rearrange("p a b -> p (a b)").rearrange("p (a b) -> p a b", a=1) for t_ in stg], stg_tk, w9_tk, rr_)
                    load_cast(Wd, w_dn_d.rearrange("(k p) c -> p k c", p=128), 32, 1024, None, stg, stg_tk, w9_tk, rr_)
                    P.barrier()
                h2T_v9 = h2T_d.rearrange("(k p) t -> p k t", p=128)
                hT9 = [sb9("hT9_%d" % i, [128, 8, TT], BF16) for i in range(2)]
                hT9_tk = [Tk() for _ in range(2)]
                uT = [sb9("uT%d" % i, [128, 32, TT], BF16) for i in range(2)]
                uT_tk = [[Tk() for _ in range(32)] for _ in range(2)]
                rl = [sb9("rl%d" % i, [128, TT], F32) for i in range(4)]
                rl_tk = [Tk() for _ in range(4)]
                x1i = [sb9("x1i%d" % i, [128, 1024], F32) for i in range(2)]
                x1i_tk = [Tk() for _ in range(2)]
                x2 = [sb9("x2_%d" % i, [128, 1024], F32) for i in range(2)]
                x2_tk = [[Tk(), Tk()] for _ in range(2)]
                junk9 = sb9("junk9", [128, 1024], BF16)
                junk9_tk = Tk()
                st9 = [sb9("st9_%d" % i, [128, 4], F32) for i in range(2)]
                st9_tk = [Tk() for _ in range(2)]
                ot = [sb9("ot%d" % i, [128, 1024], F32) for i in range(2)]
                ot_tk = [Tk() for _ in range(2)]
                rn = [0]
                out_tk = Tk(multi=True)
                for tt in range(T // TT):
                    b_ = tt % 2
                    tsl = slice(tt * TT, (tt + 1) * TT)
                    P.dma("sp", hT9[b_][:], h2T_v9[:, :, tsl], reads=[scr8_tk], writes=[hT9_tk[b_]])
                    for f in range(32):
                        bk, btk = next_bank()
                        for d_ in range(8):
                            P.op("pe", lambda e, bk=bk, d_=d_, f=f, b_=b_: e.matmul(
                                bk[:, 0:TT], lhsT=Wu[:, d_, f * 128:(f + 1) * 128], rhs=hT9[b_][:, d_, :], start=(d_ == 0), stop=(d_ == 7)),
                                reads=[w9_tk, hT9_tk[b_]], writes=[btk])
                        ri = rn[0] % 4
                        rn[0] += 1
                        P.op("act", lambda e, bk=bk, ri=ri: e.activation(out=rl[ri][:], in_=bk[:, 0:TT], func=AF.Relu), reads=[btk], writes=[rl_tk[ri]])
                        P.op("dve", lambda e, ri=ri, f=f, b_=b_: e.tensor_tensor(out=uT[b_][:, f, :], in0=rl[ri][:], in1=rl[ri][:], op=ALU.mult),
                             reads=[rl_tk[ri]], writes=[uT_tk[b_][f]])
                    for q in range(TT // 128):
                        c = tt * (TT // 128) + q
                        cb_ = c % 2
                        P.dma("sp", x1i[cb_][:], x1_d[c * 128:(c + 1) * 128, :], reads=[scr8_tk], writes=[x1i_tk[cb_]])
                        for hf in range(2):
                            bk, btk = next_bank()
                            for f in range(32):
                                P.op("pe", lambda e, bk=bk, f=f, q=q, hf=hf, b_=b_: e.matmul(
                                    bk[:], lhsT=uT[b_][:, f, q * 128:(q + 1) * 128], rhs=Wd[:, f, hf * 512:(hf + 1) * 512],
                                    start=(f == 0), stop=(f == 31)), reads=[uT_tk[b_][f], w9_tk], writes=[btk])
                            P.op("dve", lambda e, bk=bk, hf=hf, cb_=cb_: e.tensor_tensor(
                                out=x2[cb_][:, hf * 512:(hf + 1) * 512], in0=bk[:], in1=x1i[cb_][:, hf * 512:(hf + 1) * 512], op=ALU.add),
                                reads=[btk, x1i_tk[cb_]], writes=[x2_tk[cb_][hf]])
                        P.op("act", lambda e, cb_=cb_: e.activation(out=junk9[:], in_=x2[cb_][:], func=AF.Square, accum_out=st9[cb_][:, 0:1]),
                             reads=x2_tk[cb_], writes=[junk9_tk, st9_tk[cb_]])
                        P.op("act", lambda e, cb_=cb_: e.activation(out=st9[cb_][:, 1:2], in_=st9[cb_][:, 0:1], func=AF.Ln, scale=1.0 / D, bias=EPS),
                             reads=[st9_tk[cb_]], writes=[st9_tk[cb_]])
                        P.op("act", lambda e, cb_=cb_: e.activation(out=st9[cb_][:, 2:3], in_=st9[cb_][:, 1:2], func=AF.Exp, scale=-0.5),
                             reads=[st9_tk[cb_]], writes=[st9_tk[cb_]])
                        P.op("dve", lambda e, cb_=cb_: e.scalar_tensor_tensor(out=ot[cb_][:], in0=x2[cb_][:], scalar=st9[cb_][:, 2:3], in1=gfin[:],
                                                                              op0=ALU.mult, op1=ALU.mult),
                             reads=x2_tk[cb_] + [st9_tk[cb_], c_tk], writes=[ot_tk[cb_]])
                        P.dma("pool", out_d[c * 128:(c + 1) * 128, :], ot[cb_][:], reads=[ot_tk[cb_]], writes=[out_tk])
                P.barrier()
        P.finish()
    nc._in_names = in_names
    return nc


def col128(v):
    v = np.asarray(v, dtype=np.float32).reshape(-1, 128)
    return np.ascontiguousarray(v.T)


def shared_inputs(inp):
    m = dict(host_consts())
    m["w_in"] = np.ascontiguousarray(inp["w_in"][0], dtype=np.float32)
    m["gmix"] = col128(inp["norm_mix_g"][0])
    m["dtb"] = np.concatenate([inp["dt_bias_f"][0], inp["dt_bias_b"][0]]).reshape(1, 64).astype(np.float32)
    m["cw"] = np.ascontiguousarray(np.asarray(inp["conv_w"][0], np.float32).reshape(5, 24, 128).transpose(2, 1, 0))
    m["cbc"] = col128(inp["conv_b"][0])
    m["cbr"] = np.asarray(inp["conv_b"][0], np.float32).reshape(1, 3072)
    m["alog"] = np.concatenate([inp["a_log_f"][0], inp["a_log_b"][0]]).reshape(1, 64).astype(np.float32)
    m["dskip"] = np.asarray(inp["ssm_d"][0], np.float32).reshape(1, 32)
    m["gains"] = np.concatenate([col128(inp["ret_gn_g"][0]), col128(inp["ssm_norm_g"][0]), col128(inp["norm_mlp_g"][0])], 1)
    m["gfin"] = np.asarray(inp["norm_final_g"], np.float32).reshape(1, 1024)
    m["w_ret_o"] = np.ascontiguousarray(inp["w_ret_o"][0], dtype=np.float32)
    m["w_ssm_o"] = np.ascontiguousarray(inp["w_ssm_o"][0], dtype=np.float32)
    m["w_out"] = np.ascontiguousarray(inp["w_out"][0], dtype=np.float32)
    m["w_up"] = np.ascontiguousarray(inp["w_mlp_up"][0], dtype=np.float32)
    m["w_dn"] = np.ascontiguousarray(inp["w_mlp_down"][0], dtype=np.float32)
    return m


def core_inputs(inp, b, shared, names):
    m = {k: v for k, v in shared.items() if k in names}
    m["x"] = np.ascontiguousarray(inp["x"][b], dtype=np.float32)
    m["pos"] = np.ascontiguousarray(inp["positions"][b], dtype=np.int32).reshape(1, T)
    return m


_NC_CACHE = {}


def kernel(**inputs):
    inp = {k: np.asarray(v) for k, v in inputs.items()}
    if "nc" not in _NC_CACHE:
        _NC_CACHE["nc"] = build()
    nc = _NC_CACHE["nc"]
    shared = shared_inputs(inp)
    n = 8
    in_maps = [core_inputs(inp, b, shared, nc._in_names) for b in range(n)]
    res = run_bass_kernel_spmd(nc, in_maps, core_ids=list(range(n)))
    out = np.stack([np.asarray(r["out"], dtype=np.float32) for r in res.results], axis=0)
    return out
```

```python
import math
import numpy as np
import ml_dtypes
import concourse.bass as bass
import concourse.mybir as mybir
from concourse.bass_utils import run_bass_kernel_spmd

F32 = mybir.dt.float32
BF16 = mybir.dt.bfloat16
I32 = mybir.dt.int32
AF = mybir.ActivationFunctionType
ALU = mybir.AluOpType
AX = mybir.AxisListType

T = 4096
NCH = 32
D = 1024
DIN = 13376
EPS = 1e-6
OQ, OK_, OV, OG, OZ, OX, ODT, OGATE = 0, 1024, 2048, 4096, 6144, 8192, 11264, 11328


class Tk:
    __slots__ = ("name", "lw", "rd", "multi", "ws", "excl")

    def __init__(self, name="", multi=False, excl=False):
        self.name = name
        self.lw = None
        self.rd = {}
        self.multi = multi
        self.ws = {}
        self.excl = excl


class Prog:
    ENGS = ("pe", "act", "dve", "pool", "sp")

    def __init__(self, nc):
        self.nc = nc
        self.eng = {"pe": nc.tensor, "act": nc.scalar, "dve": nc.vector,
                    "pool": nc.gpsimd, "sp": nc.sync}
        self.sem = {}
        self.cnt = {}
        self.waited = {e: {} for e in self.ENGS}
        for e in self.ENGS:
            k = "E_" + e
            self.sem[k] = nc.alloc_semaphore(name=k)
            self.cnt[k] = 0
        self.dma_pool = {}
        self.dma_rr = {}
        for q, n in (("sp", 20), ("pool", 12), ("act", 8)):
            ks = []
            for i in range(n):
                k = "D_%s%d" % (q, i)
                self.sem[k] = nc.alloc_semaphore(name=k)
                self.cnt[k] = 0
                ks.append(k)
            self.dma_pool[q] = ks
            self.dma_rr[q] = 0
        self.n_inst = 0

    def _wait(self, eng, deps):
        w = self.waited[eng]
        e = self.eng[eng]
        own = "E_" + eng
        for (k, v, kind) in deps:
            if k == own and (eng == "pe" or eng == "sp" or kind == "war"):
                continue
            if w.get(k, 0) >= v:
                continue
            w[k] = v
            e.wait_ge(self.sem[k], v)

    @staticmethod
    def _deps(reads, writes):
        deps = []
        for t in reads:
            if t.multi:
                for k, v in t.ws.items():
                    deps.append((k, v, "raw"))
            elif t.lw is not None:
                deps.append((t.lw[0], t.lw[1], "raw"))
            if t.excl:
                for k, v in t.rd.items():
                    deps.append((k, v, "war"))
        for t in writes:
            if t.multi:
                continue
            if t.lw is not None:
                deps.append((t.lw[0], t.lw[1], "waw"))
            for k, v in t.rd.items():
                deps.append((k, v, "war"))
        return deps

    @staticmethod
    def _commit(ev, reads, writes):
        for t in reads:
            if t.rd.get(ev[0], 0) < ev[1]:
                t.rd[ev[0]] = ev[1]
        for t in writes:
            if t.multi:
                if t.ws.get(ev[0], 0) < ev[1]:
                    t.ws[ev[0]] = ev[1]
                continue
            t.lw = ev
            t.rd = {}

    def op(self, eng, fn, reads=(), writes=()):
        self._wait(eng, self._deps(reads, writes))
        k = "E_" + eng
        self.cnt[k] += 1
        ins = fn(self.eng[eng])
        ins.then_inc(self.sem[k], 1)
        self._commit((k, self.cnt[k]), reads, writes)
        self.n_inst += 1

    def dma(self, q, out, in_, reads=(), writes=()):
        deps = self._deps(reads, writes)
        pool = self.dma_pool[q]
        s = pool[self.dma_rr[q] % len(pool)]
        self.dma_rr[q] += 1
        if self.cnt[s] > 0:
            deps.append((s, self.cnt[s], "raw"))
        self._wait(q, deps)
        self.cnt[s] += 16
        self.eng[q].dma_start(out=out, in_=in_).then_inc(self.sem[s], 16)
        self._commit((s, self.cnt[s]), reads, writes)
        self.n_inst += 1

    def barrier(self):
        allk = [(k, v, "raw") for k, v in self.cnt.items() if v > 0]
        for e in self.ENGS:
            self._wait(e, [d for d in allk if d[0] != "E_" + e])

    def finish(self):
        allk = [(k, v, "raw") for k, v in self.cnt.items() if v > 0]
        self._wait("sp", allk)


_UNIQ = [0]


def _uniq(name):
    _UNIQ[0] += 1
    return "s%d_%s" % (_UNIQ[0], name)


def _bf(a):
    return np.asarray(a, dtype=np.float32).astype(ml_dtypes.bfloat16)


def host_consts():
    c = {}
    c["ident_bf"] = _bf(np.eye(128))
    c["ident_f"] = np.eye(128, dtype=np.float32)
    half = 128
    inv = (10000.0 ** (-np.arange(half, dtype=np.float32) / half)).astype(np.float32)
    c["invf"] = inv.reshape(128, 1).astype(np.float32)
    gam = np.array([1.0 - 2.0 ** (-5 - h) for h in range(4)], dtype=np.float64)
    idx = np.arange(128, dtype=np.float64)
    dist = np.abs(idx[:, None] - idx[None, :])
    c["dmat"] = np.stack([gam[h] ** dist / 16.0 for h in range(4)], 1).astype(np.float32)
    qf = np.stack([gam[h] ** (idx + 1.0) for h in range(4)], 0)
    qb = np.stack([gam[h] ** (128.0 - idx) for h in range(4)], 0)
    c["qdf"] = np.broadcast_to(np.repeat(qf, 2, axis=0)[None], (128, 8, 128)).astype(np.float32).copy()
    c["qdb"] = np.broadcast_to(np.repeat(qb, 2, axis=0)[None], (128, 8, 128)).astype(np.float32).copy()
    kf = np.stack([gam[h] ** (127.0 - idx) / 16.0 for h in range(4)], 1)
    kb = np.stack([gam[h] ** idx / 16.0 for h in range(4)], 1)
    c["kdec"] = np.concatenate([kf, kb], 1).astype(np.float32)
    c["maskF"] = (idx[None, :] >= idx[:, None]).astype(np.float32)
    c["maskB"] = (idx[None, :] <= idx[:, None]).astype(np.float32)
    return c


def build(stage=99, debug=()):
    nc = bass.Bass("TRN2", target_bir_lowering=False)
    P = Prog(nc)
    dbg = set(debug)

    in_names = set()

    def din(name, shape, dt):
        in_names.add(name)
        return nc.dram_tensor(name, list(shape), dt, kind="ExternalInput").ap()

    def dscr(name, shape, dt):
        kind = "ExternalOutput" if name in dbg else "Internal"
        return nc.dram_tensor(name, list(shape), dt, kind=kind).ap()

    x_d = din("x", [T, D], F32)
    pos_d = din("pos", [1, T], I32)
    w_in_d = din("w_in", [D, DIN], F32)
    gmix_d = din("gmix", [128, 8], F32)
    dtb_d = din("dtb", [1, 64], F32)
    ident_bf_d = din("ident_bf", [128, 128], BF16)
    invf_d = din("invf", [128, 1], F32)
    out_d = nc.dram_tensor("out", [T, D], F32, kind="ExternalOutput").ap()
    dmat_d = din("dmat", [128, 4, 128], F32)
    qdf_d = din("qdf", [128, 8, 128], F32)
    qdb_d = din("qdb", [128, 8, 128], F32)
    kdec_d = din("kdec", [128, 8], F32)
    maskF_d = din("maskF", [128, 128], F32)
    maskB_d = din("maskB", [128, 128], F32)
    cw_d = din("cw", [128, 24, 5], F32)
    cbc_d = din("cbc", [128, 24], F32)
    alog_d = din("alog", [1, 64], F32)
    gains_d = din("gains", [128, 40], F32)
    gfin_d = din("gfin", [1, 1024], F32)
    w_ret_o_d = din("w_ret_o", [2048, 1024], F32)
    w_ssm_o_d = din("w_ssm_o", [2048, 1024], F32)
    w_out_d = din("w_out", [1024, 1024], F32)
    w_up_d = din("w_up", [1024, 4096], F32)
    w_dn_d = din("w_dn", [4096, 1024], F32)
    dskip_d = din("dskip", [1, 32], F32)

    qT_d = dscr("qT", [1024, T], BF16)
    kT_d = dscr("kT", [1024, T], BF16)
    v_d = dscr("v", [T, 2048], BF16)
    g_d = dscr("g", [T, 2048], BF16)
    z_d = dscr("z", [T, 2048], BF16)
    xbcT_d = dscr("xbcT", [3072, T + 4], BF16)
    gateT_d = dscr("gateT", [2048, T], BF16)
    dt_d = dscr("dt", [T, 64], F32)
    ktok_d = dscr("ktok", [T, 1024], BF16)
    Sbret_d = dscr("Sbret", [NCH, 128, 4096], BF16)
    rT_d = dscr("rT", [2048, T], BF16)
    BCT_d = dscr("BCT", [1024, T], BF16)
    xtok_d = dscr("xtok", [T, 2560], BF16)
    cumT_d = dscr("cumT", [NCH, 2, 32, 128], F32)
    sm_d = dscr("sm", [NCH, 128, 384], F32)
    Sbssd_d = dscr("Sbssd", [NCH, 128, 2048], BF16)
    sT_d = dscr("sT", [2048, T], BF16)
    x1_d = dscr("x1", [T, 1024], F32)
    h2T_d = dscr("h2T", [1024, T], BF16)
    scr8_tk = Tk("scr8", multi=True)
    scr5_tk = Tk("scr5", multi=True)
    scr6_tk = Tk("scr6", multi=True)
    scr7_tk = Tk("scr7", multi=True)
    scr2_tk = Tk("scr2", multi=True)
    scr3_tk = Tk("scr3", multi=True)

    from contextlib import ExitStack
    with ExitStack() as es:
        def sb(name, shape, dt):
            return es.enter_context(nc.sbuf_tensor(_uniq(name), list(shape), dt))

        def ps(name, shape, dt):
            return es.enter_context(nc.psum_tensor("p_" + name, list(shape), dt))

        banks = [ps("bank%d" % i, [128, 512], F32) for i in range(8)]
        bank_tk = [Tk("bank%d" % i, excl=True) for i in range(8)]
        bank_rr = [0]

        def next_bank():
            i = bank_rr[0] % 8
            bank_rr[0] += 1
            return banks[i], bank_tk[i]

        ident_bf = sb("ident_bf", [128, 128], BF16)
        invf = sb("invf", [128, 1], F32)
        gmix = sb("gmix", [128, 8], F32)
        c_tk = Tk("consts")
        P.dma("sp", ident_bf[:], ident_bf_d, writes=[c_tk])
        P.dma("sp", invf[:], invf_d, writes=[c_tk])
        P.dma("sp", gmix[:], gmix_d, writes=[c_tk])

        es_h = ExitStack()
        hT = es_h.enter_context(nc.sbuf_tensor(_uniq("hT"), [128, 8, T], BF16))
        hT_tk = [Tk("hT%d" % c) for c in range(NCH)]
        with ExitStack() as es1:
            def sb1(name, shape, dt):
                return es1.enter_context(nc.sbuf_tensor(_uniq(name), list(shape), dt))
            NB = 4
            xt = [sb1("xt%d" % i, [128, D], F32) for i in range(NB)]
            xt_tk = [Tk() for _ in range(NB)]
            junk = sb1("junk", [128, D], BF16)
            junk_tk = Tk()
            st = [sb1("st%d" % i, [128, 4], F32) for i in range(NB)]
            st_tk = [Tk() for _ in range(NB)]
            hb = [sb1("hb%d" % i, [128, D], BF16) for i in range(NB)]
            hb_tk = [Tk() for _ in range(NB)]
            for c in range(NCH):
                i = c % NB
                P.dma("sp", xt[i][:], x_d[c * 128:(c + 1) * 128, :], writes=[xt_tk[i]])
                P.op("act", lambda e, i=i: e.activation(out=junk[:], in_=xt[i][:], func=AF.Square,
                                                        accum_out=st[i][:, 0:1]),
                     reads=[xt_tk[i]], writes=[junk_tk, st_tk[i]])
                P.op("act", lambda e, i=i: e.activation(out=st[i][:, 1:2], in_=st[i][:, 0:1], func=AF.Ln,
                                                        scale=1.0 / D, bias=EPS),
                     reads=[st_tk[i]], writes=[st_tk[i]])
                P.op("act", lambda e, i=i: e.activation(out=st[i][:, 2:3], in_=st[i][:, 1:2], func=AF.Exp,
                                                        scale=-0.5),
                     reads=[st_tk[i]], writes=[st_tk[i]])
                P.op("dve", lambda e, i=i: e.tensor_scalar(out=hb[i][:], in0=xt[i][:], scalar1=st[i][:, 2:3],
                                                          scalar2=None, op0=ALU.mult),
                     reads=[xt_tk[i], st_tk[i]], writes=[hb_tk[i]])
                bk, btk = next_bank()
                bkb = bk[:].bitcast(BF16)
                for k in range(8):
                    P.op("pe", lambda e, i=i, k=k, bkb=bkb: e.transpose(out=bkb[:, k * 128:(k + 1) * 128],
                                                                         in_=hb[i][:, k * 128:(k + 1) * 128],
                                                                         identity=ident_bf[:]),
                         reads=[hb_tk[i], c_tk], writes=[btk])
                P.op("dve" if c % 2 == 0 else "act",
                     (lambda e, bkb=bkb, c=c: e.tensor_copy(out=hT[:, :, c * 128:(c + 1) * 128],
                                                            in_=bkb.rearrange("p (k t) -> p k t", k=8)))
                     if c % 2 == 0 else
                     (lambda e, bkb=bkb, c=c: e.copy(out=hT[:, :, c * 128:(c + 1) * 128],
                                                     in_=bkb.rearrange("p (k t) -> p k t", k=8))),
                     reads=[btk], writes=[hT_tk[c]])
        P.barrier()
        if "hT" in dbg:
            hT_o = nc.dram_tensor("hT_o", [128, 8, T], BF16, kind="ExternalOutput").ap()
            P.dma("sp", hT_o, hT[:], reads=hT_tk)


        with ExitStack() as es2:
            def sb2(name, shape, dt):
                return es2.enter_context(nc.sbuf_tensor(_uniq(name), list(shape), dt))
            cosT = sb2("cosT", [128, T], F32)
            sinT = sb2("sinT", [128, T], F32)
            tab_tk = Tk("tab")
            with ExitStack() as es2a:
                def sb2a(name, shape, dt):
                    return es2a.enter_context(nc.sbuf_tensor(_uniq(name), list(shape), dt))
                posi = sb2a("posi", [128, T], I32)
                ang = sb2a("ang", [128, T], F32)
                tmpf = sb2a("tmpf", [128, T], F32)
                ki = sb2a("ki", [128, T], I32)
                tt_ = Tk()
                P.dma("sp", posi[:], pos_d.partition_broadcast(128), writes=[tt_])
                P.op("dve", lambda e: e.tensor_copy(out=ang[:], in_=posi[:]), reads=[tt_], writes=[tt_])
                P.op("dve", lambda e: e.tensor_scalar(out=ang[:], in0=ang[:], scalar1=invf[:, 0:1], scalar2=None,
                                                      op0=ALU.mult), reads=[tt_, c_tk], writes=[tt_])
                for (dst, shift) in ((sinT, 0.0), (cosT, math.pi / 2)):
                    P.op("dve", lambda e, shift=shift: e.tensor_scalar(out=tmpf[:], in0=ang[:], scalar1=shift,
                                                                       scalar2=None, op0=ALU.add),
                         reads=[tt_], writes=[tt_])
                    P.op("dve", lambda e: e.tensor_scalar(out=ki[:], in0=tmpf[:], scalar1=1.0 / (2 * math.pi),
                                                          scalar2=None, op0=ALU.mult), reads=[tt_], writes=[tt_])
                    P.op("dve", lambda e, dst=dst: e.tensor_copy(out=dst[:], in_=ki[:]), reads=[tt_], writes=[tt_])
                    P.op("dve", lambda e, dst=dst: e.scalar_tensor_tensor(out=dst[:], in0=dst[:], scalar=-2 * math.pi,
                                                                          in1=tmpf[:], op0=ALU.mult, op1=ALU.add),
                         reads=[tt_], writes=[tt_])
                    P.op("dve", lambda e, dst=dst: e.tensor_scalar(out=dst[:], in0=dst[:], scalar1=-math.pi,
                                                                   scalar2=math.pi, op0=ALU.max, op1=ALU.min),
                         reads=[tt_], writes=[tt_])
                    P.op("act", lambda e, dst=dst: e.activation(out=dst[:], in_=dst[:], func=AF.Sin),
                         reads=[tt_], writes=[tt_, tab_tk])
            P.barrier()

            wst = [sb2("wst%d" % i, [128, 8, 512], F32) for i in range(2)]
            wst_tk = [Tk() for _ in range(2)]
            wb = [sb2("wb%d" % i, [128, 8, 512], BF16) for i in range(2)]
            wb_tk = [[Tk() for _ in range(8)] for _ in range(2)]
            stg = [sb2("stg%d" % i, [128, 4, 512], BF16) for i in range(2)]
            stg_tk = [[Tk() for _ in range(4)] for _ in range(2)]
            rtmp = [[sb2("rt%d_%d" % (i, j), [128, 512], F32) for j in range(4)] for i in range(2)]
            rtmp_tk = [[Tk() for _ in range(4)] for _ in range(2)]
            dtall = sb2("dtall", [128, NCH, 64], F32)
            dtall_tk = Tk()
            dtb_bc = sb2("dtb_bc", [128, 64], F32)
            zpad = sb2("zpad", [128, 24, 2], BF16)
            P.dma("sp", dtb_bc[:], dtb_d.partition_broadcast(128), writes=[c_tk])
            zp_tk = Tk()
            P.op("pool", lambda e: e.memset(zpad[:], 0.0), writes=[zp_tk])
            xpad_tk = Tk(multi=True)
            P.dma("pool", xbcT_d[:, 0:2].rearrange("(c p) w -> p c w", p=128), zpad[:], reads=[zp_tk], writes=[xpad_tk])
            P.dma("pool", xbcT_d[:, T + 2:T + 4].rearrange("(c p) w -> p c w", p=128), zpad[:], reads=[zp_tk],
                  writes=[xpad_tk])

            blocks = []
            for i in range(2):
                blocks.append((OQ + i * 512, 512, "qk", qT_d, i * 512))
            for i in range(2):
                blocks.append((OK_ + i * 512, 512, "qk", kT_d, i * 512))
            for i in range(4):
                blocks.append((OV + i * 512, 512, "tokcopy", v_d, i * 512))
            for i in range(4):
                blocks.append((OG + i * 512, 512, "toksilu", g_d, i * 512))
            for i in range(4):
                blocks.append((OZ + i * 512, 512, "toksilu", z_d, i * 512))
            for i in range(6):
                blocks.append((OX + i * 512, 512, "Tcopy", xbcT_d, i * 512))
            blocks.append((ODT, 64, "dt", None, 0))
            for i in range(4):
                blocks.append((OGATE + i * 512, 512, "Tsig", gateT_d, i * 512))
            if stage < 2:
                blocks = []
            w_view = w_in_d.rearrange("(k p) c -> p k c", p=128)
            scr_tk = Tk("p2scratch", multi=True)
            stg_n = [0]
            ev_rr = [0]

            def load_w(bi):
                col0, ncols, kind, dst, d0 = blocks[bi]
                b = bi % 2
                P.dma("sp", wst[b][:, :, 0:ncols], w_view[:, :, col0:col0 + ncols], writes=[wst_tk[b]])
                for k in range(8):
                    if k % 2 == 0:
                        P.op("pool", lambda e, b=b, k=k, ncols=ncols: e.tensor_scalar(
                            out=wb[b][:, k, 0:ncols], in0=wst[b][:, k, 0:ncols], scalar1=gmix[:, k:k + 1],
                            scalar2=1.0, op0=ALU.mult, op1=ALU.mult),
                            reads=[wst_tk[b], c_tk], writes=[wb_tk[b][k]])
                    else:
                        P.op("act", lambda e, b=b, k=k, ncols=ncols: e.activation(
                            out=wb[b][:, k, 0:ncols], in_=wst[b][:, k, 0:ncols], func=AF.Copy,
                            scale=gmix[:, k:k + 1]),
                            reads=[wst_tk[b], c_tk], writes=[wb_tk[b][k]])

            if blocks:
                load_w(0)
            for bi in range(len(blocks)):
                col0, ncols, kind, dst, d0 = blocks[bi]
                b = bi % 2
                if bi + 1 < len(blocks):
                    load_w(bi + 1)
                if kind in ("qk", "Tcopy", "Tsig"):
                    for tt in range(8):
                        tsl = slice(tt * 512, (tt + 1) * 512)
                        bks = []
                        for cc in range(4):
                            bk, btk = next_bank()
                            bks.append((bk, btk))
                            for k in range(8):
                                P.op("pe", lambda e, bk=bk, b=b, k=k, cc=cc, tsl=tsl: e.matmul(
                                    bk[:], lhsT=wb[b][:, k, cc * 128:(cc + 1) * 128], rhs=hT[:, k, tsl],
                                    start=(k == 0), stop=(k == 7)),
                                    reads=[wb_tk[b][k]] + hT_tk[tt * 4:(tt + 1) * 4], writes=[btk])
                        si = stg_n[0] % 2
                        stg_n[0] += 1
                        if kind == "qk":
                            for hh in range(2):
                                (A, Atk), (B, Btk) = bks[2 * hh], bks[2 * hh + 1]
                                r = rtmp[hh]
                                rk = rtmp_tk[hh]
                                P.op("dve", lambda e, A=A, r=r, tsl=tsl: e.tensor_tensor(out=r[0][:], in0=A[:], in1=cosT[:, tsl], op=ALU.mult),
                                     reads=[Atk, tab_tk], writes=[rk[0]])
                                P.op("dve", lambda e, B=B, r=r, tsl=tsl: e.tensor_tensor(out=r[1][:], in0=B[:], in1=sinT[:, tsl], op=ALU.mult),
                                     reads=[Btk, tab_tk], writes=[rk[1]])
                                P.op("dve", lambda e, A=A, r=r, tsl=tsl: e.tensor_tensor(out=r[2][:], in0=A[:], in1=sinT[:, tsl], op=ALU.mult),
                                     reads=[Atk, tab_tk], writes=[rk[2]])
                                P.op("dve", lambda e, B=B, r=r, tsl=tsl: e.tensor_tensor(out=r[3][:], in0=B[:], in1=cosT[:, tsl], op=ALU.mult),
                                     reads=[Btk, tab_tk], writes=[rk[3]])
                                P.op("pool", lambda e, r=r, si=si, hh=hh: e.tensor_tensor(out=stg[si][:, 2 * hh, :], in0=r[0][:], in1=r[1][:], op=ALU.subtract),
                                     reads=[rk[0], rk[1]], writes=[stg_tk[si][2 * hh]])
                                P.op("pool", lambda e, r=r, si=si, hh=hh: e.tensor_tensor(out=stg[si][:, 2 * hh + 1, :], in0=r[2][:], in1=r[3][:], op=ALU.add),
                                     reads=[rk[2], rk[3]], writes=[stg_tk[si][2 * hh + 1]])
                        else:
                            for cc in range(4):
                                bk, btk = bks[cc]
                                if kind == "Tsig":
                                    P.op("act", lambda e, bk=bk, si=si, cc=cc: e.activation(out=stg[si][:, cc, :], in_=bk[:], func=AF.Sigmoid),
                                         reads=[btk], writes=[stg_tk[si][cc]])
                                elif cc % 2 == 0:
                                    P.op("dve", lambda e, bk=bk, si=si, cc=cc: e.tensor_copy(out=stg[si][:, cc, :], in_=bk[:]),
                                         reads=[btk], writes=[stg_tk[si][cc]])
                                else:
                                    P.op("act", lambda e, bk=bk, si=si, cc=cc: e.copy(out=stg[si][:, cc, :], in_=bk[:]),
                                         reads=[btk], writes=[stg_tk[si][cc]])
                        toff = 2 if kind == "Tcopy" else 0
                        P.dma("pool", dst[d0:d0 + 512, toff + tt * 512:toff + (tt + 1) * 512].rearrange("(c p) t -> p c t", p=128),
                              stg[si][:], reads=stg_tk[si], writes=[scr_tk])
                elif kind in ("tokcopy", "toksilu"):
                    for c in range(NCH):
                        if c % 4 == 0:
                            si = stg_n[0] % 2
                            stg_n[0] += 1
                        bk, btk = next_bank()
                        for k in range(8):
                            P.op("pe", lambda e, bk=bk, b=b, k=k, c=c: e.matmul(
                                bk[:], lhsT=hT[:, k, c * 128:(c + 1) * 128], rhs=wb[b][:, k, :],
                                start=(k == 0), stop=(k == 7)),
                                reads=[wb_tk[b][k], hT_tk[c]], writes=[btk])
                        q = c % 4
                        if kind == "toksilu":
                            P.op("act", lambda e, bk=bk, si=si, q=q: e.activation(out=stg[si][:, q, :], in_=bk[:], func=AF.Silu),
                                 reads=[btk], writes=[stg_tk[si][q]])
                        elif c % 2 == 0:
                            P.op("dve", lambda e, bk=bk, si=si, q=q: e.tensor_copy(out=stg[si][:, q, :], in_=bk[:]),
                                 reads=[btk], writes=[stg_tk[si][q]])
                        else:
                            P.op("act", lambda e, bk=bk, si=si, q=q: e.copy(out=stg[si][:, q, :], in_=bk[:]),
                                 reads=[btk], writes=[stg_tk[si][q]])
                        if q == 3:
                            tt = c // 4
                            P.dma("pool", dst[tt * 512:(tt + 1) * 512, d0:d0 + 512].rearrange("(q p) c -> p q c", p=128),
                                  stg[si][:], reads=stg_tk[si], writes=[scr_tk])
                elif kind == "dt":
                    for c in range(NCH):
                        bk, btk = next_bank()
                        for k in range(8):
                            P.op("pe", lambda e, bk=bk, b=b, k=k, c=c: e.matmul(
                                bk[:, 0:64], lhsT=hT[:, k, c * 128:(c + 1) * 128], rhs=wb[b][:, k, 0:64],
                                start=(k == 0), stop=(k == 7)),
                                reads=[wb_tk[b][k], hT_tk[c]], writes=[btk])
                        P.op("dve", lambda e, bk=bk, c=c: e.tensor_tensor(out=dtall[:, c, :], in0=bk[:, 0:64], in1=dtb_bc[:], op=ALU.add),
                             reads=[btk, c_tk], writes=[dtall_tk])
                    P.op("act", lambda e: e.activation(out=dtall[:], in_=dtall[:], func=AF.Exp), reads=[dtall_tk], writes=[dtall_tk])
                    P.op("act", lambda e: e.activation(out=dtall[:], in_=dtall[:], func=AF.Ln, bias=1.0), reads=[dtall_tk], writes=[dtall_tk])
                    P.dma("pool", dt_d.rearrange("(c p) h -> p c h", p=128), dtall[:], reads=[dtall_tk], writes=[scr_tk])
            P.barrier()

        es_h.close()
        if stage >= 3:
            GAM = [1.0 - 2.0 ** (-5 - h) for h in range(4)]
            CDEC = [g ** 128 for g in GAM]
            with ExitStack() as es3:
                def sb3(name, shape, dt):
                    return es3.enter_context(nc.sbuf_tensor(_uniq(name), list(shape), dt))
                dmat = sb3("dmat", [128, 4, 128], F32)
                qdf = sb3("qdf", [128, 8, 128], F32)
                qdb = sb3("qdb", [128, 8, 128], F32)
                kdec = sb3("kdec", [128, 8], F32)
                rc_tk = Tk()
                P.dma("sp", dmat[:], dmat_d, writes=[rc_tk])
                P.dma("sp", qdf[:], qdf_d, writes=[rc_tk])
                P.dma("sp", qdb[:], qdb_d, writes=[rc_tk])
                P.dma("sp", kdec[:], kdec_d, writes=[rc_tk])
                kT_v = kT_d.rearrange("(k p) t -> p k t", p=128)
                qT_v = qT_d.rearrange("(k p) t -> p k t", p=128)
                rT_v = rT_d.rearrange("(e p) t -> p e t", p=128)
                G2 = 256
                with ExitStack() as es3a:
                    def sba(name, shape, dt):
                        return es3a.enter_context(nc.sbuf_tensor(_uniq(name), list(shape), dt))
                    Sb = sba("Sb", [128, 8, 512], F32)
                    Sb_tk = [Tk() for _ in range(8)]
                    Sbb = [sba("Sbb%d" % i, [128, 8, 512], BF16) for i in range(2)]
                    Sbb_tk = [[Tk() for _ in range(8)] for _ in range(2)]
                    kTg = [sba("kTg%d" % i, [128, 8, G2], BF16) for i in range(2)]
                    kTg_tk = [Tk() for _ in range(2)]
                    vg = [sba("vg%d" % i, [128, 2, 2048], BF16) for i in range(2)]
                    vg_tk = [Tk() for _ in range(2)]
                    ktok = [sba("ktok%d" % i, [128, 1024], BF16) for i in range(2)]
                    ktok_tk = [Tk() for _ in range(2)]
                    kb = [sba("kb%d" % i, [128, 1024], BF16) for i in range(2)]
                    kb_tk = [[Tk() for _ in range(4)] for _ in range(2)]
                    P.op("pool", lambda e: e.memset(Sb[:], 0.0), writes=Sb_tk)
                    for c in range(NCH - 1, -1, -1):
                        gi = c // 2
                        gb = gi % 2
                        ci = c % 2
                        if ci == 1:
                            P.dma("sp", kTg[gb][:], kT_v[:, :, gi * G2:(gi + 1) * G2], reads=[scr_tk], writes=[kTg_tk[gb]])
                            P.dma("sp", vg[gb][:], v_d[gi * G2:(gi + 1) * G2, :].rearrange("(q p) c -> p q c", p=128),
                                  reads=[scr_tk], writes=[vg_tk[gb]])
                        cb_ = c % 2
                        bk, btk = next_bank()
                        bkb = bk[:].bitcast(BF16)
                        for kk in range(8):
                            P.op("pe", lambda e, bkb=bkb, kk=kk, gb=gb, ci=ci: e.transpose(
                                out=bkb[:, kk * 128:(kk + 1) * 128], in_=kTg[gb][:, kk, ci * 128:(ci + 1) * 128],
                                identity=ident_bf[:]), reads=[kTg_tk[gb], c_tk], writes=[btk])
                        P.op("act", lambda e, bkb=bkb, cb_=cb_: e.copy(out=ktok[cb_][:], in_=bkb), reads=[btk], writes=[ktok_tk[cb_]])
                        P.dma("pool", ktok_d[c * 128:(c + 1) * 128, :], ktok[cb_][:], reads=[ktok_tk[cb_]], writes=[scr2_tk])
                        for h in range(4):
                            P.op("dve", lambda e, bkb=bkb, cb_=cb_, h=h: e.tensor_scalar(
                                out=kb[cb_][:, h * 256:(h + 1) * 256], in0=bkb[:, h * 256:(h + 1) * 256],
                                scalar1=kdec[:, 4 + h:5 + h], scalar2=None, op0=ALU.mult),
                                reads=[btk, rc_tk], writes=[kb_tk[cb_][h]])
                        for idx in range(8):
                            h = idx // 2
                            P.op("act", lambda e, cb_=cb_, idx=idx: e.copy(out=Sbb[cb_][:, idx, :], in_=Sb[:, idx, :]),
                                 reads=[Sb_tk[idx]], writes=[Sbb_tk[cb_][idx]])
                            bk2, btk2 = next_bank()
                            P.op("pe", lambda e, bk2=bk2, cb_=cb_, idx=idx, h=h, gb=gb, ci=ci: e.matmul(
                                bk2[:], lhsT=kb[cb_][:, idx * 128:(idx + 1) * 128], rhs=vg[gb][:, ci, h * 512:(h + 1) * 512],
                                start=True, stop=True), reads=[kb_tk[cb_][h], vg_tk[gb]], writes=[btk2])
                            P.op("dve", lambda e, bk2=bk2, idx=idx, h=h: e.scalar_tensor_tensor(
                                out=Sb[:, idx, :], in0=Sb[:, idx, :], scalar=CDEC[h], in1=bk2[:], op0=ALU.mult, op1=ALU.add),
                                reads=[btk2, Sb_tk[idx]], writes=[Sb_tk[idx]])
                        P.dma("pool", Sbret_d[c], Sbb[cb_][:].rearrange("p a b -> p (a b)"), reads=Sbb_tk[cb_], writes=[scr2_tk])
                P.barrier()
                with ExitStack() as es3b:
                  if stage >= 4:
                      def sbb_(name, shape, dt):
                          return es3b.enter_context(nc.sbuf_tensor(_uniq(name), list(shape), dt))
                      Sf = sbb_("Sf", [128, 8, 512], F32)
                      Sf_tk = [Tk() for _ in range(8)]
                      Sfb2 = [sbb_("Sfb%d" % i, [128, 8, 512], BF16) for i in range(2)]
                      Sfb2_tk = [[Tk() for _ in range(8)] for _ in range(2)]
                      SbL = [sbb_("SbL%d" % i, [128, 8, 512], BF16) for i in range(2)]
                      SbL_tk = [Tk() for _ in range(2)]
                      qTg = [sbb_("qTg%d" % i, [128, 8, G2], BF16) for i in range(2)]
                      kTg = [sbb_("kTg%d" % i, [128, 8, G2], BF16) for i in range(2)]
                      ktg = [sbb_("ktg%d" % i, [128, 2, 1024], BF16) for i in range(2)]
                      vg = [sbb_("vg%d" % i, [128, 2, 2048], BF16) for i in range(2)]
                      gg = [sbb_("gg%d" % i, [128, 2, 2048], BF16) for i in range(2)]
                      ld_tk = [Tk() for _ in range(2)]
                      Pm = [sbb_("Pm%d" % i, [128, 4, 128], BF16) for i in range(2)]
                      Pm_tk = [Tk() for _ in range(2)]
                      qf = [sbb_("qf%d" % i, [128, 8, 128], BF16) for i in range(2)]
                      qf_tk = [Tk() for _ in range(2)]
                      qb = [sbb_("qb%d" % i, [128, 8, 128], BF16) for i in range(2)]
                      qb_tk = [Tk() for _ in range(2)]
                      kf = [sbb_("kf%d" % i, [128, 1024], BF16) for i in range(2)]
                      kf_tk = [[Tk() for _ in range(4)] for _ in range(2)]
                      stats = [sbb_("stats%d" % i, [128, 4, 6], F32) for i in range(2)]
                      mv = [sbb_("mv%d" % i, [128, 4, 2], F32) for i in range(2)]
                      rs = [sbb_("rs%d" % i, [128, 12], F32) for i in range(2)]
                      st_tk = [[Tk() for _ in range(4)] for _ in range(2)]
                      rs_tk = [Tk() for _ in range(2)]
                      yn = [sbb_("yn%d" % i, [128, 512], F32) for i in range(4)]
                      yn_tk = [Tk() for _ in range(4)]
                      rr = [sbb_("rr%d" % i, [128, 2048], BF16) for i in range(2)]
                      rr_tk = [[Tk() for _ in range(4)] for _ in range(2)]
                      rTs = [sbb_("rTs%d" % i, [128, 16, G2], BF16) for i in range(2)]
                      rTs_tk = [[Tk() for _ in range(4)] for _ in range(2)]
                      P.op("pool", lambda e: e.memset(Sf[:], 0.0), writes=Sf_tk)
                      P.op("pool", lambda e: e.memset(Sfb2[0][:], 0.0), writes=Sfb2_tk[0])
                      yn_n = [0]
                      for c in range(NCH):
                          gi = c // 2
                          gb = gi % 2
                          ci = c % 2
                          cb_ = c % 2
                          csl = slice(ci * 128, (ci + 1) * 128)
                          if ci == 0:
                              gs = slice(gi * G2, (gi + 1) * G2)
                              P.dma("sp", qTg[gb][:], qT_v[:, :, gs], reads=[scr_tk], writes=[ld_tk[gb]])
                              P.dma("sp", kTg[gb][:], kT_v[:, :, gs], reads=[scr_tk], writes=[ld_tk[gb]])
                              P.dma("sp", ktg[gb][:], ktok_d[gs, :].rearrange("(q p) c -> p q c", p=128), reads=[scr2_tk], writes=[ld_tk[gb]])
                              P.dma("sp", vg[gb][:], v_d[gs, :].rearrange("(q p) c -> p q c", p=128), reads=[scr_tk], writes=[ld_tk[gb]])
                              P.dma("sp", gg[gb][:], g_d[gs, :].rearrange("(q p) c -> p q c", p=128), reads=[scr_tk], writes=[ld_tk[gb]])
                          P.dma("sp", SbL[cb_][:].rearrange("p a b -> p (a b)"), Sbret_d[c], reads=[scr2_tk], writes=[SbL_tk[cb_]])
                          bkS, btkS = next_bank()
                          for h in range(4):
                              for dc in range(2):
                                  P.op("pe", lambda e, bkS=bkS, h=h, dc=dc, gb=gb, csl=csl: e.matmul(
                                      bkS[:, h * 128:(h + 1) * 128], lhsT=kTg[gb][:, 2 * h + dc, csl], rhs=qTg[gb][:, 2 * h + dc, csl],
                                      start=(dc == 0), stop=(dc == 1)), reads=[ld_tk[gb]], writes=[btkS])
                          P.op("dve", lambda e, bkS=bkS, cb_=cb_: e.tensor_tensor(
                              out=Pm[cb_][:].rearrange("p a b -> p (a b)"), in0=bkS[:], in1=dmat[:].rearrange("p a b -> p (a b)"),
                              op=ALU.mult), reads=[btkS, rc_tk], writes=[Pm_tk[cb_]])
                          P.op("dve", lambda e, cb_=cb_, gb=gb, csl=csl: e.tensor_tensor(
                              out=qf[cb_][:], in0=qTg[gb][:, :, csl], in1=qdf[:], op=ALU.mult),
                              reads=[ld_tk[gb], rc_tk], writes=[qf_tk[cb_]])
                          P.op("pool", lambda e, cb_=cb_, gb=gb, csl=csl: e.tensor_tensor(
                              out=qb[cb_][:], in0=qTg[gb][:, :, csl], in1=qdb[:], op=ALU.mult),
                              reads=[ld_tk[gb], rc_tk], writes=[qb_tk[cb_]])
                          for h in range(4):
                              P.op("act", lambda e, cb_=cb_, h=h, gb=gb, ci=ci: e.activation(
                                  out=kf[cb_][:, h * 256:(h + 1) * 256], in_=ktg[gb][:, ci, h * 256:(h + 1) * 256],
                                  func=AF.Copy, scale=kdec[:, h:h + 1]),
                                  reads=[ld_tk[gb], rc_tk], writes=[kf_tk[cb_][h]])
                          for idx in range(8):
                              h = idx // 2
                              bk2, btk2 = next_bank()
                              P.op("pe", lambda e, bk2=bk2, cb_=cb_, idx=idx, h=h, gb=gb, ci=ci: e.matmul(
                                  bk2[:], lhsT=kf[cb_][:, idx * 128:(idx + 1) * 128], rhs=vg[gb][:, ci, h * 512:(h + 1) * 512],
                                  start=True, stop=True), reads=[kf_tk[cb_][h], ld_tk[gb]], writes=[btk2])
                              P.op("dve", lambda e, bk2=bk2, idx=idx, h=h: e.scalar_tensor_tensor(
                                  out=Sf[:, idx, :], in0=Sf[:, idx, :], scalar=CDEC[h], in1=bk2[:], op0=ALU.mult, op1=ALU.add),
                                  reads=[btk2, Sf_tk[idx]], writes=[Sf_tk[idx]])
                              P.op("act", lambda e, idx=idx, cb_=cb_: e.copy(out=Sfb2[1 - cb_][:, idx, :], in_=Sf[:, idx, :]),
                                   reads=[Sf_tk[idx]], writes=[Sfb2_tk[1 - cb_][idx]])
                          ybk = []
                          for h in range(4):
                              bk, btk = next_bank()
                              ybk.append((bk, btk))
                              P.op("pe", lambda e, bk=bk, h=h, cb_=cb_, gb=gb, ci=ci: e.matmul(
                                  bk[:], lhsT=Pm[cb_][:, h, :], rhs=vg[gb][:, ci, h * 512:(h + 1) * 512], start=True, stop=False),
                                  reads=[Pm_tk[cb_], ld_tk[gb]], writes=[btk])
                              for dc in range(2):
                                  P.op("pe", lambda e, bk=bk, h=h, dc=dc, cb_=cb_: e.matmul(
                                      bk[:], lhsT=qf[cb_][:, 2 * h + dc, :], rhs=Sfb2[cb_][:, 2 * h + dc, :], start=False, stop=False),
                                      reads=[qf_tk[cb_], Sfb2_tk[cb_][2 * h + dc]], writes=[btk])
                              for dc in range(2):
                                  P.op("pe", lambda e, bk=bk, h=h, dc=dc, cb_=cb_: e.matmul(
                                      bk[:], lhsT=qb[cb_][:, 2 * h + dc, :], rhs=SbL[cb_][:, 2 * h + dc, :], start=False, stop=(dc == 1)),
                                      reads=[qb_tk[cb_], SbL_tk[cb_]], writes=[btk])
                              P.op("dve", lambda e, bk=bk, h=h, cb_=cb_: e.bn_stats(out=stats[cb_][:, h, :], in_=bk[:]),
                                   reads=[btk], writes=[st_tk[cb_][h]])
                              P.op("dve", lambda e, h=h, cb_=cb_: e.bn_aggr(out=mv[cb_][:, h, :], in_=stats[cb_][:, h, :]),
                                   reads=[st_tk[cb_][h]], writes=[st_tk[cb_][h]])
                          P.op("act", lambda e, cb_=cb_: e.activation(out=rs[cb_][:, 0:4], in_=mv[cb_][:, :, 1], func=AF.Ln, bias=EPS),
                               reads=st_tk[cb_], writes=[rs_tk[cb_]])
                          P.op("act", lambda e, cb_=cb_: e.activation(out=rs[cb_][:, 4:8], in_=rs[cb_][:, 0:4], func=AF.Exp, scale=-0.5),
                               reads=[rs_tk[cb_]], writes=[rs_tk[cb_]])
                          P.op("dve", lambda e, cb_=cb_: e.scalar_tensor_tensor(
                              out=rs[cb_][:, 8:12], in0=mv[cb_][:, :, 0], scalar=-1.0, in1=rs[cb_][:, 4:8], op0=ALU.mult, op1=ALU.mult),
                              reads=st_tk[cb_] + [rs_tk[cb_]], writes=[rs_tk[cb_]])
                          for h in range(4):
                              bk, btk = ybk[h]
                              yi = yn_n[0] % 4
                              yn_n[0] += 1
                              P.op("act", lambda e, bk=bk, h=h, cb_=cb_, yi=yi: e.activation(
                                  out=yn[yi][:], in_=bk[:], func=AF.Identity, scale=rs[cb_][:, 4 + h:5 + h], bias=rs[cb_][:, 8 + h:9 + h]),
                                  reads=[btk, rs_tk[cb_]], writes=[yn_tk[yi]])
                              P.op("dve", lambda e, h=h, cb_=cb_, yi=yi, gb=gb, ci=ci: e.tensor_tensor(
                                  out=rr[cb_][:, h * 512:(h + 1) * 512], in0=yn[yi][:], in1=gg[gb][:, ci, h * 512:(h + 1) * 512], op=ALU.mult),
                                  reads=[yn_tk[yi], ld_tk[gb]], writes=[rr_tk[cb_][h]])
                          for half_ in range(2):
                              bk, btk = next_bank()
                              bkb = bk[:].bitcast(BF16)
                              for j in range(8):
                                  e_ = half_ * 8 + j
                                  P.op("pe", lambda e, bkb=bkb, j=j, e_=e_, cb_=cb_: e.transpose(
                                      out=bkb[:, j * 128:(j + 1) * 128], in_=rr[cb_][:, e_ * 128:(e_ + 1) * 128], identity=ident_bf[:]),
                                      reads=[rr_tk[cb_][e_ // 4], c_tk], writes=[btk])
                              P.op("act" if half_ == 0 else "dve",
                                   (lambda e, bkb=bkb, gb=gb, half_=half_, csl=csl: e.copy(
                                       out=rTs[gb][:, half_ * 8:(half_ + 1) * 8, csl], in_=bkb.rearrange("p (a b) -> p a b", a=8)))
                                   if half_ == 0 else
                                   (lambda e, bkb=bkb, gb=gb, half_=half_, csl=csl: e.tensor_copy(
                                       out=rTs[gb][:, half_ * 8:(half_ + 1) * 8, csl], in_=bkb.rearrange("p (a b) -> p a b", a=8))),
                                   reads=[btk], writes=[rTs_tk[gb][ci * 2 + half_]])
                          if ci == 1:
                              P.dma("pool", rT_v[:, :, gi * G2:(gi + 1) * G2], rTs[gb][:], reads=rTs_tk[gb], writes=[scr3_tk])
                P.barrier()

        if stage >= 5:
            with ExitStack() as es5:
                def sb5(name, shape, dt):
                    return es5.enter_context(nc.sbuf_tensor(_uniq(name), list(shape), dt))
                maskF = sb5("maskF", [128, 128], F32)
                maskB = sb5("maskB", [128, 128], F32)
                sc_tk = Tk()
                P.dma("sp", maskF[:], maskF_d, writes=[sc_tk])
                P.dma("sp", maskB[:], maskB_d, writes=[sc_tk])
                BCT_v = BCT_d.rearrange("(k p) t -> p k t", p=128)
                sT_v = sT_d.rearrange("(e p) t -> p e t", p=128)
                with ExitStack() as es5a:
                    def sba(name, shape, dt):
                        return es5a.enter_context(nc.sbuf_tensor(_uniq(name), list(shape), dt))
                    ones_f = sba("ones_f", [128, 128], F32)
                    cw = sba("cw", [128, 24, 5], F32)
                    cbc = sba("cbc", [128, 24], F32)
                    a_bc = sba("a_bc", [128, 64], F32)
                    diag = sba("diag", [128, 24, 5, 128], BF16)
                    P.op("pool", lambda e: e.memset(ones_f[:], 1.0), writes=[sc_tk])
                    P.dma("sp", cw[:], cw_d, writes=[sc_tk])
                    P.dma("sp", cbc[:], cbc_d, writes=[sc_tk])
                    P.dma("sp", a_bc[:], alog_d.partition_broadcast(128), writes=[sc_tk])
                    P.op("act", lambda e: e.activation(out=a_bc[:], in_=a_bc[:], func=AF.Exp), reads=[sc_tk], writes=[sc_tk])
                    P.op("dve", lambda e: e.tensor_scalar(out=a_bc[:], in0=a_bc[:], scalar1=-1.0, scalar2=None, op0=ALU.mult),
                         reads=[sc_tk], writes=[sc_tk])
                    for cc in range(24):
                        for w in range(5):
                            eng = ("dve", "pool")[(cc * 5 + w) % 2]
                            P.op(eng, lambda e, cc=cc, w=w: e.tensor_scalar(
                                out=diag[:, cc, w, :], in0=ident_bf[:], scalar1=cw[:, cc, w:w + 1], scalar2=1.0,
                                op0=ALU.mult, op1=ALU.mult), reads=[sc_tk, c_tk], writes=[sc_tk])
                    with ExitStack() as es5d:
                        def sbd(name, shape, dt):
                            return es5d.enter_context(nc.sbuf_tensor(_uniq(name), list(shape), dt))
                        sm = sbd("sm", [128, NCH, 384], F32)
                        dta = sbd("dta", [128, NCH, 64], F32)
                        laa = sbd("laa", [128, NCH, 128], F32)
                        cts = [sbd("cts%d" % i, [64, 128], F32) for i in range(2)]
                        cts_tk = [Tk() for _ in range(2)]
                        sm_tk = Tk()
                        dta_tk = Tk()
                        P.dma("sp", dta[:], dt_d.rearrange("(c p) h -> p c h", p=128), reads=[scr_tk], writes=[dta_tk])
                        P.op("pool", lambda e: e.memset(laa[:], 0.0), writes=[dta_tk])
                        P.op("dve", lambda e: e.tensor_tensor(out=laa[:, :, 0:64], in0=dta[:], in1=a_bc[:].unsqueeze(1).to_broadcast([128, NCH, 64]),
                                                              op=ALU.mult), reads=[dta_tk, sc_tk], writes=[dta_tk])
                        for c in range(NCH):
                            bk, btk = next_bank()
                            P.op("pe", lambda e, bk=bk, c=c: e.matmul(bk[:, 0:32], lhsT=maskF[:], rhs=laa[:, c, 0:32], start=True, stop=True),
                                 reads=[sc_tk, dta_tk], writes=[btk])
                            P.op("pe", lambda e, bk=bk, c=c: e.matmul(bk[:, 32:64], lhsT=maskB[:], rhs=laa[:, c, 32:64], start=True, stop=True),
                                 reads=[sc_tk, dta_tk], writes=[btk])
                            P.op("pe", lambda e, bk=bk, c=c: e.matmul(bk[:, 64:128], lhsT=ones_f[:], rhs=laa[:, c, 0:64], start=True, stop=True),
                                 reads=[sc_tk, dta_tk], writes=[btk])
                            P.op("pe", lambda e, bk=bk, c=c: e.matmul(bk[:, 128:256], lhsT=laa[:, c, :], rhs=maskF[:], start=True, stop=True),
                                 reads=[sc_tk, dta_tk], writes=[btk])
                            P.op("pe", lambda e, bk=bk, c=c: e.matmul(bk[:, 256:384], lhsT=laa[:, c, :], rhs=maskB[:], start=True, stop=True),
                                 reads=[sc_tk, dta_tk], writes=[btk])
                            P.op("dve", lambda e, bk=bk, c=c: e.tensor_copy(out=sm[:, c, 0:128], in_=bk[:, 0:128]), reads=[btk], writes=[sm_tk])
                            ci_ = c % 2
                            P.op("act", lambda e, bk=bk, ci_=ci_: e.copy(out=cts[ci_][0:32, :], in_=bk[0:32, 128:256]),
                                 reads=[btk], writes=[cts_tk[ci_]])
                            P.op("act", lambda e, bk=bk, ci_=ci_: e.copy(out=cts[ci_][32:64, :], in_=bk[32:64, 256:384]),
                                 reads=[btk, cts_tk[ci_]], writes=[cts_tk[ci_]])
                            P.dma("pool", cumT_d[c].rearrange("d h i -> (d h) i"), cts[ci_][:], reads=[cts_tk[ci_]], writes=[scr5_tk])
                        P.op("act", lambda e: e.activation(out=sm[:, :, 128:192], in_=sm[:, :, 0:64], func=AF.Exp), reads=[sm_tk], writes=[sm_tk])
                        P.op("act", lambda e: e.activation(out=sm[:, :, 192:256], in_=sm[:, :, 64:128], func=AF.Exp), reads=[sm_tk], writes=[sm_tk])
                        P.op("dve", lambda e: e.tensor_tensor(out=sm[:, :, 256:320], in0=sm[:, :, 64:128], in1=sm[:, :, 0:64], op=ALU.subtract),
                             reads=[sm_tk], writes=[sm_tk])
                        P.op("act", lambda e: e.activation(out=sm[:, :, 256:320], in_=sm[:, :, 256:320], func=AF.Exp), reads=[sm_tk], writes=[sm_tk])
                        P.op("dve", lambda e: e.tensor_tensor(out=sm[:, :, 256:320], in0=sm[:, :, 256:320], in1=dta[:], op=ALU.mult),
                             reads=[sm_tk, dta_tk], writes=[sm_tk])
                        P.op("dve", lambda e: e.tensor_copy(out=sm[:, :, 64:128], in_=dta[:]), reads=[sm_tk, dta_tk], writes=[sm_tk])
                        P.op("act", lambda e: e.activation(out=sm[:, :, 352:384], in_=dta[:, :, 0:32], func=AF.Ln), reads=[sm_tk, dta_tk], writes=[sm_tk])
                        P.op("dve", lambda e: e.tensor_tensor(out=sm[:, :, 320:352], in0=sm[:, :, 352:384], in1=sm[:, :, 0:32], op=ALU.subtract),
                             reads=[sm_tk], writes=[sm_tk])
                        P.dma("pool", sm_d.rearrange("c p k -> p c k"), sm[:], reads=[sm_tk], writes=[scr5_tk])
                    P.barrier()
                    with ExitStack() as es5c:
                        def sbc(name, shape, dt):
                            return es5c.enter_context(nc.sbuf_tensor(_uniq(name), list(shape), dt))
                        xwin = [sbc("xwin%d" % i, [128, 24, 516], BF16) for i in range(2)]
                        xwin_tk = [Tk() for _ in range(2)]
                        bcs = [sbc("bcs%d" % i, [128, 24, 512], BF16) for i in range(2)]
                        bcs_tk = [[Tk() for _ in range(24)] for _ in range(2)]
                        xts = [sbc("xts%d" % i, [128, 2560], BF16) for i in range(3)]
                        xts_tk = [[Tk() for _ in range(3)] for _ in range(3)]
                        xbc_v = xbcT_d.rearrange("(k p) t -> p k t", p=128)
                        xn = [0]
                        S6 = sbc("S6", [128, 4, 512], F32)
                        S6_tk = [Tk() for _ in range(4)]
                        Sbf6 = [sbc("Sbf6_%d" % i, [128, 4, 512], BF16) for i in range(2)]
                        Sbf6_tk = [[Tk() for _ in range(4)] for _ in range(2)]
                        smc6 = [sbc("smc6_%d" % i, [128, 384], F32) for i in range(2)]
                        smc6_tk = [Tk() for _ in range(2)]
                        xs6 = [sbc("xs6_%d" % i, [128, 2048], BF16) for i in range(2)]
                        xs6_tk = [Tk() for _ in range(2)]
                        P.op("pool", lambda e: e.memset(S6[:], 0.0), writes=S6_tk)
                        def conv_load(tt):
                            wb_ = tt % 2
                            P.dma("sp", xwin[wb_][:], xbc_v[:, :, tt * 512:tt * 512 + 516], reads=[scr_tk, xpad_tk], writes=[xwin_tk[wb_]])

                        def conv_part(tt, part):
                            wb_ = tt % 2
                            for cc in range(part * 6, part * 6 + 6):
                                bk, btk = next_bank()
                                for w in range(5):
                                    P.op("pe", lambda e, bk=bk, cc=cc, w=w, wb_=wb_: e.matmul(
                                        bk[:], lhsT=diag[:, cc, w, :], rhs=xwin[wb_][:, cc, w:w + 512], start=(w == 0), stop=(w == 4)),
                                        reads=[sc_tk, xwin_tk[wb_]], writes=[btk])
                                P.op("act", lambda e, bk=bk, cc=cc, wb_=wb_: e.activation(
                                    out=bcs[wb_][:, cc, :], in_=bk[:], func=AF.Silu, bias=cbc[:, cc:cc + 1]),
                                    reads=[btk, sc_tk], writes=[bcs_tk[wb_][cc]])
                            if part == 3:
                                P.dma("pool", BCT_v[:, :, tt * 512:(tt + 1) * 512], bcs[wb_][:, 16:24, :], reads=bcs_tk[wb_][16:24], writes=[scr5_tk])

                        def chunk_work(tt, q):
                            wb_ = tt % 2
                            c = tt * 4 + q
                            xb_ = xn[0] % 3
                            xn[0] += 1
                            for grp in range(3):
                                n_ = 8 if grp < 2 else 4
                                bk, btk = next_bank()
                                bkb = bk[:].bitcast(BF16)
                                for j in range(n_):
                                    cc = grp * 8 + j
                                    P.op("pe", lambda e, bkb=bkb, cc=cc, j=j, q=q, wb_=wb_: e.transpose(
                                        out=bkb[:, j * 128:(j + 1) * 128], in_=bcs[wb_][:, cc, q * 128:(q + 1) * 128], identity=ident_bf[:]),
                                        reads=[bcs_tk[wb_][cc], c_tk], writes=[btk])
                                if grp % 2 == 0:
                                    P.op("dve", lambda e, bkb=bkb, grp=grp, xb_=xb_, n_=n_: e.tensor_copy(
                                        out=xts[xb_][:, grp * 1024:grp * 1024 + n_ * 128], in_=bkb[:, 0:n_ * 128]),
                                        reads=[btk], writes=[xts_tk[xb_][grp]])
                                else:
                                    P.op("act", lambda e, bkb=bkb, grp=grp, xb_=xb_, n_=n_: e.copy(
                                        out=xts[xb_][:, grp * 1024:grp * 1024 + n_ * 128], in_=bkb[:, 0:n_ * 128]),
                                        reads=[btk], writes=[xts_tk[xb_][grp]])
                            P.dma("pool", xtok_d[c * 128:(c + 1) * 128, :], xts[xb_][:], reads=xts_tk[xb_], writes=[scr5_tk])
                            return c, xb_

                        def s_phase(c, xb_):
                            sb_ = c % 2
                            P.dma("sp", smc6[sb_][:], sm_d[c], reads=[], writes=[smc6_tk[sb_]])
                            P.op("dve", lambda e, xb_=xb_, sb_=sb_: e.tensor_tensor(
                                out=xs6[sb_][:].rearrange("p (h d) -> p h d", h=32), in0=xts[xb_][:, 0:2048].rearrange("p (h d) -> p h d", h=32),
                                in1=smc6[sb_][:, 288:320].unsqueeze(2).to_broadcast([128, 32, 64]), op=ALU.mult),
                                reads=xts_tk[xb_] + [smc6_tk[sb_]], writes=[xs6_tk[sb_]])
                            for g in range(4):
                                P.op("act", lambda e, xb_=xb_, sb_=sb_, g=g: e.copy(out=Sbf6[sb_][:, g, :], in_=S6[:, g, :]), reads=[S6_tk[g]], writes=[Sbf6_tk[sb_][g]])
                                bk, btk = next_bank()
                                P.op("pe", lambda e, bk=bk, xb_=xb_, sb_=sb_, g=g: e.matmul(
                                    bk[:], lhsT=xts[xb_][:, 2048 + g * 128:2048 + (g + 1) * 128], rhs=xs6[sb_][:, g * 512:(g + 1) * 512],
                                    start=True, stop=True), reads=[xts_tk[xb_][2], xs6_tk[sb_]], writes=[btk])
                                P.op("dve", lambda e, xb_=xb_, sb_=sb_, g=g: e.tensor_tensor(
                                    out=S6[:, g, :].rearrange("p (h d) -> p h d", h=8), in0=S6[:, g, :].rearrange("p (h d) -> p h d", h=8),
                                    in1=smc6[sb_][:, 224 + g * 8:224 + (g + 1) * 8].unsqueeze(2).to_broadcast([128, 8, 64]), op=ALU.mult),
                                    reads=[S6_tk[g], smc6_tk[sb_]], writes=[S6_tk[g]])
                                P.op("dve", lambda e, bk=bk, g=g: e.tensor_tensor(out=S6[:, g, :], in0=S6[:, g, :], in1=bk[:], op=ALU.add),
                                     reads=[btk, S6_tk[g]], writes=[S6_tk[g]])
                            P.dma("pool", Sbssd_d[c], Sbf6[sb_][:].rearrange("p a b -> p (a b)"), reads=Sbf6_tk[sb_], writes=[scr6_tk])

                        conv_load(7)
                        for part in range(4):
                            conv_part(7, part)
                        pend = None
                        for tt in range(7, -1, -1):
                            if tt > 0:
                                conv_load(tt - 1)
                            for idx, q in enumerate((3, 2, 1, 0)):
                                if tt > 0:
                                    conv_part(tt - 1, idx)
                                cur = chunk_work(tt, q)
                                if pend is not None:
                                    s_phase(*pend)
                                pend = cur
                        s_phase(*pend)
                    P.barrier()
                P.barrier()
                d_bc = sb5("d_bc", [128, 32], F32)
                P.dma("sp", d_bc[:], dskip_d.partition_broadcast(128), writes=[sc_tk])

                if stage >= 7:
                    with ExitStack() as es7:
                        def sb7(name, shape, dt):
                            return es7.enter_context(nc.sbuf_tensor(_uniq(name), list(shape), dt))
                        cumT2 = cumT_d.rearrange("c d h i -> (c d) (h i)")
                        NBC = 3
                        bc = [sb7("bc%d" % i, [128, 32, 128], F32) for i in range(NBC)]
                        bc_tk = [[Tk() for _ in range(32)] for _ in range(NBC)]
                        L = [[sb7("L%d_%d" % (d, i), [128, 32, 128], BF16) for i in range(2)] for d in range(2)]
                        L_tk = [[Tk() for _ in range(2)] for _ in range(2)]
                        M_tk = [[[Tk() for _ in range(4)] for _ in range(2)] for _ in range(2)]
                        xt = [sb7("xt%d" % i, [128, 2560], BF16) for i in range(3)]
                        smc = [sb7("smc%d" % i, [128, 384], F32) for i in range(3)]
                        bct = [sb7("bct%d" % i, [128, 8, 128], BF16) for i in range(3)]
                        zt = [sb7("zt%d" % i, [128, 2048], BF16) for i in range(3)]
                        Sbl = [sb7("Sbl%d" % i, [128, 2048], BF16) for i in range(3)]
                        ld_tk = [Tk() for _ in range(3)]
                        ld2_tk = [Tk() for _ in range(3)]
                        cbF = [sb7("cbF%d" % i, [128, 4, 128], BF16) for i in range(2)]
                        cbB = [sb7("cbB%d" % i, [128, 4, 128], BF16) for i in range(2)]
                        cb_tk = [Tk() for _ in range(2)]
                        xdtb = [sb7("xdtb%d" % i, [128, 2048], BF16) for i in range(2)]
                        xdtb_tk = [Tk() for _ in range(2)]
                        xd = [sb7("xd%d" % i, [128, 2048], BF16) for i in range(2)]
                        xd_tk = [Tk() for _ in range(2)]
                        t1 = sb7("t1", [128, 4, 512], BF16)
                        t1_tk = [Tk() for _ in range(4)]
                        t2 = sb7("t2", [128, 4, 512], BF16)
                        t2_tk = [Tk() for _ in range(4)]
                        acc = sb7("acc", [128, 4, 512], F32)
                        acc_tk = [Tk() for _ in range(4)]
                        junk7 = sb7("junk7", [128, 512], BF16)
                        junk7_tk = Tk()
                        ms = [sb7("ms%d" % i, [128, 12], F32) for i in range(2)]
                        ms_tk = [[Tk() for _ in range(4)] for _ in range(2)]
                        rs_tk = [Tk() for _ in range(2)]
                        sbf = [sb7("sbf%d" % i, [128, 2048], BF16) for i in range(2)]
                        sbf_tk = [[Tk() for _ in range(4)] for _ in range(2)]
                        sTs = [sb7("sTs%d" % i, [128, 16, 128], BF16) for i in range(2)]
                        sTs_tk = [[Tk() for _ in range(2)] for _ in range(2)]
                        S = sb7("S", [128, 4, 512], F32)
                        S_tk = [Tk() for _ in range(4)]
                        Sfb2 = [sb7("Sfb%d" % i, [128, 4, 512], BF16) for i in range(2)]
                        Sfb2_tk = [[Tk() for _ in range(4)] for _ in range(2)]
                        xs = sb7("xs", [128, 2048], BF16)
                        xs_tk = Tk()
                        P.op("pool", lambda e: e.memset(S[:], 0.0), writes=S_tk)
                        P.op("pool", lambda e: e.memset(Sfb2[0][:], 0.0), writes=Sfb2_tk[0])
                        bc_n = [0]
                        bcmap = {}

                        def loads(c):
                            l_ = c % 3
                            P.dma("sp", xt[l_][:], xtok_d[c * 128:(c + 1) * 128, :], reads=[scr5_tk], writes=[ld_tk[l_]])
                            P.dma("sp", smc[l_][:], sm_d[c], reads=[scr5_tk], writes=[ld_tk[l_]])
                            P.dma("sp", bct[l_][:], BCT_v[:, :, c * 128:(c + 1) * 128], reads=[scr5_tk], writes=[ld_tk[l_]])
                            bcb = []
                            for d in range(2):
                                k_ = bc_n[0] % NBC
                                bc_n[0] += 1
                                bcb.append(k_)
                                P.dma("sp", bc[k_][:].rearrange("p h i -> p (h i)"),
                                      cumT2[c * 2 + d:c * 2 + d + 1, :].partition_broadcast(128), reads=[scr5_tk], writes=bc_tk[k_])
                            bcmap[c] = bcb
                            P.dma("sp", zt[l_][:], z_d[c * 128:(c + 1) * 128, :], reads=[scr_tk], writes=[ld2_tk[l_]])
                            P.dma("sp", Sbl[l_][:], Sbssd_d[c], reads=[scr6_tk], writes=[ld2_tk[l_]])

                        def stageA1(c):
                            b_ = c % 2
                            l_ = c % 3
                            bcb = bcmap[c]
                            bk, btk = next_bank()
                            for g in range(4):
                                P.op("pe", lambda e, bk=bk, g=g, l_=l_: e.matmul(
                                    bk[:, g * 128:(g + 1) * 128], lhsT=bct[l_][:, g, :], rhs=bct[l_][:, 4 + g, :], start=True, stop=True),
                                    reads=[ld_tk[l_]], writes=[btk])
                            P.op("dve", lambda e, bk=bk, b_=b_: e.tensor_tensor(
                                out=cbF[b_][:], in0=bk[:].rearrange("p (g i) -> p g i", g=4),
                                in1=maskF[:].unsqueeze(1).to_broadcast([128, 4, 128]), op=ALU.mult), reads=[btk, sc_tk], writes=[cb_tk[b_]])
                            P.op("dve", lambda e, bk=bk, b_=b_: e.tensor_tensor(
                                out=cbB[b_][:], in0=bk[:].rearrange("p (g i) -> p g i", g=4),
                                in1=maskB[:].unsqueeze(1).to_broadcast([128, 4, 128]), op=ALU.mult), reads=[btk, sc_tk, cb_tk[b_]], writes=[cb_tk[b_]])
                            for d in range(2):
                                k_ = bcb[d]
                                for h in range(32):
                                    if d == 0:
                                        P.op("dve", lambda e, k_=k_, h=h, l_=l_: e.tensor_scalar(
                                            out=bc[k_][:, h, :], in0=bc[k_][:, h, :], scalar1=smc[l_][:, 320 + h:321 + h],
                                            scalar2=smc[l_][:, 352 + h:353 + h], op0=ALU.add, op1=ALU.min),
                                            reads=[bc_tk[k_][h], ld_tk[l_]], writes=[bc_tk[k_][h]])
                                    else:
                                        P.op("act", lambda e, k_=k_, h=h, l_=l_: e.activation(
                                            out=bc[k_][:, h, :], in_=bc[k_][:, h, :], func=AF.Relu, scale=-1.0, bias=smc[l_][:, 32 + h:33 + h]),
                                            reads=[bc_tk[k_][h], ld_tk[l_]], writes=[bc_tk[k_][h]])
                                P.op("act", lambda e, k_=k_, d=d, b_=b_: e.activation(out=L[d][b_][:], in_=bc[k_][:], func=AF.Exp, scale=(1.0 if d == 0 else -1.0)),
                                     reads=bc_tk[k_], writes=[L_tk[d][b_]] + M_tk[d][b_])

                        def stageA2(c):
                            b_ = c % 2
                            l_ = c % 3
                            for d in range(2):
                                cbx = cbF if d == 0 else cbB
                                for g in range(4):
                                    P.op("dve", lambda e, d=d, g=g, b_=b_, cbx=cbx: e.tensor_tensor(
                                        out=L[d][b_][:, g * 8:(g + 1) * 8, :], in0=L[d][b_][:, g * 8:(g + 1) * 8, :],
                                        in1=cbx[b_][:, g, :].unsqueeze(1).to_broadcast([128, 8, 128]), op=ALU.mult),
                                        reads=[L_tk[d][b_], cb_tk[b_]], writes=[M_tk[d][b_][g]])
                            P.op("dve", lambda e, b_=b_, l_=l_: e.tensor_tensor(
                                out=xdtb[b_][:].rearrange("p (h q) -> p h q", h=32), in0=xt[l_][:, 0:2048].rearrange("p (h q) -> p h q", h=32),
                                in1=smc[l_][:, 96:128].unsqueeze(2).to_broadcast([128, 32, 64]), op=ALU.mult),
                                reads=[ld_tk[l_]], writes=[xdtb_tk[b_]])
                            P.op("dve", lambda e, b_=b_, l_=l_: e.tensor_tensor(
                                out=xd[b_][:].rearrange("p (h q) -> p h q", h=32), in0=xt[l_][:, 0:2048].rearrange("p (h q) -> p h q", h=32),
                                in1=d_bc[:].unsqueeze(2).to_broadcast([128, 32, 64]), op=ALU.mult),
                                reads=[ld_tk[l_], sc_tk], writes=[xd_tk[b_]])

                        def stageB(c):
                            b_ = c % 2
                            l_ = c % 3
                            P.op("dve", lambda e, l_=l_: e.tensor_tensor(
                                out=xs[:].rearrange("p (h q) -> p h q", h=32), in0=xt[l_][:, 0:2048].rearrange("p (h q) -> p h q", h=32),
                                in1=smc[l_][:, 256:288].unsqueeze(2).to_broadcast([128, 32, 64]), op=ALU.mult),
                                reads=[ld_tk[l_]], writes=[xs_tk])
                            for g in range(4):
                                bk, btk = next_bank()
                                P.op("pe", lambda e, bk=bk, g=g, l_=l_: e.matmul(
                                    bk[:], lhsT=xt[l_][:, 2048 + g * 128:2048 + (g + 1) * 128], rhs=xs[:, g * 512:(g + 1) * 512], start=True, stop=True),
                                    reads=[ld_tk[l_], xs_tk], writes=[btk])
                                P.op("dve", lambda e, g=g, l_=l_: e.tensor_tensor(
                                    out=S[:, g, :].rearrange("p (h q) -> p h q", h=8), in0=S[:, g, :].rearrange("p (h q) -> p h q", h=8),
                                    in1=smc[l_][:, 192 + g * 8:192 + (g + 1) * 8].unsqueeze(2).to_broadcast([128, 8, 64]), op=ALU.mult),
                                    reads=[S_tk[g], ld_tk[l_]], writes=[S_tk[g]])
                                P.op("dve", lambda e, bk=bk, g=g: e.tensor_tensor(out=S[:, g, :], in0=S[:, g, :], in1=bk[:], op=ALU.add),
                                     reads=[btk, S_tk[g]], writes=[S_tk[g]])
                                P.op("act", lambda e, g=g, b_=b_: e.copy(out=Sfb2[1 - b_][:, g, :], in_=S[:, g, :]), reads=[S_tk[g]], writes=[Sfb2_tk[1 - b_][g]])

                            for g in range(4):
                                bk, btk = next_bank()
                                P.op("pe", lambda e, bk=bk, g=g, l_=l_, b_=b_: e.matmul(bk[:], lhsT=bct[l_][:, 4 + g, :], rhs=Sfb2[b_][:, g, :], start=True, stop=True),
                                     reads=[ld_tk[l_], Sfb2_tk[b_][g]], writes=[btk])
                                P.op("dve", lambda e, bk=bk, g=g, l_=l_: e.tensor_tensor(
                                    out=t1[:, g, :].rearrange("p (h q) -> p h q", h=8), in0=bk[:].rearrange("p (h q) -> p h q", h=8),
                                    in1=smc[l_][:, 128 + g * 8:128 + (g + 1) * 8].unsqueeze(2).to_broadcast([128, 8, 64]), op=ALU.mult),
                                    reads=[btk, ld_tk[l_]], writes=[t1_tk[g]])
                            for g in range(4):
                                bk, btk = next_bank()
                                P.op("pe", lambda e, bk=bk, g=g, l_=l_: e.matmul(bk[:], lhsT=bct[l_][:, 4 + g, :], rhs=Sbl[l_][:, g * 512:(g + 1) * 512],
                                                                              start=True, stop=True), reads=[ld_tk[l_], ld2_tk[l_]], writes=[btk])
                                P.op("dve", lambda e, bk=bk, g=g, l_=l_: e.tensor_tensor(
                                    out=t2[:, g, :].rearrange("p (h q) -> p h q", h=8), in0=bk[:].rearrange("p (h q) -> p h q", h=8),
                                    in1=smc[l_][:, 160 + g * 8:160 + (g + 1) * 8].unsqueeze(2).to_broadcast([128, 8, 64]), op=ALU.mult),
                                    reads=[btk, ld_tk[l_]], writes=[t2_tk[g]])
                            for g in range(4):
                                bk, btk = next_bank()
                                P.op("pe", lambda e, bk=bk, g=g, b_=b_: e.matmul(bk[:], lhsT=ident_bf[:], rhs=xd[b_][:, g * 512:(g + 1) * 512], start=True, stop=False),
                                     reads=[xd_tk[b_], c_tk], writes=[btk])
                                for hh in range(8):
                                    h = g * 8 + hh
                                    P.op("pe", lambda e, bk=bk, hh=hh, h=h, b_=b_, l_=l_: e.matmul(
                                        bk[:, hh * 64:(hh + 1) * 64], lhsT=L[0][b_][:, h, :], rhs=xt[l_][:, h * 64:(h + 1) * 64],
                                        start=False, stop=False), reads=[M_tk[0][b_][g], ld_tk[l_]], writes=[btk])
                                    P.op("pe", lambda e, bk=bk, hh=hh, h=h, b_=b_: e.matmul(
                                        bk[:, hh * 64:(hh + 1) * 64], lhsT=L[1][b_][:, h, :], rhs=xdtb[b_][:, h * 64:(h + 1) * 64],
                                        start=False, stop=False), reads=[M_tk[1][b_][g], xdtb_tk[b_]], writes=[btk])
                                P.op("pe", lambda e, bk=bk, g=g: e.matmul(bk[:], lhsT=ident_bf[:], rhs=t1[:, g, :], start=False, stop=False),
                                     reads=[t1_tk[g], c_tk], writes=[btk])
                                P.op("pe", lambda e, bk=bk, g=g: e.matmul(bk[:], lhsT=ident_bf[:], rhs=t2[:, g, :], start=False, stop=True),
                                     reads=[t2_tk[g], c_tk], writes=[btk])
                                P.op("dve", lambda e, bk=bk, g=g, l_=l_: e.tensor_tensor(out=acc[:, g, :], in0=bk[:], in1=zt[l_][:, g * 512:(g + 1) * 512], op=ALU.mult),
                                     reads=[btk, ld2_tk[l_]], writes=[acc_tk[g]])
                                P.op("act", lambda e, g=g, b_=b_: e.activation(out=junk7[:], in_=acc[:, g, :], func=AF.Square, accum_out=ms[b_][:, g:g + 1]),
                                     reads=[acc_tk[g]], writes=[junk7_tk, ms_tk[b_][g]])
                            P.op("act", lambda e, b_=b_: e.activation(out=ms[b_][:, 4:8], in_=ms[b_][:, 0:4], func=AF.Ln, scale=1.0 / 512, bias=EPS),
                                 reads=ms_tk[b_], writes=[rs_tk[b_]])
                            P.op("act", lambda e, b_=b_: e.activation(out=ms[b_][:, 8:12], in_=ms[b_][:, 4:8], func=AF.Exp, scale=-0.5),
                                 reads=[rs_tk[b_]], writes=[rs_tk[b_]])
                            for g in range(4):
                                P.op("act", lambda e, g=g, b_=b_: e.activation(out=sbf[b_][:, g * 512:(g + 1) * 512], in_=acc[:, g, :], func=AF.Copy,
                                                                                scale=ms[b_][:, 8 + g:9 + g]),
                                     reads=[acc_tk[g], rs_tk[b_]], writes=[sbf_tk[b_][g]])
                            for half_ in range(2):
                                bk, btk = next_bank()
                                bkb = bk[:].bitcast(BF16)
                                for j in range(8):
                                    e_ = half_ * 8 + j
                                    P.op("pe", lambda e, bkb=bkb, j=j, e_=e_, b_=b_: e.transpose(
                                        out=bkb[:, j * 128:(j + 1) * 128], in_=sbf[b_][:, e_ * 128:(e_ + 1) * 128], identity=ident_bf[:]),
                                        reads=[sbf_tk[b_][e_ // 4], c_tk], writes=[btk])
                                P.op("act", lambda e, bkb=bkb, b_=b_, half_=half_: e.copy(
                                    out=sTs[b_][:, half_ * 8:(half_ + 1) * 8, :], in_=bkb.rearrange("p (a b) -> p a b", a=8)),
                                    reads=[btk], writes=[sTs_tk[b_][half_]])
                            P.dma("pool", sT_v[:, :, c * 128:(c + 1) * 128], sTs[b_][:], reads=sTs_tk[b_], writes=[scr7_tk])

                        loads(0)
                        stageA1(0)
                        loads(1)
                        stageA2(0)
                        for c in range(NCH):
                            if c + 1 < NCH:
                                stageA1(c + 1)
                            if c + 2 < NCH:
                                loads(c + 2)
                            stageB(c)
                            if c + 1 < NCH:
                                stageA2(c + 1)
                    P.barrier()

        TT = 256

        def load_cast(wdst, w_dram_v, nk, ncol, gain, stg, stg_tk, wtk, rr):
            kper = max(1, 4096 // ncol)
            for k0 in range(0, nk, kper):
                i = rr[0] % len(stg)
                rr[0] += 1
                P.dma("sp", stg[i][:, 0:kper, 0:ncol], w_dram_v[:, k0:k0 + kper, :], writes=[stg_tk[i]])
                for kk in range(kper):
                    k = k0 + kk
                    eng = ("act", "pool", "dve")[k % 3]
                    if gain is None:
                        if eng == "act":
                            P.op("act", lambda e, i=i, kk=kk, k=k: e.copy(out=wdst[:, k, :], in_=stg[i][:, kk, 0:ncol]), reads=[stg_tk[i]], writes=[wtk])
                        else:
                            P.op(eng, lambda e, i=i, kk=kk, k=k: e.tensor_copy(out=wdst[:, k, :], in_=stg[i][:, kk, 0:ncol]), reads=[stg_tk[i]], writes=[wtk])
                    elif eng == "act":
                        P.op("act", lambda e, i=i, kk=kk, k=k: e.activation(out=wdst[:, k, :], in_=stg[i][:, kk, 0:ncol], func=AF.Copy,
                                                                            scale=gain[:, k:k + 1]), reads=[stg_tk[i], c_tk], writes=[wtk])
                    else:
                        P.op(eng, lambda e, i=i, kk=kk, k=k: e.tensor_scalar(out=wdst[:, k, :], in0=stg[i][:, kk, 0:ncol], scalar1=gain[:, k:k + 1],
                                                                             scalar2=1.0, op0=ALU.mult, op1=ALU.mult), reads=[stg_tk[i], c_tk], writes=[wtk])

        if stage >= 8:
            with ExitStack() as es8:
                def sb8(name, shape, dt):
                    return es8.enter_context(nc.sbuf_tensor(_uniq(name), list(shape), dt))
                gains = sb8("gains", [128, 40], F32)
                P.dma("sp", gains[:], gains_d, writes=[c_tk])
                Wr = sb8("Wr", [128, 16, 1024], BF16)
                Ws = sb8("Ws", [128, 16, 1024], BF16)
                Wo = sb8("Wo", [128, 8, 1024], BF16)
                w8_tk = Tk(multi=True)
                with ExitStack() as es8s:
                    stg = [es8s.enter_context(nc.sbuf_tensor(_uniq("wstg"), [128, 4, 1024], F32)) for _ in range(4)]
                    stg_tk = [Tk() for _ in range(4)]
                    rr_ = [0]
                    load_cast(Wr, w_ret_o_d.rearrange("(k p) c -> p k c", p=128), 16, 1024, gains[:, 0:16], stg, stg_tk, w8_tk, rr_)
                    load_cast(Ws, w_ssm_o_d.rearrange("(k p) c -> p k c", p=128), 16, 1024, gains[:, 16:32], stg, stg_tk, w8_tk, rr_)
                    load_cast(Wo, w_out_d.rearrange("(k p) c -> p k c", p=128), 8, 1024, None, stg, stg_tk, w8_tk, rr_)
                    P.barrier()
                rT_v8 = rT_d.rearrange("(e p) t -> p e t", p=128)
                sT_v8 = sT_d.rearrange("(e p) t -> p e t", p=128)
                gT_v8 = gateT_d.rearrange("(e p) t -> p e t", p=128)
                h2T_v = h2T_d.rearrange("(k p) t -> p k t", p=128)
                rTt = [sb8("rTt%d" % i, [128, 16, TT], BF16) for i in range(2)]
                sTt = [sb8("sTt%d" % i, [128, 16, TT], BF16) for i in range(2)]
                gTt = [sb8("gTt%d" % i, [128, 16, TT], BF16) for i in range(2)]
                ld_tk = [Tk() for _ in range(2)]
                t1 = [sb8("t1_%d" % i, [128, TT], F32) for i in range(2)]
                t2 = [sb8("t2_%d" % i, [128, TT], F32) for i in range(2)]
                t_tk = [[Tk(), Tk()] for _ in range(2)]
                mixT = [sb8("mixT%d" % i, [128, 8, TT], BF16) for i in range(2)]
                mix_tk = [[Tk() for _ in range(8)] for _ in range(2)]
                xin = [sb8("xin%d" % i, [128, 1024], F32) for i in range(2)]
                xin_tk = [Tk() for _ in range(2)]
                x1t = [sb8("x1t%d" % i, [128, 1024], F32) for i in range(2)]
                x1_tk = [[Tk(), Tk()] for _ in range(2)]
                junk8 = sb8("junk8", [128, 1024], BF16)
                junk8_tk = Tk()
                st8 = [sb8("st8_%d" % i, [128, 4], F32) for i in range(2)]
                st8_tk = [Tk() for _ in range(2)]
                h2 = [sb8("h2_%d" % i, [128, 1024], BF16) for i in range(2)]
                h2_tk = [Tk() for _ in range(2)]
                h2Ts = [sb8("h2Ts%d" % i, [128, 8, 128], BF16) for i in range(2)]
                h2Ts_tk = [Tk() for _ in range(2)]
                tn = [0]
                def p8_loads(tt):
                    b_ = tt % 2
                    tsl = slice(tt * TT, (tt + 1) * TT)
                    P.dma("sp", rTt[b_][:], rT_v8[:, :, tsl], reads=[scr3_tk], writes=[ld_tk[b_]])
                    P.dma("sp", sTt[b_][:], sT_v8[:, :, tsl], reads=[scr7_tk], writes=[ld_tk[b_]])
                    P.dma("sp", gTt[b_][:], gT_v8[:, :, tsl], reads=[scr_tk], writes=[ld_tk[b_]])
                def p8_head(tt, d0, d1):
                    b_ = tt % 2
                    for dch in range(d0, d1):
                        bR, bRtk = next_bank()
                        for e_ in range(16):
                            P.op("pe", lambda e, bR=bR, e_=e_, dch=dch, b_=b_: e.matmul(
                                bR[:, 0:TT], lhsT=Wr[:, e_, dch * 128:(dch + 1) * 128], rhs=rTt[b_][:, e_, :], start=(e_ == 0), stop=(e_ == 15)),
                                reads=[w8_tk, ld_tk[b_]], writes=[bRtk])
                        bS, bStk = next_bank()
                        for e_ in range(16):
                            P.op("pe", lambda e, bS=bS, e_=e_, dch=dch, b_=b_: e.matmul(
                                bS[:, 0:TT], lhsT=Ws[:, e_, dch * 128:(dch + 1) * 128], rhs=sTt[b_][:, e_, :], start=(e_ == 0), stop=(e_ == 15)),
                                reads=[w8_tk, ld_tk[b_]], writes=[bStk])
                        ti = tn[0] % 2
                        tn[0] += 1
                        P.op("dve", lambda e, bR=bR, ti=ti, dch=dch, b_=b_: e.tensor_tensor(out=t1[ti][:], in0=bR[:, 0:TT], in1=gTt[b_][:, dch, :], op=ALU.mult),
                             reads=[bRtk, ld_tk[b_]], writes=[t_tk[ti][0]])
                        P.op("dve", lambda e, bS=bS, ti=ti, dch=dch, b_=b_: e.tensor_tensor(out=t2[ti][:], in0=bS[:, 0:TT], in1=gTt[b_][:, 8 + dch, :], op=ALU.mult),
                             reads=[bStk, ld_tk[b_]], writes=[t_tk[ti][1]])
                        P.op("dve", lambda e, ti=ti, dch=dch, b_=b_: e.tensor_tensor(out=mixT[b_][:, dch, :], in0=t1[ti][:], in1=t2[ti][:], op=ALU.add),
                             reads=t_tk[ti], writes=[mix_tk[b_][dch]])
                def p8_tail(tt):
                    b_ = tt % 2
                    for q in range(TT // 128):
                        c = tt * (TT // 128) + q
                        cb_ = c % 2
                        P.dma("sp", xin[cb_][:], x_d[c * 128:(c + 1) * 128, :], writes=[xin_tk[cb_]])
                        for hf in range(2):
                            bk, btk = next_bank()
                            for d_ in range(8):
                                P.op("pe", lambda e, bk=bk, d_=d_, q=q, hf=hf, b_=b_: e.matmul(
                                    bk[:], lhsT=mixT[b_][:, d_, q * 128:(q + 1) * 128], rhs=Wo[:, d_, hf * 512:(hf + 1) * 512],
                                    start=(d_ == 0), stop=(d_ == 7)), reads=[mix_tk[b_][d_], w8_tk], writes=[btk])
                            P.op("dve", lambda e, bk=bk, hf=hf, cb_=cb_: e.tensor_tensor(
                                out=x1t[cb_][:, hf * 512:(hf + 1) * 512], in0=bk[:], in1=xin[cb_][:, hf * 512:(hf + 1) * 512], op=ALU.add),
                                reads=[btk, xin_tk[cb_]], writes=[x1_tk[cb_][hf]])
                        P.dma("pool", x1_d[c * 128:(c + 1) * 128, :], x1t[cb_][:], reads=x1_tk[cb_], writes=[scr8_tk])
                        P.op("act", lambda e, cb_=cb_: e.activation(out=junk8[:], in_=x1t[cb_][:], func=AF.Square, accum_out=st8[cb_][:, 0:1]),
                             reads=x1_tk[cb_], writes=[junk8_tk, st8_tk[cb_]])
                        P.op("act", lambda e, cb_=cb_: e.activation(out=st8[cb_][:, 1:2], in_=st8[cb_][:, 0:1], func=AF.Ln, scale=1.0 / D, bias=EPS),
                             reads=[st8_tk[cb_]], writes=[st8_tk[cb_]])
                        P.op("act", lambda e, cb_=cb_: e.activation(out=st8[cb_][:, 2:3], in_=st8[cb_][:, 1:2], func=AF.Exp, scale=-0.5),
                             reads=[st8_tk[cb_]], writes=[st8_tk[cb_]])
                        P.op("dve", lambda e, cb_=cb_: e.tensor_scalar(out=h2[cb_][:], in0=x1t[cb_][:], scalar1=st8[cb_][:, 2:3], scalar2=None, op0=ALU.mult),
                             reads=x1_tk[cb_] + [st8_tk[cb_]], writes=[h2_tk[cb_]])
                        bk, btk = next_bank()
                        bkb = bk[:].bitcast(BF16)
                        for k in range(8):
                            P.op("pe", lambda e, bkb=bkb, k=k, cb_=cb_: e.transpose(out=bkb[:, k * 128:(k + 1) * 128], in_=h2[cb_][:, k * 128:(k + 1) * 128],
                                                                                 identity=ident_bf[:]), reads=[h2_tk[cb_], c_tk], writes=[btk])
                        P.op("act", lambda e, bkb=bkb, cb_=cb_: e.copy(out=h2Ts[cb_][:], in_=bkb.rearrange("p (a b) -> p a b", a=8)),
                             reads=[btk], writes=[h2Ts_tk[cb_]])
                        P.dma("pool", h2T_v[:, :, c * 128:(c + 1) * 128], h2Ts[cb_][:], reads=[h2Ts_tk[cb_]], writes=[scr8_tk])
                NT8 = T // TT
                p8_loads(0)
                p8_head(0, 0, 8)
                for tt in range(NT8):
                    if tt + 1 < NT8:
                        p8_loads(tt + 1)
                        p8_head(tt + 1, 0, 4)
                    p8_tail(tt)
                    if tt + 1 < NT8:
                        p8_head(tt + 1, 4, 8)
                P.barrier()

        if stage >= 9:
            with ExitStack() as es9:
                def sb9(name, shape, dt):
                    return es9.enter_context(nc.sbuf_tensor(_uniq(name), list(shape), dt))
                gains9 = sb9("gains9", [128, 40], F32)
                gfin = sb9("gfin", [128, 1024], F32)
                P.dma("sp", gains9[:], gains_d, writes=[c_tk])
                P.dma("sp", gfin[:], gfin_d.partition_broadcast(128), writes=[c_tk])
                Wu = sb9("Wu", [128, 8, 4096], BF16)
                Wd = sb9("Wd", [128, 32, 1024], BF16)
                w9_tk = Tk(multi=True)
                with ExitStack() as es9s:
                    stg = [es9s.enter_context(nc.sbuf_tensor(_uniq("wstg9"), [128, 4, 1024], F32)) for _ in range(4)]
                    stg_tk = [Tk() for _ in range(4)]
                    rr_ = [0]
                    load_cast(Wu, w_up_d.rearrange("(k p) c -> p k c", p=128), 8, 4096, gains9[:, 32:40],
                              [t_[:].rearrange("p a b -> p (a b)").rearrange("p (a b) -> p a b", a=1) for t_ in stg], stg_tk, w9_tk, rr_)
                    load_cast(Wd, w_dn_d.rearrange("(k p) c -> p k c", p=128), 32, 1024, None, stg, stg_tk, w9_tk, rr_)
                    P.barrier()
                h2T_v9 = h2T_d.rearrange("(k p) t -> p k t", p=128)
                hT9 = [sb9("hT9_%d" % i, [128, 8, TT], BF16) for i in range(2)]
                hT9_tk = [Tk() for _ in range(2)]
                uT = [sb9("uT%d" % i, [128, 32, TT], BF16) for i in range(2)]
                uT_tk = [[Tk() for _ in range(32)] for _ in range(2)]
                rl = [sb9("rl%d" % i, [128, TT], F32) for i in range(4)]
                rl_tk = [Tk() for _ in range(4)]
                x1i = [sb9("x1i%d" % i, [128, 1024], F32) for i in range(2)]
                x1i_tk = [Tk() for _ in range(2)]
                x2 = [sb9("x2_%d" % i, [128, 1024], F32) for i in range(2)]
                x2_tk = [[Tk(), Tk()] for _ in range(2)]
                junk9 = sb9("junk9", [128, 1024], BF16)
                junk9_tk = Tk()
                st9 = [sb9("st9_%d" % i, [128, 4], F32) for i in range(2)]
                st9_tk = [Tk() for _ in range(2)]
                ot = [sb9("ot%d" % i, [128, 1024], F32) for i in range(2)]
                ot_tk = [Tk() for _ in range(2)]
                rn = [0]
                out_tk = Tk(multi=True)
                for tt in range(T // TT):
                    b_ = tt % 2
                    tsl = slice(tt * TT, (tt + 1) * TT)
                    P.dma("sp", hT9[b_][:], h2T_v9[:, :, tsl], reads=[scr8_tk], writes=[hT9_tk[b_]])
                    for f in range(32):
                        bk, btk = next_bank()
                        for d_ in range(8):
                            P.op("pe", lambda e, bk=bk, d_=d_, f=f, b_=b_: e.matmul(
                                bk[:, 0:TT], lhsT=Wu[:, d_, f * 128:(f + 1) * 128], rhs=hT9[b_][:, d_, :], start=(d_ == 0), stop=(d_ == 7)),
                                reads=[w9_tk, hT9_tk[b_]], writes=[btk])
                        ri = rn[0] % 4
                        rn[0] += 1
                        P.op("act", lambda e, bk=bk, ri=ri: e.activation(out=rl[ri][:], in_=bk[:, 0:TT], func=AF.Relu), reads=[btk], writes=[rl_tk[ri]])
                        P.op("dve", lambda e, ri=ri, f=f, b_=b_: e.tensor_tensor(out=uT[b_][:, f, :], in0=rl[ri][:], in1=rl[ri][:], op=ALU.mult),
                             reads=[rl_tk[ri]], writes=[uT_tk[b_][f]])
                    for q in range(TT // 128):
                        c = tt * (TT // 128) + q
                        cb_ = c % 2
                        P.dma("sp", x1i[cb_][:], x1_d[c * 128:(c + 1) * 128, :], reads=[scr8_tk], writes=[x1i_tk[cb_]])
                        for hf in range(2):
                            bk, btk = next_bank()
                            for f in range(32):
                                P.op("pe", lambda e, bk=bk, f=f, q=q, hf=hf, b_=b_: e.matmul(
                                    bk[:], lhsT=uT[b_][:, f, q * 128:(q + 1) * 128], rhs=Wd[:, f, hf * 512:(hf + 1) * 512],
                                    start=(f == 0), stop=(f == 31)), reads=[uT_tk[b_][f], w9_tk], writes=[btk])
                            P.op("dve", lambda e, bk=bk, hf=hf, cb_=cb_: e.tensor_tensor(
                                out=x2[cb_][:, hf * 512:(hf + 1) * 512], in0=bk[:], in1=x1i[cb_][:, hf * 512:(hf + 1) * 512], op=ALU.add),
                                reads=[btk, x1i_tk[cb_]], writes=[x2_tk[cb_][hf]])
                        P.op("act", lambda e, cb_=cb_: e.activation(out=junk9[:], in_=x2[cb_][:], func=AF.Square, accum_out=st9[cb_][:, 0:1]),
                             reads=x2_tk[cb_], writes=[junk9_tk, st9_tk[cb_]])
                        P.op("act", lambda e, cb_=cb_: e.activation(out=st9[cb_][:, 1:2], in_=st9[cb_][:, 0:1], func=AF.Ln, scale=1.0 / D, bias=EPS),
                             reads=[st9_tk[cb_]], writes=[st9_tk[cb_]])
                        P.op("act", lambda e, cb_=cb_: e.activation(out=st9[cb_][:, 2:3], in_=st9[cb_][:, 1:2], func=AF.Exp, scale=-0.5),
                             reads=[st9_tk[cb_]], writes=[st9_tk[cb_]])
                        P.op("dve", lambda e, cb_=cb_: e.scalar_tensor_tensor(out=ot[cb_][:], in0=x2[cb_][:], scalar=st9[cb_][:, 2:3], in1=gfin[:],
                                                                              op0=ALU.mult, op1=ALU.mult),
                             reads=x2_tk[cb_] + [st9_tk[cb_], c_tk], writes=[ot_tk[cb_]])
                        P.dma("pool", out_d[c * 128:(c + 1) * 128, :], ot[cb_][:], reads=[ot_tk[cb_]], writes=[out_tk])
                P.barrier()
        P.finish()
    nc._in_names = in_names
    return nc


def col128(v):
    v = np.asarray(v, dtype=np.float32).reshape(-1, 128)
    return np.ascontiguousarray(v.T)


def shared_inputs(inp):
    m = dict(host_consts())
    m["w_in"] = np.ascontiguousarray(inp["w_in"][0], dtype=np.float32)
    m["gmix"] = col128(inp["norm_mix_g"][0])
    m["dtb"] = np.concatenate([inp["dt_bias_f"][0], inp["dt_bias_b"][0]]).reshape(1, 64).astype(np.float32)
    m["cw"] = np.ascontiguousarray(np.asarray(inp["conv_w"][0], np.float32).reshape(5, 24, 128).transpose(2, 1, 0))
    m["cbc"] = col128(inp["conv_b"][0])
    m["cbr"] = np.asarray(inp["conv_b"][0], np.float32).reshape(1, 3072)
    m["alog"] = np.concatenate([inp["a_log_f"][0], inp["a_log_b"][0]]).reshape(1, 64).astype(np.float32)
    m["dskip"] = np.asarray(inp["ssm_d"][0], np.float32).reshape(1, 32)
    m["gains"] = np.concatenate([col128(inp["ret_gn_g"][0]), col128(inp["ssm_norm_g"][0]), col128(inp["norm_mlp_g"][0])], 1)
    m["gfin"] = np.asarray(inp["norm_final_g"], np.float32).reshape(1, 1024)
    m["w_ret_o"] = np.ascontiguousarray(inp["w_ret_o"][0], dtype=np.float32)
    m["w_ssm_o"] = np.ascontiguousarray(inp["w_ssm_o"][0], dtype=np.float32)
    m["w_out"] = np.ascontiguousarray(inp["w_out"][0], dtype=np.float32)
    m["w_up"] = np.ascontiguousarray(inp["w_mlp_up"][0], dtype=np.float32)
    m["w_dn"] = np.ascontiguousarray(inp["w_mlp_down"][0], dtype=np.float32)
    return m


def core_inputs(inp, b, shared, names):
    m = {k: v for k, v in shared.items() if k in names}
    m["x"] = np.ascontiguousarray(inp["x"][b], dtype=np.float32)
    m["pos"] = np.ascontiguousarray(inp["positions"][b], dtype=np.int32).reshape(1, T)
    return m


_NC_CACHE = {}


def kernel(**inputs):
    inp = {k: np.asarray(v) for k, v in inputs.items()}
    if "nc" not in _NC_CACHE:
        _NC_CACHE["nc"] = build()
    nc = _NC_CACHE["nc"]
    shared = shared_inputs(inp)
    n = 8
    in_maps = [core_inputs(inp, b, shared, nc._in_names) for b in range(n)]
    res = run_bass_kernel_spmd(nc, in_maps, core_ids=list(range(n)))
    out = np.stack([np.asarray(r["out"], dtype=np.float32) for r in res.results], axis=0)
    return out
```
